# Optimizing a Trainium2 kernel written in Bass

```python
import math
import jax
import jax.numpy as jnp
from jax import lax
import numpy as np

D_MODEL = 1024
BATCH = 4
SEQ = 4096
DEPTH = 4

N_A_LAYERS = DEPTH // 2
N_B_LAYERS = DEPTH - N_A_LAYERS
MEM_LEN = 256
MEM_HEADS = 4
MEM_HEAD_DIM = 64
MEM_W = MEM_HEADS * MEM_HEAD_DIM
MIX_W = D_MODEL
MAIN_W = MIX_W - MEM_W
GLA_HEADS = 4
GLA_DV = MAIN_W // GLA_HEADS
GLA_DK = GLA_DV // 2
GLA_RANK = 16
GLA_GATE_NORM = 16.0
GLA_CHUNK = 16
NSA_HEADS = 12
NSA_GROUPS = 4
NSA_HEAD_DIM = MAIN_W // NSA_HEADS
NSA_REP = NSA_HEADS // NSA_GROUPS
CMP_BLOCK = 32
CMP_STRIDE = 16
CMP_HIDDEN = 128
SEL_BLOCK = 64
SEL_TOPK = 16
WINDOW = 512
Q_BLOCK = 64
REL_BUCKETS = 32
REL_MAX_DIST = 128
FFN_DIM = 2816
CONV_WIDTH = 3
EPS = 1e-6
GLA_SIZES = (GLA_HEADS * GLA_DK, GLA_HEADS * GLA_DK, GLA_HEADS * GLA_DV, GLA_HEADS * GLA_DV, GLA_RANK, MEM_W)
GLA_IN = sum(GLA_SIZES)
NSA_SIZES = (NSA_HEADS * NSA_HEAD_DIM, NSA_HEADS * 3, MEM_W)
NSA_IN = sum(NSA_SIZES)
SHARED_KV_W = 6 * NSA_GROUPS * NSA_HEAD_DIM

kernel_name = 'hybrid_gla_nsa_yoco_trunk'


def rmsnorm(x, g):
    xf = x.astype(jnp.float32)
    y = xf * lax.rsqrt(jnp.mean(xf * xf, axis=-1, keepdims=True) + EPS)
    return (y * g.astype(jnp.float32)).astype(x.dtype)


def split_cols(x, sizes):
    out = []
    start = 0
    for n in sizes:
        out.append(x[..., start:start + n])
        start += n
    return out


def rel_bucket(dist):
    dist = jnp.maximum(dist, 0)
    max_exact = REL_BUCKETS // 2
    large = max_exact + (jnp.log(jnp.maximum(dist, 1).astype(jnp.float32) / max_exact)
                         / math.log(REL_MAX_DIST / max_exact) * (REL_BUCKETS - max_exact)).astype(jnp.int32)
    large = jnp.minimum(large, REL_BUCKETS - 1)
    return jnp.where(dist < max_exact, dist, large)


def masked_softmax(s, mask):
    s = jnp.where(mask, s.astype(jnp.float32), -1e30)
    m = jnp.max(s, axis=-1, keepdims=True)
    p = jnp.where(mask, jnp.exp(s - m), 0.0)
    return p / jnp.maximum(jnp.sum(p, axis=-1, keepdims=True), 1e-30)


def gla_mixer(q, k, v, log_a, g_out, out_norm):
    dtype = v.dtype
    B, S = q.shape[0], q.shape[1]
    C = GLA_CHUNK
    N = S // C

    def chunked(t):
        return t.reshape(B, N, C, GLA_HEADS, t.shape[-1]).transpose(0, 3, 1, 2, 4).astype(jnp.float32)

    qc = chunked(q) * GLA_DK ** -0.5
    kc = chunked(k)
    vc = chunked(v)
    b = lax.cumsum(chunked(log_a), axis=3)
    causal = jnp.tril(jnp.ones((C, C), dtype=bool))
    diff = b[..., :, None, :] - b[..., None, :, :]
    decay = jnp.exp(jnp.where(causal[:, :, None], diff, -jnp.inf))
    attn = jnp.einsum('bhnid,bhnjd,bhnijd->bhnij', qc, kc, decay)
    o_intra = jnp.einsum('bhnij,bhnjv->bhniv', attn, vc)
    b_last = b[..., -1:, :]
    q_in = qc * jnp.exp(b)
    k_out = kc * jnp.exp(b_last - b)
    a_last = jnp.exp(b_last[..., 0, :])

    def step(state, inp):
        qi, ki, vi, ai = inp
        o = jnp.einsum('bhcd,bhdv->bhcv', qi, state)
        state = ai[..., None] * state + jnp.einsum('bhcd,bhcv->bhdv', ki, vi)
        return state, o

    xs = (jnp.moveaxis(q_in, 2, 0), jnp.moveaxis(k_out, 2, 0), jnp.moveaxis(vc, 2, 0), jnp.moveaxis(a_last, 2, 0))
    state0 = jnp.zeros((B, GLA_HEADS, GLA_DK, GLA_DV), jnp.float32)
    _, o_inter = lax.scan(step, state0, xs)
    o = o_intra + jnp.moveaxis(o_inter, 0, 2)
    o = o.transpose(0, 2, 3, 1, 4).reshape(B, S, GLA_HEADS, GLA_DV)
    o = rmsnorm(o, out_norm) * jax.nn.silu(g_out.astype(jnp.float32))
    return o.reshape(B, S, GLA_HEADS * GLA_DV).astype(dtype)


def mem_attention(q, mem_kv):
    B, S = q.shape[0], q.shape[1]
    M = mem_kv.shape[1]
    k, v = split_cols(mem_kv, (MEM_W, MEM_W))
    qh = q.reshape(B, S, MEM_HEADS, MEM_HEAD_DIM)
    kh = k.reshape(B, M, MEM_HEADS, MEM_HEAD_DIM)
    vh = v.reshape(B, M, MEM_HEADS, MEM_HEAD_DIM)
    s = jnp.einsum('bshd,bmhd->bhsm', qh, kh).astype(jnp.float32) * MEM_HEAD_DIM ** -0.5
    p = jax.nn.softmax(s, axis=-1)
    o = jnp.einsum('bhsm,bmhd->bshd', p, vh.astype(jnp.float32))
    return o.reshape(B, S, MEM_W).astype(q.dtype)


def shared_kv(h, kv_norm, w_kv, cmp_pos, cmp_w1, cmp_b1, cmp_w2, cmp_b2):
    B, S = h.shape[0], h.shape[1]
    G, Dh = NSA_GROUPS, NSA_HEAD_DIM
    kv = (rmsnorm(h, kv_norm) @ w_kv).reshape(B, S, 6, G, Dh).transpose(2, 0, 3, 1, 4)
    NC = (S - CMP_BLOCK) // CMP_STRIDE + 1
    win = jnp.arange(NC)[:, None] * CMP_STRIDE + jnp.arange(CMP_BLOCK)[None, :]

    def compress(t, j):
        blocks = t[:, :, win] + cmp_pos[j]
        flat = blocks.reshape(B, G, NC, CMP_BLOCK * Dh)
        return jax.nn.silu(flat @ cmp_w1[j] + cmp_b1[j]) @ cmp_w2[j] + cmp_b2[j]

    kc = compress(kv[0], 0)
    vc = compress(kv[1], 1)
    return (kc, vc, kv[2], kv[3], kv[4], kv[5])


def nsa_mixer(q, gates, shared, rel_bias):
    kc, vc, ks, vs, kw, vw = shared
    B, S = q.shape[0], q.shape[1]
    G, R, Dh = NSA_GROUPS, NSA_REP, NSA_HEAD_DIM
    NC = kc.shape[2]
    NSB = S // SEL_BLOCK
    n_sel = min(SEL_TOPK, NSB)
    qg = (q * Dh ** -0.5).reshape(B, S, G, R, Dh).transpose(0, 2, 3, 1, 4)
    gg = gates.reshape(B, S, G, R, 3).transpose(0, 2, 3, 1, 4)
    kb = ks.reshape(B, G, NSB, SEL_BLOCK, Dh)
    vb = vs.reshape(B, G, NSB, SEL_BLOCK, Dh)
    kw_p = jnp.pad(kw, ((0, 0), (0, 0), (WINDOW, 0), (0, 0)))
    vw_p = jnp.pad(vw, ((0, 0), (0, 0), (WINDOW, 0), (0, 0)))
    cmp_start = jnp.arange(NC) * CMP_STRIDE
    cmp_end = cmp_start + CMP_BLOCK - 1
    sel_idx = jnp.arange(NSB)
    sel_start = sel_idx * SEL_BLOCK
    overlap = ((cmp_start[:, None] < sel_start[None, :] + SEL_BLOCK)
               & (cmp_end[:, None] >= sel_start[None, :])).astype(jnp.float32)
    tbl = rel_bias.astype(jnp.float32).reshape(REL_BUCKETS, G, R)
    g_idx = jnp.arange(G)[None, :, None, None]

    def common_bias(dist):
        return tbl[rel_bucket(dist)].transpose(2, 3, 0, 1)

    def group_bias(dist):
        return jnp.moveaxis(tbl[rel_bucket(dist), g_idx], -1, 2)

    def block(c):
        t0 = c * Q_BLOCK
        t = t0 + jnp.arange(Q_BLOCK)
        qb = lax.dynamic_slice_in_dim(qg, t0, Q_BLOCK, axis=3)
        gb = lax.dynamic_slice_in_dim(gg, t0, Q_BLOCK, axis=3)
        d_c = t[:, None] - cmp_end[None, :]
        s_c = jnp.einsum('bgrqd,bgkd->bgrqk', qb, kc).astype(jnp.float32) + common_bias(d_c)
        p_c = masked_softmax(s_c, d_c >= 0)
        o_c = jnp.einsum('bgrqk,bgkd->bgrqd', p_c, vc.astype(jnp.float32))
        imp = jnp.einsum('bgrqk,kj->bgqj', p_c, overlap)
        cur = (t // SEL_BLOCK)[:, None]
        jj = sel_idx[None, :]
        forced = (jj == 0) | (jj == cur) | (jj == cur - 1)
        score = jnp.where(forced, 1e4, jnp.where(jj <= cur, imp, -1.0))
        _, idx = lax.top_k(score, n_sel)
        gather = jax.vmap(jax.vmap(lambda blk, ix: blk[ix]))
        k_sel = gather(kb, idx).reshape(B, G, Q_BLOCK, n_sel * SEL_BLOCK, Dh)
        v_sel = gather(vb, idx).reshape(B, G, Q_BLOCK, n_sel * SEL_BLOCK, Dh)
        pos = (idx[..., None] * SEL_BLOCK + jnp.arange(SEL_BLOCK)).reshape(B, G, Q_BLOCK, n_sel * SEL_BLOCK)
        d_s = t[None, None, :, None] - pos
        s_s = jnp.einsum('bgrqd,bgqkd->bgrqk', qb, k_sel).astype(jnp.float32) + group_bias(d_s)
        p_s = masked_softmax(s_s, (d_s >= 0)[:, :, None])
        o_s = jnp.einsum('bgrqk,bgqkd->bgrqd', p_s, v_sel.astype(jnp.float32))
        kwb = lax.dynamic_slice_in_dim(kw_p, t0, WINDOW + Q_BLOCK, axis=2)
        vwb = lax.dynamic_slice_in_dim(vw_p, t0, WINDOW + Q_BLOCK, axis=2)
        s_pos = t0 - WINDOW + jnp.arange(WINDOW + Q_BLOCK)
        d_w = t[:, None] - s_pos[None, :]
        mask_w = (d_w >= 0) & (d_w < WINDOW) & (s_pos[None, :] >= 0)
        s_w = jnp.einsum('bgrqd,bgkd->bgrqk', qb, kwb).astype(jnp.float32) + common_bias(d_w)
        p_w = masked_softmax(s_w, mask_w)
        o_w = jnp.einsum('bgrqk,bgkd->bgrqd', p_w, vwb.astype(jnp.float32))
        o = gb[..., 0:1] * o_c + gb[..., 1:2] * o_s + gb[..., 2:3] * o_w
        return o.astype(q.dtype)

    out = lax.map(block, jnp.arange(S // Q_BLOCK))
    return out.transpose(1, 0, 4, 2, 3, 5).reshape(B, S, NSA_HEADS * Dh)


def conv_ffn(x, w_up, conv_w, conv_b, w_down):
    S = x.shape[1]
    u = x @ w_up
    u_pad = jnp.pad(u, ((0, 0), (CONV_WIDTH - 1, 0), (0, 0)))
    hc = conv_b
    for j in range(CONV_WIDTH):
        hc = hc + conv_w[j] * u_pad[:, j:j + S]
    a, b = split_cols(hc, (FFN_DIM, FFN_DIM))
    return (jax.nn.silu(a) * b) @ w_down


def setup_inputs(seed: int = 0) -> dict:
    key = jax.random.key(seed)
    ks = jax.random.split(key, 26)
    f32 = jnp.float32

    def nrm(k, shape, scale):
        return jax.random.normal(k, shape, f32) * scale

    def gain(k, shape):
        return 1.0 + 0.02 * jax.random.normal(k, shape, f32)

    res = (2.0 * DEPTH) ** -0.5
    return {
        'x': nrm(ks[0], (BATCH, SEQ, D_MODEL), 1.0),
        'mem': nrm(ks[1], (BATCH, MEM_LEN, D_MODEL), 1.0),
        'norm_mix': gain(ks[2], (DEPTH, D_MODEL)),
        'norm_mem': gain(ks[3], (DEPTH, D_MODEL)),
        'w_mem_kv': nrm(ks[4], (DEPTH, D_MODEL, 2 * MEM_W), D_MODEL ** -0.5),
        'w_out': nrm(ks[5], (DEPTH, MIX_W, D_MODEL), MIX_W ** -0.5 * res),
        'norm_ffn': gain(ks[6], (DEPTH, D_MODEL)),
        'w_up': nrm(ks[7], (DEPTH, D_MODEL, 2 * FFN_DIM), D_MODEL ** -0.5),
        'conv_w': nrm(ks[8], (DEPTH, CONV_WIDTH, 2 * FFN_DIM), CONV_WIDTH ** -0.5),
        'conv_b': nrm(ks[9], (DEPTH, 2 * FFN_DIM), 0.01),
        'w_down': nrm(ks[10], (DEPTH, FFN_DIM, D_MODEL), FFN_DIM ** -0.5 * res),
        'gla_w_in': nrm(ks[11], (N_A_LAYERS, D_MODEL, GLA_IN), D_MODEL ** -0.5),
        'gla_w_gate_up': nrm(ks[12], (N_A_LAYERS, GLA_RANK, GLA_HEADS * GLA_DK), GLA_RANK ** -0.5),
        'gla_b_gate': nrm(ks[13], (N_A_LAYERS, GLA_HEADS * GLA_DK), 0.01),
        'gla_out_norm': gain(ks[14], (N_A_LAYERS, GLA_DV)),
        'nsa_w_in': nrm(ks[15], (N_B_LAYERS, D_MODEL, NSA_IN), D_MODEL ** -0.5),
        'kv_norm': gain(ks[16], (D_MODEL,)),
        'w_kv_shared': nrm(ks[17], (D_MODEL, SHARED_KV_W), D_MODEL ** -0.5),
        'cmp_pos': nrm(ks[18], (2, CMP_BLOCK, NSA_HEAD_DIM), 0.1),
        'cmp_w1': nrm(ks[19], (2, CMP_BLOCK * NSA_HEAD_DIM, CMP_HIDDEN), (CMP_BLOCK * NSA_HEAD_DIM) ** -0.5),
        'cmp_b1': nrm(ks[20], (2, CMP_HIDDEN), 0.01),
        'cmp_w2': nrm(ks[21], (2, CMP_HIDDEN, NSA_HEAD_DIM), CMP_HIDDEN ** -0.5),
        'cmp_b2': nrm(ks[22], (2, NSA_HEAD_DIM), 0.01),
        'rel_bias': nrm(ks[23], (REL_BUCKETS, NSA_HEADS), 0.5),
        'final_norm': gain(ks[24], (D_MODEL,)),
    }


def reference(x, mem, norm_mix, norm_mem, w_mem_kv, w_out, norm_ffn, w_up, conv_w, conv_b, w_down,
              gla_w_in, gla_w_gate_up, gla_b_gate, gla_out_norm, nsa_w_in, kv_norm, w_kv_shared,
              cmp_pos, cmp_w1, cmp_b1, cmp_w2, cmp_b2, rel_bias, final_norm):
    B, S = x.shape[0], x.shape[1]
    h = x
    shared = None
    for i in range(DEPTH):
        if i == N_A_LAYERS:
            shared = shared_kv(h, kv_norm, w_kv_shared, cmp_pos, cmp_w1, cmp_b1, cmp_w2, cmp_b2)
        xn = rmsnorm(h, norm_mix[i])
        mem_kv = rmsnorm(mem, norm_mem[i]) @ w_mem_kv[i]
        if i < N_A_LAYERS:
            q, k, v, g, lr, mq = split_cols(xn @ gla_w_in[i], GLA_SIZES)
            log_a = jax.nn.log_sigmoid((lr @ gla_w_gate_up[i] + gla_b_gate[i]).astype(jnp.float32)) / GLA_GATE_NORM
            main = gla_mixer(q.reshape(B, S, GLA_HEADS, GLA_DK), k.reshape(B, S, GLA_HEADS, GLA_DK),
                             v.reshape(B, S, GLA_HEADS, GLA_DV), log_a.reshape(B, S, GLA_HEADS, GLA_DK),
                             g.reshape(B, S, GLA_HEADS, GLA_DV), gla_out_norm[i])
        else:
            q, gl, mq = split_cols(xn @ nsa_w_in[i - N_A_LAYERS], NSA_SIZES)
            gates = jax.nn.sigmoid(gl.astype(jnp.float32)).reshape(B, S, NSA_HEADS, 3)
            main = nsa_mixer(q.reshape(B, S, NSA_HEADS, NSA_HEAD_DIM), gates, shared, rel_bias)
        mo = mem_attention(mq, mem_kv)
        h = h + jnp.concatenate([main.astype(xn.dtype), mo.astype(xn.dtype)], axis=-1) @ w_out[i]
        h = h + conv_ffn(rmsnorm(h, norm_ffn[i]), w_up[i], conv_w[i], conv_b[i], w_down[i])
    return rmsnorm(h, final_norm)
```

```python
import numpy as np
import concourse.bass as bass
import concourse.mybir as mybir
from concourse.bass_utils import run_bass_kernel_spmd
from contextlib import ExitStack

F32 = mybir.dt.float32
BF16 = mybir.dt.bfloat16
AF = mybir.ActivationFunctionType
ALU = mybir.AluOpType
AX = mybir.AxisListType

D = 1024
S = 4096
DEPTH = 4
FFN = 2816
MEM = 256
EPS = 1e-6
NDSEM = 24
SAME_ENGINE_SYNC = True


class Op:
    __slots__ = ("eng", "fn", "dma", "idx", "waits", "signal", "sigval", "dsem", "dval")


class Prog:
    ENGS = ("pe", "act", "dve", "pool", "sync")

    def __init__(self, nc):
        self.nc = nc
        self.ops = {e: [] for e in self.ENGS}
        self.last_w = {}
        self.readers = {}
        self.waited_c = {e: {x: -1 for x in self.ENGS} for e in self.ENGS}
        self.waited_d = {e: {} for e in self.ENGS}
        self.ndma = 0
        self.dma_since_barrier = {}
        self.out_dmas = []

    def _add_dep(self, o, d):
        if d is None or d is o:
            return
        e = o.eng
        if d.dma:
            if self.waited_d[e].get(d.dsem, 0) >= d.dval:
                return
            self.waited_d[e][d.dsem] = d.dval
            o.waits.append(("d", d.dsem, d.dval))
            return
        if d.eng == e:
            if e == "pe" or not SAME_ENGINE_SYNC:
                return
        if self.waited_c[e][d.eng] >= d.idx:
            return
        self.waited_c[e][d.eng] = d.idx
        d.signal = True
        o.waits.append(("c", d.eng, d))

    def op(self, eng, fn, r=(), w=(), dma=False):
        o = Op()
        o.eng = eng
        o.fn = fn
        o.dma = dma
        o.idx = len(self.ops[eng])
        o.waits = []
        o.signal = False
        o.sigval = 0
        o.dsem = o.dval = None
        if dma:
            n = self.ndma
            self.ndma += 1
            o.dsem = n % NDSEM
            o.dval = 16 * (n // NDSEM + 1)
            if n >= NDSEM:
                prev = 16 * (n // NDSEM)
                if self.waited_d[eng].get(o.dsem, 0) < prev:
                    self.waited_d[eng][o.dsem] = prev
                    o.waits.append(("d", o.dsem, prev))
            self.dma_since_barrier[o.dsem] = o
        for b in r:
            self._add_dep(o, self.last_w.get(b))
            if isinstance(b, tuple) and b[0] == "ps":
                for t in self.readers.get(b, ()):
                    if t.eng != eng:
                        self._add_dep(o, t)
        for b in w:
            self._add_dep(o, self.last_w.get(b))
            for t in self.readers.get(b, ()):
                self._add_dep(o, t)
        for b in r:
            self.readers.setdefault(b, []).append(o)
        for b in w:
            self.last_w[b] = o
            self.readers[b] = []
        self.ops[eng].append(o)
        return o

    def dma(self, eng, out, in_, r=(), w=()):
        return self.op(eng, lambda e: e.dma_start(out=out, in_=in_), r=r, w=w, dma=True)

    def barrier(self):
        lasts = {}
        for e in self.ENGS:
            real = [o for o in self.ops[e][-64:] if o.fn is not None]
            if not real:
                real = [o for o in self.ops[e] if o.fn is not None]
            lasts[e] = real[-1] if real else None
        dmas = list(self.dma_since_barrier.values())
        for e in self.ENGS:
            o = self.op(e, None)
            for x in self.ENGS:
                d = lasts[x]
                if d is not None and x != e and not d.dma:
                    self._add_dep(o, d)
                elif d is not None and d.dma:
                    self._add_dep(o, d)
            for d in dmas:
                self._add_dep(o, d)
        self.dma_since_barrier = {}
        self.last_w = {}
        self.readers = {}

    def emit(self, block, sems, dsems):
        nc = self.nc
        for e in self.ENGS:
            c = 0
            for o in self.ops[e]:
                if o.signal:
                    c += 1
                    o.sigval = c
        self.sig_counts = {e: sum(1 for o in self.ops[e] if o.signal) for e in self.ENGS}

        def run(ename, eng):
            for o in self.ops[ename]:
                for wt in o.waits:
                    if wt[0] == "d":
                        eng.wait_ge(dsems[wt[1]], wt[2])
                    else:
                        eng.wait_ge(sems[wt[1]], wt[2].sigval)
                if o.fn is None:
                    continue
                ins = o.fn(eng)
                if o.dma:
                    ins.then_inc(dsems[o.dsem], 16)
                elif o.signal:
                    ins.then_inc(sems[ename], 1)

        @block.tensor
        def _(eng):
            run("pe", eng)

        @block.scalar
        def _(eng):
            run("act", eng)

        @block.vector
        def _(eng):
            run("dve", eng)

        @block.gpsimd
        def _(eng):
            run("pool", eng)

        @block.sync
        def _(eng):
            run("sync", eng)


class Arena:
    def __init__(self, tens, nwords):
        self.t = tens
        self.n = nwords
        self.off = 0

    def reset(self):
        self.off = 0

    def f32(self, shape):
        n = int(np.prod(shape[1:]))
        assert self.off + n <= self.n, ("arena overflow", self.off, n, self.n)
        ap = self.t[0:shape[0], self.off:self.off + n]
        self.off += n
        return _shape(ap, shape)

    def bf16(self, shape):
        n = int(np.prod(shape[1:]))
        nw = (n + 1) // 2
        assert self.off + nw <= self.n, ("arena overflow", self.off, nw, self.n)
        ap = self.t[0:shape[0], self.off:self.off + nw].bitcast(BF16)
        if 2 * nw != n:
            ap = ap[:, 0:n]
        self.off += nw
        return _shape(ap, shape)


def _shape(ap, shape):
    if len(shape) == 2:
        return ap
    if len(shape) == 3:
        return ap.rearrange("p (a b) -> p a b", a=shape[1], b=shape[2])
    if len(shape) == 4:
        return ap.rearrange("p (a b c) -> p a b c", a=shape[1], b=shape[2], c=shape[3])
    raise ValueError(shape)


class Ctx:
    pass


def load_cast(P, cx, dst, src, key, shape, cast_i):
    sb = cast_i % 2
    n = int(np.prod(shape[1:]))
    stg = _shape(cx.stg[sb][0:shape[0], 0:n], shape)
    P.dma("sync", stg, src, w=[("stg", sb)])
    eng = ("dve", "pool", "act")[cast_i % 3]
    if eng == "act":
        P.op("act", lambda e: e.copy(out=dst, in_=stg), r=[("stg", sb)], w=[key])
    else:
        P.op(eng, lambda e: e.tensor_copy(out=dst, in_=stg), r=[("stg", sb)], w=[key])


def rms_chunk(P, cx, Hc, hkey, gcol, xnT, xkey, TC, pool_share=False):
    ps = cx.ps[0][:, 0:TC]
    for kc in range(8):
        sq = cx.sq[kc % 2][:, 0:TC]
        P.op("act", lambda e, sq=sq, kc=kc: e.activation(out=sq, in_=Hc[:, kc, :], func=AF.Square),
             r=[hkey], w=[("sq", kc % 2)])
        P.op("pe", lambda e, sq=sq, kc=kc: e.matmul(ps, lhsT=cx.ones_bf[:, :], rhs=sq, start=(kc == 0), stop=(kc == 7)),
             r=[("sq", kc % 2)], w=[("ps", 0)])
    rstd = cx.rstd[:, 0:TC]
    P.op("dve", lambda e: e.tensor_scalar(out=rstd, in0=ps, scalar1=EPS, scalar2=None, op0=ALU.add),
         r=[("ps", 0)], w=["rstd"])
    P.op("act", lambda e: e.activation(out=rstd, in_=rstd, func=AF.Sqrt), r=["rstd"], w=["rstd"])
    P.op("dve", lambda e: e.reciprocal(out=rstd, in_=rstd), r=["rstd"], w=["rstd"])
    for kc in range(8):
        eng = "pool" if (pool_share and kc % 2 == 1) else "dve"
        P.op(eng, lambda e, kc=kc: e.scalar_tensor_tensor(out=xnT[:, kc, :], in0=Hc[:, kc, :], scalar=gcol[:, kc:kc + 1],
                                                          in1=rstd, op0=ALU.mult, op1=ALU.mult),
             r=[hkey, "rstd", "gn"], w=[(xkey, kc)])


def MM(P, out, lhsT, rhs, start, stop, r, w, **kw):
    return P.op("pe", lambda e: e.matmul(out, lhsT=lhsT, rhs=rhs, start=start, stop=stop, **kw), r=r, w=w)


def ACT(P, out, in_, func, r, w, **kw):
    return P.op("act", lambda e: e.activation(out=out, in_=in_, func=func, **kw), r=r, w=w)


def TT(P, eng, out, in0, in1, op, r, w):
    return P.op(eng, lambda e: e.tensor_tensor(out=out, in0=in0, in1=in1, op=op), r=r, w=w)


def STT(P, out, in0, scalar, in1, op0, op1, r, w):
    return P.op("dve", lambda e: e.scalar_tensor_tensor(out=out, in0=in0, scalar=scalar, in1=in1, op0=op0, op1=op1), r=r, w=w)


def TS(P, eng, out, in0, s1, s2, op0, op1, r, w):
    if s2 is None:
        return P.op(eng, lambda e: e.tensor_scalar(out=out, in0=in0, scalar1=s1, scalar2=None, op0=op0), r=r, w=w)
    return P.op(eng, lambda e: e.tensor_scalar(out=out, in0=in0, scalar1=s1, scalar2=s2, op0=op0, op1=op1), r=r, w=w)


def CP(P, eng, out, in_, r, w):
    if eng == "act":
        return P.op("act", lambda e: e.copy(out=out, in_=in_), r=r, w=w)
    return P.op(eng, lambda e: e.tensor_copy(out=out, in_=in_), r=r, w=w)


def MEMSET(P, eng, out, val, w):
    return P.op(eng, lambda e: e.memset(out, val), w=w)


def RECIP(P, out, in_, r, w):
    return P.op("dve", lambda e: e.reciprocal(out=out, in_=in_), r=r, w=w)


def ffn_pass(P, cx, l, h_in, h_out):
    TC = 256
    NCH = S // TC
    A = cx.arena
    A.reset()
    WU = A.bf16([128, 8, 2 * FFN])
    WD = A.bf16([128, 22, D])
    cx.stg = [A.f32([128, 2048]), A.f32([128, 2048])]
    Hb = [_shape(cx.stg[i], [128, 8, TC]) for i in range(2)]
    xnT = A.bf16([128, 8, TC])
    cx.sq = [A.bf16([128, TC]) for _ in range(2)]
    cx.rstd = A.f32([128, TC])
    U = [A.f32([128, TC + 2]) for _ in range(4)]
    T1 = [A.f32([128, TC]) for _ in range(4)]
    SA = [A.f32([128, TC]) for _ in range(2)]
    actT = A.bf16([128, 22, TC])
    HALO = A.f32([128, 44, 2])
    cw = A.f32([128, 44, 3])
    cb = A.f32([128, 44])
    gn = A.f32([128, 8])

    P.dma("sync", cw, cx.d["conv_wT"][l], w=["cw"])
    P.dma("sync", cb, cx.d["conv_bT"][l], w=["cb"])
    P.dma("sync", gn, cx.d["norm_ffnT"][l], w=["gn"])
    wu_src = cx.d["w_up"][l].rearrange("(kc p) c -> p kc c", p=128)
    ci = 0
    for c0 in range(0, 2 * FFN, 256):
        load_cast(P, cx, WU[:, :, c0:c0 + 256], wu_src[:, :, c0:c0 + 256], "WU", [128, 8, 256], ci)
        ci += 1
    wd_src = cx.d["w_down"][l].rearrange("(fc p) n -> p fc n", p=128)
    for f0 in range(0, 22, 2):
        load_cast(P, cx, WD[:, f0:f0 + 2, :], wd_src[:, f0:f0 + 2, :], "WD", [128, 2, D], ci)
        ci += 1
    P.op("pool", lambda e: e.memset(HALO, 0.0), w=["halo"])

    hin_v = h_in.rearrange("(kc p) t -> p kc t", p=128)
    P.dma("sync", Hb[0], hin_v[:, :, 0:TC], w=[("H", 0), ("stg", 0)])
    for ch in range(NCH):
        t0 = ch * TC
        b = ch % 2
        Hc = Hb[b]
        hkey = ("H", b)
        if ch + 1 < NCH:
            P.dma("sync", Hb[1 - b], hin_v[:, :, t0 + TC:t0 + 2 * TC], w=[("H", 1 - b), ("stg", 1 - b)])
        rms_chunk(P, cx, Hc, hkey, gn, xnT, "xnT", TC)
        for fc in range(22):
            tt = []
            for half in range(2):
                cc = fc + 22 * half
                ui = (2 * fc + half) % 4
                pst = cx.ps[1 + ui][:, 0:TC]
                pk = ("psu", ui)
                for kc in range(8):
                    P.op("pe", lambda e, pst=pst, kc=kc, cc=cc: e.matmul(
                        pst, lhsT=WU[:, kc, cc * 128:(cc + 1) * 128], rhs=xnT[:, kc, :], start=(kc == 0), stop=(kc == 7)),
                        r=[("xnT", kc), "WU"], w=[pk])
                Ut = U[ui]
                uk = ("U", ui)
                t1 = T1[ui]
                tk = ("T1", ui)
                P.op("pool", lambda e, Ut=Ut, cc=cc: e.tensor_copy(out=Ut[:, 0:2], in_=HALO[:, cc, :]), r=["halo"], w=[uk])
                P.op("act", lambda e, Ut=Ut, pst=pst: e.copy(out=Ut[:, 2:TC + 2], in_=pst), r=[pk], w=[uk])
                P.op("act", lambda e, t1=t1, pst=pst, cc=cc: e.activation(
                    out=t1, in_=pst, func=AF.Identity, bias=cb[:, cc:cc + 1], scale=cw[:, cc, 2:3]),
                    r=[pk, "cw", "cb"], w=[tk])
                P.op("dve", lambda e, t1=t1, Ut=Ut, cc=cc: e.scalar_tensor_tensor(
                    out=t1, in0=Ut[:, 1:TC + 1], scalar=cw[:, cc, 1:2], in1=t1, op0=ALU.mult, op1=ALU.add),
                    r=[uk, tk], w=[tk])
                P.op("dve", lambda e, t1=t1, Ut=Ut, cc=cc: e.scalar_tensor_tensor(
                    out=t1, in0=Ut[:, 0:TC], scalar=cw[:, cc, 0:1], in1=t1, op0=ALU.mult, op1=ALU.add),
                    r=[uk, tk], w=[tk])
                P.op("pool", lambda e, Ut=Ut, cc=cc: e.tensor_copy(out=HALO[:, cc, :], in_=Ut[:, TC:TC + 2]), r=[uk], w=["halo"])
                tt.append((t1, tk))
            sa = SA[fc % 2]
            sk = ("SA", fc % 2)
            P.op("act", lambda e, sa=sa, ta=tt[0][0]: e.activation(out=sa, in_=ta, func=AF.Silu), r=[tt[0][1]], w=[sk])
            P.op("pool", lambda e, sa=sa, tb=tt[1][0], fc=fc: e.tensor_tensor(out=actT[:, fc, :], in0=sa, in1=tb, op=ALU.mult),
                 r=[sk, tt[1][1]], w=[("actT", fc)])
        for n in range(8):
            pd = cx.ps[5 + n % 2][:, 0:TC]
            pk = ("psd", n % 2)
            for fc in range(22):
                P.op("pe", lambda e, pd=pd, fc=fc, n=n: e.matmul(
                    pd, lhsT=WD[:, fc, n * 128:(n + 1) * 128], rhs=actT[:, fc, :], start=(fc == 0), stop=(fc == 21)),
                    r=[("actT", fc), "WD"], w=[pk])
            P.op("dve", lambda e, pd=pd, n=n, Hc=Hc: e.tensor_tensor(out=Hc[:, n, :], in0=Hc[:, n, :], in1=pd, op=ALU.add),
                 r=[pk, hkey], w=[hkey])
        P.dma("sync", h_out.rearrange("(kc p) t -> p kc t", p=128)[:, :, t0:t0 + TC], Hc, r=[hkey])
    P.barrier()


class PsPool:
    def __init__(self, cx, banks):
        self.cx = cx
        self.banks = banks
        self.i = 0

    def get(self):
        b = self.banks[self.i % len(self.banks)]
        self.i += 1
        return self.cx.ps[b], ("ps", b)


def mem_setup(P, cx, l, WMK, xmT, KmT, Vm, gm, Hstage):
    pp = PsPool(cx, [1, 2, 3])
    P.dma("sync", gm, cx.d["norm_memT"][l], w=["gn"])
    src = cx.d["w_mem_kv"][l].rearrange("(kc p) c -> p kc c", p=128)
    for i, c0 in enumerate(range(0, 512, 256)):
        load_cast(P, cx, WMK[:, :, c0:c0 + 256], src[:, :, c0:c0 + 256], "WMK", [128, 8, 256], i)
    Hm = _shape(Hstage[:, 0:8 * MEM], [128, 8, MEM])
    P.dma("sync", Hm, cx.d["memT"].rearrange("(kc p) t -> p kc t", p=128), w=[("stg", 0)])
    rms_chunk(P, cx, Hm, ("stg", 0), gm, xmT, "xmT", MEM)
    for h in range(4):
        ps, pk = pp.get()
        for kc in range(8):
            MM(P, ps[0:64, 0:MEM], WMK[:, kc, h * 64:(h + 1) * 64], xmT[:, kc, :], kc == 0, kc == 7,
               r=[("xmT", kc), "WMK"], w=[pk])
        CP(P, "act", KmT[0:64, h, :], ps[0:64, 0:MEM], r=[pk], w=["KmT"])
    for mt in range(2):
        ps, pk = pp.get()
        for kc in range(8):
            MM(P, ps[:, 0:256], xmT[:, kc, mt * 128:(mt + 1) * 128], WMK[:, kc, 256:512], kc == 0, kc == 7,
               r=[("xmT", kc), "WMK"], w=[pk])
        CP(P, "dve", Vm[:, mt, :], ps[:, 0:256], r=[pk], w=["Vm"])


def mem_attn_chunk(P, cx, pp, MQ, KmT, Vm, PT, Rb, MIXT, TC):
    for h in range(4):
        po = (h % 2) * 64
        for mt in range(2):
            ps, pk = pp.get()
            MM(P, ps[:, 0:TC], KmT[0:64, h, mt * 128:(mt + 1) * 128], MQ[0:64, h, :], True, True,
               r=["KmT", ("MQ", h)], w=[pk])
            ACT(P, PT[mt][:, 0:TC], ps[:, 0:TC], AF.Exp, r=[pk], w=[("PT", mt)], scale=0.125)
        pso = cx.ps[4]
        pss = cx.ps[5]
        for mt in range(2):
            MM(P, pso[po:po + 64, 0:TC], Vm[:, mt, h * 64:(h + 1) * 64], PT[mt][:, 0:TC], mt == 0, mt == 1,
               r=["Vm", ("PT", mt)], w=[("ps", 4)])
        for mt in range(2):
            MM(P, pss[:, 0:TC], cx.ones1[:, :], PT[mt][:, 0:TC], mt == 0, mt == 1,
               r=[("PT", mt)], w=[("ps", 5)])
        RECIP(P, Rb[:, 0:TC], pss[:, 0:TC], r=[("ps", 5)], w=["Rb"])
        TT(P, "dve", MIXT[po:po + 64, 6 + h // 2, :], pso[po:po + 64, 0:TC], Rb[po:po + 64, 0:TC], ALU.mult,
           r=[("ps", 4), "Rb"], w=[("MIXT", 6 + h // 2)])


def out_proj_chunk(P, cx, pp, WO, MIXT, Hc, hkey, TC):
    for n in range(8):
        ps, pk = pp.get()
        for c in range(8):
            MM(P, ps[:, 0:TC], WO[:, c, n * 128:(n + 1) * 128], MIXT[:, c, :], c == 0, c == 7,
               r=[("MIXT", c), "WO"], w=[pk])
        TT(P, "dve", Hc[:, n, :], Hc[:, n, :], ps[:, 0:TC], ALU.add, r=[pk, hkey], w=[hkey])


def gla_pass(P, cx, l, h_in, h_out):
    TC = 512
    NCH = S // TC
    A = cx.arena
    A.reset()
    WIN = A.bf16([128, 8, 2576])
    WO = A.bf16([128, 8, D])
    Hst = A.f32([128, 4096])
    cx.stg = [Hst[:, 0:2048], Hst[:, 2048:4096]]
    Hc = _shape(Hst, [128, 8, TC])
    hkey = "H"
    xnT = A.bf16([128, 8, TC])
    cx.sq = [A.bf16([128, TC]) for _ in range(2)]
    cx.rstd = A.f32([128, TC])
    gn = A.f32([128, 8])
    gm = A.f32([128, 8])
    LR = A.bf16([32, TC])
    WGf = A.f32([32, 384])
    WGa = A.bf16([32, 384])
    LA = A.f32([128, 4, 384])
    E1 = A.f32([128, 384])
    EQ = A.f32([96, 4, TC])
    EK = A.f32([96, 4, TC])
    EKO = A.f32([128, 4, 384])
    QIN = A.bf16([96, 4, TC])
    KDEC = A.bf16([96, 4, TC])
    V = A.bf16([128, 4, 768])
    GS = A.f32([128, 768])
    GG = A.bf16([128, 4, 768])
    KOUT = A.bf16([128, 4, 384])
    MQ = A.bf16([64, 4, TC])
    Sst = A.f32([96, 4, 192])
    Sbf = A.bf16([96, 4, 192])
    ATm = [A.bf16([128, 128]) for _ in range(2)]
    MAIN = [A.bf16([128, 768]) for _ in range(2)]
    MIXT = A.bf16([128, 8, TC])
    PT = [A.bf16([128, TC]) for _ in range(2)]
    Rb = A.f32([128, TC])
    KmT = A.bf16([64, 4, MEM])
    Vm = A.bf16([128, 2, MEM])
    ON = A.f32([128, 192])
    SS = A.f32([128, 4])
    JUNK = A.f32([128, 192])
    xmT = A.bf16([128, 8, MEM])
    WMK = A.bf16([128, 8, 512])

    pp = PsPool(cx, [1, 2, 3])
    P.dma("sync", gn, cx.d["norm_mixT"][l], w=["gn"])
    mem_setup(P, cx, l, WMK, xmT, KmT, Vm, gm, Hst)
    P.dma("sync", gn, cx.d["norm_mixT"][l], w=["gn"])
    P.dma("sync", ON, cx.d["gla_out_norm_rep"][l], w=["ON"])
    src = cx.d["gla_w_in"][l].rearrange("(kc p) c -> p kc c", p=128)
    ci = 0
    for c0 in range(0, 2576, 256):
        c1 = min(2576, c0 + 256)
        load_cast(P, cx, WIN[:, :, c0:c1], src[:, :, c0:c1], "WIN", [128, 8, c1 - c0], ci)
        ci += 1
    src = cx.d["w_out"][l].rearrange("(kc p) c -> p kc c", p=128)
    for c0 in range(0, D, 256):
        load_cast(P, cx, WO[:, :, c0:c0 + 256], src[:, :, c0:c0 + 256], "WO", [128, 8, 256], ci)
        ci += 1
    MEMSET(P, "dve", WGf, 0.0, w=["WGf"])
    P.dma("sync", WGf[0:16, :], cx.d["gla_w_gate_up"][l], w=["WGf"])
    P.dma("sync", WGf[16:17, :], cx.d["gla_b_gate"][l:l + 1, :], w=["WGf"])
    CP(P, "dve", WGa, WGf, r=["WGf"], w=["WGa"])
    MEMSET(P, "dve", LR, 1.0, w=["LR"])
    MEMSET(P, "dve", Sst, 0.0, w=["S"])
    MEMSET(P, "dve", Sbf, 0.0, w=["Sbf"])

    hin_v = h_in.rearrange("(kc p) t -> p kc t", p=128)
    hout_v = h_out.rearrange("(kc p) t -> p kc t", p=128)
    for ch in range(NCH):
        t0 = ch * TC
        P.dma("sync", Hc, hin_v[:, :, t0:t0 + TC], w=[hkey, ("stg", 0), ("stg", 1)])
        rms_chunk(P, cx, Hc, hkey, gn, xnT, "xnT", TC)
        xr = [("xnT", kc) for kc in range(8)]
        ps, pk = pp.get()
        for kc in range(8):
            MM(P, ps[0:16, 0:TC], WIN[:, kc, 2304:2320], xnT[:, kc, :], kc == 0, kc == 7, r=[("xnT", kc), "WIN"], w=[pk])
        CP(P, "act", LR[0:16, :], ps[0:16, 0:TC], r=[pk], w=["LR"])
        for s in range(4):
            ts = slice(s * 128, (s + 1) * 128)
            ps, pk = pp.get()
            MM(P, ps[:, 0:384], LR[0:32, ts], WGa[0:32, :], True, True, r=["LR", "WGa"], w=[pk])
            ACT(P, E1, ps[:, 0:384], AF.Exp, r=[pk], w=["E1"], scale=-1.0)
            ACT(P, LA[:, s, :], E1, AF.Ln, r=["E1"], w=[("LA", s)], bias=cx.one_col[:, 0:1])
            psb = cx.ps[6]
            for h in range(4):
                MM(P, psb[0:96, h * 128:(h + 1) * 128], LA[:, s, h * 96:(h + 1) * 96], cx.triu[:, :], True, True,
                   r=[("LA", s)], w=[("ps", 6)])
            ACT(P, EQ[:, :, ts], psb[0:96, :].rearrange("p (h t) -> p h t", h=4), AF.Exp, r=[("ps", 6)], w=[("EQ", s)])
            ACT(P, EK[:, :, ts], psb[0:96, :].rearrange("p (h t) -> p h t", h=4), AF.Exp, r=[("ps", 6)], w=[("EK", s)], scale=-1.0)
            psl = cx.ps[7]
            MM(P, psl[:, 0:384], cx.strictl[:, :], LA[:, s, :], True, True, r=[("LA", s)], w=[("ps", 7)])
            ACT(P, EKO[:, s, :], psl[:, 0:384], AF.Exp, r=[("ps", 7)], w=[("EKO", s)])
        eqr = [("EQ", s) for s in range(4)]
        ekr = [("EK", s) for s in range(4)]
        for h in range(4):
            ps, pk = pp.get()
            for kc in range(8):
                MM(P, ps[0:96, 0:TC], WIN[:, kc, h * 96:(h + 1) * 96], xnT[:, kc, :], kc == 0, kc == 7, r=[("xnT", kc), "WIN"], w=[pk])
            STT(P, QIN[:, h, :], ps[0:96, 0:TC], float(96 ** -0.5), EQ[:, h, :], ALU.mult, ALU.mult, r=[pk] + eqr, w=[("QIN", h)])
            ps, pk = pp.get()
            for kc in range(8):
                MM(P, ps[0:96, 0:TC], WIN[:, kc, 384 + h * 96:384 + (h + 1) * 96], xnT[:, kc, :], kc == 0, kc == 7, r=[("xnT", kc), "WIN"], w=[pk])
            TT(P, "dve", KDEC[:, h, :], ps[0:96, 0:TC], EK[:, h, :], ALU.mult, r=[pk] + ekr, w=[("KDEC", h)])
            ps, pk = pp.get()
            for kc in range(8):
                MM(P, ps[0:64, 0:TC], WIN[:, kc, 2320 + h * 64:2320 + (h + 1) * 64], xnT[:, kc, :], kc == 0, kc == 7, r=[("xnT", kc), "WIN"], w=[pk])
            CP(P, "act", MQ[:, h, :], ps[0:64, 0:TC], r=[pk], w=[("MQ", h)])
        for s in range(4):
            ts = slice(s * 128, (s + 1) * 128)
            for half in range(2):
                ps, pk = pp.get()
                for kc in range(8):
                    MM(P, ps[:, 0:384], xnT[:, kc, ts], WIN[:, kc, 768 + half * 384:768 + (half + 1) * 384], kc == 0, kc == 7,
                       r=[("xnT", kc), "WIN"], w=[pk])
                CP(P, "act", V[:, s, half * 384:(half + 1) * 384], ps[:, 0:384], r=[pk], w=[("V", s)])
            for half in range(2):
                ps, pk = pp.get()
                for kc in range(8):
                    MM(P, ps[:, 0:384], xnT[:, kc, ts], WIN[:, kc, 1536 + half * 384:1536 + (half + 1) * 384], kc == 0, kc == 7,
                       r=[("xnT", kc), "WIN"], w=[pk])
                ACT(P, GS[:, half * 384:(half + 1) * 384], ps[:, 0:384], AF.Silu, r=[pk], w=[("GS", half)])
            TT(P, "pool", GG[:, s, :].rearrange("p (h v) -> p h v", h=4), GS.rearrange("p (h v) -> p h v", h=4),
               ON[:, None, :].to_broadcast([128, 4, 192]), ALU.mult, r=[("GS", 0), ("GS", 1), "ON"], w=[("GG", s)])
            ps, pk = pp.get()
            for kc in range(8):
                MM(P, ps[:, 0:384], xnT[:, kc, ts], WIN[:, kc, 384:768], kc == 0, kc == 7, r=[("xnT", kc), "WIN"], w=[pk])
            TT(P, "dve", KOUT[:, s, :], ps[:, 0:384], EKO[:, s, :], ALU.mult, r=[pk, ("EKO", s)], w=[("KOUT", s)])
            MEMSET(P, "dve", SS, 0.0, w=["SS"])
            for h in range(4):
                ps, pk = pp.get()
                MM(P, ps[:, 0:128], KDEC[0:96, h, ts], QIN[0:96, h, ts], True, True, r=[("KDEC", h), ("QIN", h)], w=[pk])
                at = ATm[h % 2]
                TT(P, "dve", at, ps[:, 0:128], cx.maskji[:, :], ALU.mult, r=[pk], w=[("ATm", h % 2)])
                pso = cx.ps[4 + h]
                ok = ("ps", 4 + h)
                osl = slice(0, 192)
                MM(P, pso[:, osl], at, V[:, s, h * 192:(h + 1) * 192], True, False, r=[("ATm", h % 2), ("V", s)], w=[ok])
                MM(P, pso[:, osl], QIN[0:96, h, ts], Sbf[0:96, h, :], False, True, r=[("QIN", h), "Sbf"], w=[ok])
                ps, pk = pp.get()
                MM(P, ps[0:96, 0:192], KOUT[:, s, h * 96:(h + 1) * 96], V[:, s, h * 192:(h + 1) * 192], True, True,
                   r=[("KOUT", s), ("V", s)], w=[pk])
                STT(P, Sst[:, h, :], Sst[:, h, :], EQ[:, h, s * 128 + 127:s * 128 + 128], ps[0:96, 0:192], ALU.mult, ALU.add,
                    r=[pk, "S", ("EQ", s)], w=["S"])
                CP(P, "act", Sbf[:, h, :], Sst[:, h, :], r=["S"], w=["Sbf"])
                ACT(P, JUNK, pso[:, osl], AF.Square, r=[ok], w=["JUNK", "SS"], accum_out=SS[:, h:h + 1])
            TS(P, "dve", SS, SS, 1.0 / 192, EPS, ALU.mult, ALU.add, r=["SS"], w=["SS"])
            ACT(P, SS, SS, AF.Sqrt, r=["SS"], w=["SS"])
            RECIP(P, SS, SS, r=["SS"], w=["SS"])
            mn = MAIN[s % 2]
            mk = ("MAIN", s % 2)
            for h in range(4):
                pso = cx.ps[4 + h]
                ok = ("ps", 4 + h)
                osl = slice(0, 192)
                STT(P, mn[:, h * 192:(h + 1) * 192], pso[:, osl], SS[:, h:h + 1], GG[:, s, h * 192:(h + 1) * 192], ALU.mult, ALU.mult,
                    r=[ok, "SS", ("GG", s)], w=[mk])
            for c in range(6):
                ps, pk = pp.get()
                MM(P, ps[:, 0:128], mn[:, c * 128:(c + 1) * 128], cx.ident[:, :], True, True, r=[mk], w=[pk])
                CP(P, "act" if c % 2 == 0 else "dve", MIXT[:, c, ts], ps[:, 0:128], r=[pk], w=[("MIXT", c)])
        mem_attn_chunk(P, cx, pp, MQ, KmT, Vm, PT, Rb, MIXT, TC)
        out_proj_chunk(P, cx, pp, WO, MIXT, Hc, hkey, TC)
        P.dma("sync", hout_v[:, :, t0:t0 + TC], Hc, r=[hkey])
    P.barrier()


def kv_pass(P, cx, h_in):
    TC = 512
    NCH = S // TC
    A = cx.arena
    A.reset()
    WKV = A.bf16([128, 8, 1536])
    Hst = A.f32([128, 4096])
    cx.stg = [Hst[:, 0:2048], Hst[:, 2048:4096]]
    Hc = _shape(Hst, [128, 8, TC])
    xnT = A.bf16([128, 8, TC])
    cx.sq = [A.bf16([128, TC]) for _ in range(2)]
    cx.rstd = A.f32([128, TC])
    gn = A.f32([128, 8])
    KT = A.bf16([128, 4, S])
    CF = A.bf16([128, 4, S])
    VS = A.bf16([128, 32, 4, 65])
    VW = A.bf16([128, 32, 4, 65])
    W1f = A.f32([128, 32 * 128])
    W1 = A.bf16([128, 32, 128])
    W2f = A.f32([128, 2, 64])
    W2 = A.bf16([128, 2, 64])
    POSf = A.f32([128, 32])
    POS = A.bf16([128, 32])
    B1 = A.f32([128, 2])
    B2c = A.f32([64, 2])
    B2rf = A.f32([1, 64])
    B2r = A.bf16([1, 64])
    CST = A.f32([128, 2])
    HID = A.bf16([128, 256])
    KC = A.bf16([64, 4, 256])
    VC = A.bf16([128, 2, 4, 65])
    pp = PsPool(cx, [1, 2, 3, 4, 5, 6, 7])

    P.dma("sync", gn, cx.d["kv_normT"], w=["gn"])
    src = cx.d["w_kv_shared"].rearrange("(kc p) c -> p kc c", p=128)
    for i, c0 in enumerate(range(0, 1536, 256)):
        load_cast(P, cx, WKV[:, :, c0:c0 + 256], src[:, :, c0:c0 + 256], "WKV", [128, 8, 256], i)
    MEMSET(P, "pool", VS, 1.0, w=["VS"])
    MEMSET(P, "pool", VW, 1.0, w=["VW"])
    hin_v = h_in.rearrange("(kc p) t -> p kc t", p=128)
    for ch in range(NCH):
        t0 = ch * TC
        P.dma("sync", Hc, hin_v[:, :, t0:t0 + TC], w=["H", ("stg", 0), ("stg", 1)])
        rms_chunk(P, cx, Hc, "H", gn, xnT, "xnT", TC)
        for (j, dst, po, key) in ((0, CF, 0, "CF"), (1, CF, 64, "CF"), (2, KT, 0, "KT"), (4, KT, 64, "KT")):
            for g in range(4):
                ps, pk = pp.get()
                c0 = j * 256 + g * 64
                for kc in range(8):
                    MM(P, ps[po:po + 64, 0:TC], WKV[:, kc, c0:c0 + 64], xnT[:, kc, :], kc == 0, kc == 7,
                       r=[("xnT", kc), "WKV"], w=[pk])
                CP(P, "act" if g % 2 == 0 else "dve", dst[po:po + 64, g, t0:t0 + TC], ps[po:po + 64, 0:TC], r=[pk], w=[key])
        for s4 in range(4):
            tile = ch * 4 + s4
            ts = slice(s4 * 128, (s4 + 1) * 128)
            for (j, dst, key) in ((3, VS, "VS"), (5, VW, "VW")):
                ps, pk = pp.get()
                for kc in range(8):
                    MM(P, ps[:, 0:256], xnT[:, kc, ts], WKV[:, kc, j * 256:(j + 1) * 256], kc == 0, kc == 7,
                       r=[("xnT", kc), "WKV"], w=[pk])
                CP(P, "act" if j == 3 else "dve", dst[:, tile, :, 0:64], ps[:, 0:256].rearrange("p (g d) -> p g d", g=4),
                   r=[pk], w=[key])
    P.dma("sync", W1f[0:64, :].rearrange("p (l n) -> p l n", l=32), cx.d["cmp_w1"][0].rearrange("(l d) n -> d l n", d=64), w=["W1f"])
    P.dma("sync", W1f[64:128, :].rearrange("p (l n) -> p l n", l=32), cx.d["cmp_w1"][1].rearrange("(l d) n -> d l n", d=64), w=["W1f"])
    CP(P, "dve", W1.rearrange("p l n -> p (l n)"), W1f, r=["W1f"], w=["W1"])
    P.dma("sync", W2f, cx.d["cmp_w2"].rearrange("j n d -> n j d"), w=["W2f"])
    CP(P, "dve", W2, W2f, r=["W2f"], w=["W2"])
    P.dma("sync", POSf[0:64, :], cx.d["cmp_posT"][0], w=["POSf"])
    P.dma("sync", POSf[64:128, :], cx.d["cmp_posT"][1], w=["POSf"])
    CP(P, "dve", POS, POSf, r=["POSf"], w=["POS"])
    P.dma("sync", B1, cx.d["cmp_b1T"], w=["B1"])
    P.dma("sync", B2c, cx.d["cmp_b2T"], w=["B2c"])
    P.dma("sync", B2rf, cx.d["cmp_b2"][1:2, :], w=["B2rf"])
    CP(P, "dve", B2r, B2rf, r=["B2rf"], w=["B2r"])
    MEMSET(P, "dve", KC, 0.0, w=["KC"])
    MEMSET(P, "dve", VC, 0.0, w=["VC"])
    MEMSET(P, "dve", HID, 0.0, w=["HID"])
    for j in range(2):
        po = j * 64
        ps, pk = pp.get()
        for l in range(32):
            MM(P, ps[:, 0:1], W1[po:po + 64, l, :], POS[po:po + 64, l:l + 1], l == 0, l == 31, r=["W1", "POS"], w=[pk])
        TT(P, "dve", CST[:, j:j + 1], ps[:, 0:1], B1[:, j:j + 1], ALU.add, r=[pk, "B1"], w=["CST"])
        for g in range(4):
            ps, pk = pp.get()
            for l in range(32):
                MM(P, ps[:, 0:255], W1[po:po + 64, l, :], CF[po:po + 64, g, l:l + 16 * 254 + 1:16], l == 0, l == 31,
                   r=["W1", "CF"], w=[pk])
            ACT(P, HID[:, 0:255], ps[:, 0:255], AF.Silu, r=[pk, "CST"], w=["HID"], bias=CST[:, j:j + 1])
            if j == 0:
                ps, pk = pp.get()
                MM(P, ps[0:64, 0:255], W2[:, 0, :], HID[:, 0:255], True, True, r=["W2", "HID"], w=[pk])
                ACT(P, KC[:, g, 0:255], ps[0:64, 0:255], AF.Identity, r=[pk, "B2c"], w=["KC"], bias=B2c[:, 0:1])
            else:
                for kt in range(2):
                    n = 128 if kt == 0 else 127
                    ps, pk = pp.get()
                    MM(P, ps[0:n, 0:64], HID[:, kt * 128:kt * 128 + n], W2[:, 1, :], True, False, r=["W2", "HID"], w=[pk])
                    MM(P, ps[0:n, 0:64], cx.ones1[0:1, 0:n], B2r[0:1, :], False, True, r=["B2r"], w=[pk])
                    CP(P, "dve", VC[0:n, kt, g, 0:64], ps[0:n, 0:64], r=[pk], w=["VC"])
    MEMSET(P, "dve", VC[:, :, :, 64:65], 1.0, w=["VC"])
    P.dma("sync", cx.d_KT, KT.rearrange("p g t -> p (g t)"), r=["KT"])
    P.dma("sync", cx.d_VS, VS.rearrange("p a g d -> p (a g d)"), r=["VS"])
    P.dma("sync", cx.d_VW, VW.rearrange("p a g d -> p (a g d)"), r=["VW"])
    P.dma("sync", cx.d_KC, KC.rearrange("p g t -> p (g t)"), r=["KC"])
    P.dma("sync", cx.d_VC, VC.rearrange("p a g d -> p (a g d)"), r=["VC"])
    P.barrier()


NSA_U = 768
import os
NSA_DBG = int(os.environ.get('NSA_DBG', '9'))


def nsa_pass(P, cx, l, h_in, h_out):
    li = l - 2
    TC = 256
    NCH = S // TC
    A = cx.arena
    A.reset()
    Hst = A.f32([128, 4096])
    cx.stg = [Hst[:, 0:2048], Hst[:, 2048:4096]]
    Hc = _shape(Hst[:, 0:2048], [128, 8, TC])
    hkey = ("stg", 0)
    cx.sq = [A.bf16([128, TC]) for _ in range(2)]
    cx.rstd = A.f32([128, TC])
    gn = A.f32([128, 8])
    gm = A.f32([128, 8])
    KmT = A.bf16([64, 4, MEM])
    Vm = A.bf16([128, 2, MEM])
    mark = A.off
    xmT = A.bf16([128, 8, MEM])
    WMK = A.bf16([128, 8, 512])
    pp = PsPool(cx, [1, 2])
    psall = cx.psall
    mem_setup(P, cx, l, WMK, xmT, KmT, Vm, gm, Hst)
    P.barrier()
    A.off = mark
    WIN = A.bf16([128, 8, 1060])
    WO = A.bf16([128, 8, D])
    xnT = A.bf16([128, 8, TC])
    KT = A.bf16([128, 4, S])
    IND = A.bf16([128, S])
    VS = A.bf16([128, 32, 4, 65])
    VW = A.bf16([128, 32, 4, 65])
    KcT = A.bf16([64, 4, 256])
    VC = A.bf16([128, 2, 4, 65])
    OV = A.bf16([128, 2, 64])
    CPT = [A.f32([128, 12, 128]) for _ in range(2)]
    WMP = A.f32([128, 128])
    Fn = A.f32([64, 12, 12])
    RB = A.f32([33, 12])
    OHs = Hst[0:33, 2048:2048 + NSA_U]
    GVs = Hst[0:12, 2816:2816 + NSA_U]
    QS = A.bf16([128, 12, TC])
    QMP = [A.bf16([64, 12, 128]) for _ in range(2)]
    OT = [A.f32([65, 384]) for _ in range(2)]
    GT = A.f32([64, 4, 36])
    MQ = A.bf16([64, 4, TC])
    EC = A.f32([64, 3, 256])
    PC = A.bf16([64, 3, 256])
    PCT = A.bf16([128, 2, 192])
    SUMC = A.f32([64, 4])
    SC = A.f32([64, 4, 64])
    SC2 = A.f32([64, 64])
    M8 = A.f32([64, 16])
    SCT4 = A.f32([64, 4, 2, 64])
    CMPC = [A.bf16([64, 2, 12, 64]) for _ in range(2)]
    SMpad = A.bf16([64, 4, 128])
    SB = [A.f32([128, 384]) for _ in range(2)]
    PTs = [A.bf16([128, 384]) for _ in range(3)]
    MAINQ = A.bf16([64, 2, 768])
    TMP = [A.f32([64, 2, 3, 64]) for _ in range(2)]
    WX = A.f32([64, 2, 3, 4])
    MIXT = A.bf16([128, 8, TC])
    PT = [A.bf16([128, TC]) for _ in range(2)]
    Rb = A.f32([128, TC])
    P.dma("sync", gn, cx.d["norm_mixT"][l], w=["gn"])
    src = cx.d["nsa_w_in"][li].rearrange("(kc p) c -> p kc c", p=128)
    ci = 0
    for c0 in range(0, 1060, 256):
        c1 = min(1060, c0 + 256)
        load_cast(P, cx, WIN[:, :, c0:c1], src[:, :, c0:c1], "WIN", [128, 8, c1 - c0], ci)
        ci += 1
    src = cx.d["w_out"][l].rearrange("(kc p) c -> p kc c", p=128)
    for c0 in range(0, D, 256):
        load_cast(P, cx, WO[:, :, c0:c0 + 256], src[:, :, c0:c0 + 256], "WO", [128, 8, 256], ci)
        ci += 1
    for c0 in range(0, S, 2048):
        load_cast(P, cx, IND[:, c0:c0 + 2048], cx.d["ind"][:, c0:c0 + 2048], "IND", [128, 2048], ci)
        ci += 1
    load_cast(P, cx, OV.rearrange("p a j -> p (a j)"), cx.d["ovl"], "OV", [128, 128], ci)
    ci += 1
    P.dma("sync", KT.rearrange("p g t -> p (g t)"), cx.d_KT, w=["KT"])
    P.dma("sync", VS.rearrange("p a g d -> p (a g d)"), cx.d_VS, w=["VS"])
    P.dma("sync", VW.rearrange("p a g d -> p (a g d)"), cx.d_VW, w=["VW"])
    P.dma("sync", VC.rearrange("p a g d -> p (a g d)"), cx.d_VC, w=["VC"])
    P.dma("sync", KcT.rearrange("p g t -> p (g t)"), cx.d_KC, w=["KcT"])
    MEMSET(P, "dve", RB, -30000.0, w=["RB"])
    P.dma("sync", RB[0:32, :], cx.d["rel_bias"], w=["RB"])
    P.dma("sync", OHs, cx.d["oh_u"], w=[("stg", 1)])
    for u0 in range(0, NSA_U, 384):
        ps, pk = pp.get()
        MM(P, ps[0:12, 0:384], RB[0:33, :], OHs[0:33, u0:u0 + 384], True, True, r=["RB", ("stg", 1)], w=[pk])
        CP(P, "dve", GVs[:, u0:u0 + 384], ps[0:12, 0:384], r=[pk], w=[("stg", 1)])
    gv_w = P.dma("sync", cx.d_GV, GVs, r=[("stg", 1)], w=["dGV"])
    XH = _shape(Hst[:, 0:768], [128, 12, 64])
    for mi, m in enumerate((0, 64, 128, 192, 512, 576)):
        src = bass.AP(tensor=cx.d_GV.tensor, offset=m, ap=[[1, 128], [NSA_U, 12], [1, 64]])
        P.dma("sync", XH, src, r=["dGV"], w=[("stg", 0)])
        for half in range(2):
            ps, pk = pp.get()
            MM(P, ps[:, 0:384], cx.antiid[:, :], XH.rearrange("p h i -> p (h i)")[:, half * 384:(half + 1) * 384], True, True,
               r=[("stg", 0)], w=[pk])
            if mi < 4:
                CP(P, "dve", CPT[mi // 2][:, 6 * half:6 * half + 6, (mi % 2) * 64:(mi % 2) * 64 + 64],
                   ps[:, 0:384].rearrange("p (h i) -> p h i", h=6), r=[pk], w=[("CPT", mi // 2)])
            elif half == 0:
                CP(P, "dve", WMP[:, (mi - 4) * 64:(mi - 4) * 64 + 64], ps[:, 0:64], r=[pk], w=["WMP"])
    for xq in range(12):
        src = bass.AP(tensor=cx.d_GV.tensor, offset=240 - 16 * xq, ap=[[1, 64], [NSA_U, 12], [1, 1]])
        P.op("sync", lambda e, xq=xq, src=src: e.dma_start(out=Fn[:, :, xq:xq + 1], in_=src, allow_slow_non_contiguous=True),
             r=["dGV"], w=["Fn"], dma=True)
    MEMSET(P, "dve", PC, 0.0, w=["PC"])
    MEMSET(P, "dve", PCT, 0.0, w=["PCT"])
    MEMSET(P, "dve", SMpad, 0.0, w=["SM"])
    MEMSET(P, "dve", EC, 0.0, w=["EC"])

    hin_v = h_in.rearrange("(kc p) t -> p kc t", p=128)
    hout_v = h_out.rearrange("(kc p) t -> p kc t", p=128)
    for ch in range(NCH):
        t0 = ch * TC
        P.dma("sync", Hc, hin_v[:, :, t0:t0 + TC], w=[hkey])
        rms_chunk(P, cx, Hc, hkey, gn, xnT, "xnT", TC)
        for h in range(12):
            ps, pk = pp.get()
            for kc in range(8):
                MM(P, ps[0:64, 0:TC], WIN[:, kc, h * 64:(h + 1) * 64], xnT[:, kc, :], kc == 0, kc == 7, r=[("xnT", kc), "WIN"], w=[pk])
            ACT(P, QS[0:64, h, :], ps[0:64, 0:TC], AF.Identity, r=[pk], w=[("QS", h)], scale=0.125)
            CP(P, "dve", QS[64:128, h, :], QS[0:64, h, :], r=[("QS", h)], w=[("QSw", h)])
        for h in range(4):
            ps, pk = pp.get()
            for kc in range(8):
                MM(P, ps[0:64, 0:TC], WIN[:, kc, 804 + h * 64:804 + (h + 1) * 64], xnT[:, kc, :], kc == 0, kc == 7, r=[("xnT", kc), "WIN"], w=[pk])
            CP(P, "act", MQ[:, h, :], ps[0:64, 0:TC], r=[pk], w=[("MQ", h)])
        for cl in range(4):
            ps, pk = pp.get()
            for kc in range(8):
                MM(P, ps[0:64, 0:36], xnT[:, kc, cl * 64:(cl + 1) * 64], WIN[:, kc, 768:804], kc == 0, kc == 7, r=[("xnT", kc), "WIN"], w=[pk])
            ACT(P, GT[:, cl, :], ps[0:64, 0:36], AF.Sigmoid, r=[pk], w=[("GT", cl)])
        def phase1(c, cl):
            tsl = slice(cl * 64, (cl + 1) * 64)
            hf = c % 2
            pr = (c // 2) % 2
            ncol = min(255, 4 * c + 3)
            nkt = 1 if ncol <= 128 else 2
            sct = SCT4[:, cl, :, :]
            sctk = "SCT4"
            kn0, kn1 = max(0, 4 * c - 9), min(ncol, 4 * c + 3)
            x0 = kn0 - (4 * c - 9)
            qm = QMP[pr]
            for g in range(4):
                hs = slice(3 * g, 3 * g + 3)
                qmk = ("QM", pr, g, hf)
                pcs = psall[0:64, 6 * 512:6 * 512 + 768].rearrange("p (r k) -> p r k", r=3)
                ck = [("ps", 6), ("ps", 7)]
                for r in range(3):
                    h = 3 * g + r
                    MM(P, pcs[:, r, 0:ncol], QS[0:64, h, tsl], KcT[0:64, g, 0:ncol], True, True, r=[("QS", h), "KcT"], w=ck)
                yield
                TT(P, "dve", pcs[:, :, kn0:kn1], pcs[:, :, kn0:kn1], Fn[:, hs, x0:x0 + (kn1 - kn0)], ALU.add, r=ck + ["Fn"], w=ck)
                yield
                ACT(P, EC[:, :, 0:ncol], pcs[:, :, 0:ncol], AF.Exp, r=ck, w=["EC"])
                yield
                P.op("dve", lambda e, ncol=ncol: e.reduce_sum(out=SUMC[:, 0:3], in_=EC[:, :, 0:ncol], axis=AX.X), r=["EC"], w=["SUMC"])
                yield
                TS(P, "dve", SUMC[:, 0:3], SUMC[:, 0:3], 1e-30, None, ALU.max, None, r=["SUMC"], w=["SUMC"])
                yield
                RECIP(P, SUMC[:, 0:3], SUMC[:, 0:3], r=["SUMC"], w=["SUMC"])
                yield
                TT(P, "dve", PC[:, :, 0:ncol], EC[:, :, 0:ncol], SUMC[:, 0:3, None].to_broadcast([64, 3, ncol]), ALU.mult,
                   r=["EC", "SUMC"], w=["PC"])
                yield
                ps, pk = pp.get()
                for kt in range(nkt):
                    for r in range(3):
                        MM(P, ps[:, (kt * 3 + r) * 64:(kt * 3 + r + 1) * 64], PC[0:64, r, kt * 128:(kt + 1) * 128], cx.ident[0:64, 0:64],
                           True, True, r=["PC"], w=[pk])
                CP(P, "act", PCT[:, 0:nkt, :].rearrange("p a b -> p (a b)"), ps[:, 0:nkt * 192], r=[pk], w=["PCT"])
                yield
                poc = cx.ps[5]
                for r in range(3):
                    for kt in range(nkt):
                        MM(P, poc[0:64, r * 65:(r + 1) * 65], PCT[:, kt, r * 64:(r + 1) * 64], VC[:, kt, g, :], kt == 0, kt == nkt - 1,
                           r=["PCT", "VC"], w=[("ps", 5)])
                n = 0
                for r in range(3):
                    for kt in range(nkt):
                        MM(P, poc[0:64, 256:320], PCT[:, kt, r * 64:(r + 1) * 64], OV[:, kt, :], n == 0, n == 3 * nkt - 1,
                           r=["PCT", "OV"], w=[("ps", 5)])
                        n += 1
                yield
                gv = GT[:, cl, 9 * g:9 * g + 9].rearrange("p (r x) -> p r x", r=3)[:, :, 0:1]
                ov = poc[0:64, 0:195].rearrange("p (r d) -> p r d", r=3)
                TT(P, "dve", CMPC[pr][:, hf, hs, :], ov[:, :, 0:64], gv.to_broadcast([64, 3, 64]), ALU.mult,
                   r=[("ps", 5), ("GT", cl)], w=[("CMPC", pr, g, hf)])
                yield
                TT(P, "dve", SC[:, g, :], poc[0:64, 256:320], sct[:, 0, :], ALU.mult, r=[("ps", 5), sctk], w=[("SC", g)])
                yield
                TT(P, "dve", SC[:, g, :], SC[:, g, :], sct[:, 1, :], ALU.add, r=[("SC", g), sctk], w=[("SC", g)])
                yield
                P.op("dve", lambda e, g=g: e.max(out=M8[:, 0:8], in_=SC[:, g, :]), r=[("SC", g)], w=["M8"])
                yield
                P.op("dve", lambda e, g=g: e.match_replace(out=SC2, in_to_replace=M8[:, 0:8], in_values=SC[:, g, :], imm_value=-1e9),
                     r=["M8", ("SC", g)], w=["SC2"])
                yield
                P.op("dve", lambda e: e.max(out=M8[:, 8:16], in_=SC2), r=["SC2"], w=["M8"])
                yield
                TS(P, "dve", SMpad[:, g, 64:128], SC[:, g, :], M8[:, 15:16], None, ALU.is_ge, None, r=[("SC", g), "M8"], w=["SM"])
                yield
                ps, pk = pp.get()
                MM(P, ps[0:64, 0:64], SMpad[0:64, g, 64:128], cx.ident[0:64, 0:64], True, True, r=["SM"], w=[pk])
                TS(P, "dve", qm[0:64, hs, hf * 64:(hf + 1) * 64], ps[0:64, None, 0:64].to_broadcast([64, 3, 64]), 30000.0, -30000.0,
                   ALU.mult, ALU.add, r=[pk], w=[qmk])
                yield

        def phase2(c0, cl0):
            tsl2 = slice(cl0 * 64, cl0 * 64 + 128)
            kt_c = c0 // 2
            pr = (c0 // 2) % 2
            qm = QMP[pr]
            tok = cx.ps[3]
            tkey = ("ps", 3)
            for g in range(4):
                hs = slice(3 * g, 3 * g + 3)
                for br in range(2):
                    if br == 0:
                        kts = list(range(0, kt_c + 1))
                        qkey = [("QS", 3 * g + r) for r in range(3)] + [("QM", pr, g, 0), ("QM", pr, g, 1)]
                        VA = VS
                        vkey = "VS"
                        pob = cx.ps[4]
                        okey = ("ps", 4)
                    else:
                        kts = list(range(max(0, (c0 - 8) // 2), kt_c + 1))
                        qkey = [("QSw", 3 * g + r) for r in range(3)]
                        VA = VW
                        vkey = "VW"
                        pob = cx.ps[0]
                        okey = ("ps", 0)
                    for ki, kt in enumerate(kts):
                        ps, pk = pp.get()
                        if br == 0:
                            MM(P, ps[:, 0:384], KT[0:64, g, kt * 128:(kt + 1) * 128], QS[0:64, hs, tsl2], True, False,
                               r=["KT"] + qkey, w=[pk])
                            MM(P, ps[:, 0:384], IND[0:64, kt * 128:(kt + 1) * 128], qm[0:64, hs, :], False, True,
                               r=["IND"] + qkey, w=[pk])
                        else:
                            MM(P, ps[:, 0:384], KT[64:128, g, kt * 128:(kt + 1) * 128], QS[64:128, hs, tsl2], True, True,
                               r=["KT"] + qkey, w=[pk])
                        pt = PTs[(ki + br) % 3]
                        ptk = ("PTs", (ki + br) % 3)
                        corr = None
                        if kt == kt_c:
                            corr, corrk = CPT[0][:, hs, :], ("CPT", 0)
                        elif kt == kt_c - 1:
                            corr, corrk = CPT[1][:, hs, :], ("CPT", 1)
                        elif br == 1 and c0 >= 8 and ki == 0:
                            corr, corrk = WMP[:, None, :].to_broadcast([128, 3, 128]), "WMP"
                        if corr is not None:
                            sb = SB[ki % 2]
                            TT(P, "dve", sb.rearrange("p (r q) -> p r q", r=3), ps[:, 0:384].rearrange("p (r q) -> p r q", r=3),
                               corr, ALU.add, r=[pk, corrk], w=[("SB", ki % 2)])
                            ACT(P, pt, sb, AF.Exp, r=[("SB", ki % 2)], w=[ptk])
                        else:
                            ACT(P, pt, ps[:, 0:384], AF.Exp, r=[pk], w=[ptk])
                        MM(P, pob[0:65, 0:384], VA[:, kt, g, :], pt, ki == 0, ki == len(kts) - 1, r=[ptk, vkey], w=[okey])
                        yield
                    CP(P, "act" if br == 0 else "dve", OT[br], pob[0:65, 0:384], r=[okey], w=[("OT", br)])
                    for hf in range(2):
                        for r in range(3):
                            MM(P, tok[0:64, (hf * 3 + r) * 65:(hf * 3 + r + 1) * 65], OT[br][0:65, r * 128 + hf * 64:r * 128 + hf * 64 + 64],
                               cx.identf[0:65, 0:65], True, True, r=[("OT", br)], w=[tkey])
                    yield
                    x = br + 1
                    ov = tok[0:64, 0:390].rearrange("p (h r d) -> p h r d", h=2, r=3)
                    wx = WX[:, :, :, x:x + 1]
                    TS(P, "dve", wx, ov[:, :, :, 64:65], 1e-30, None, ALU.max, None, r=[tkey], w=[("WX", x)])
                    RECIP(P, wx, wx, r=[("WX", x)], w=[("WX", x)])
                    gv = GT[:, cl0:cl0 + 2, 9 * g:9 * g + 9].rearrange("p h (r x) -> p h r x", r=3)[:, :, :, x:x + 1]
                    TT(P, "dve", wx, wx, gv, ALU.mult, r=[("WX", x), ("GT", cl0), ("GT", cl0 + 1)], w=[("WX", x)])
                    TT(P, "dve", TMP[br], ov[:, :, :, 0:64], wx.to_broadcast([64, 2, 3, 64]), ALU.mult,
                       r=[tkey, ("WX", x)], w=[("TMP", br)])
                    yield
                mq = MAINQ[:, :, 192 * g:192 * (g + 1)].rearrange("p h (r d) -> p h r d", r=3)
                TT(P, "dve", TMP[0], TMP[0], CMPC[pr][:, :, hs, :], ALU.add,
                   r=[("TMP", 0), ("CMPC", pr, g, 0), ("CMPC", pr, g, 1)], w=[("TMP", 0)])
                TT(P, "dve", mq, TMP[0], TMP[1], ALU.add, r=[("TMP", 0), ("TMP", 1)], w=["MAINQ"])
                yield
            for hf in range(2):
                ps, pk = pp.get()
                for c6 in range(6):
                    MM(P, ps[:, c6 * 64:(c6 + 1) * 64], MAINQ[0:64, hf, c6 * 128:(c6 + 1) * 128], cx.ident[0:64, 0:64], True, True, r=["MAINQ"], w=[pk])
                CP(P, "act", MIXT[:, 0:6, (cl0 + hf) * 64:(cl0 + hf + 1) * 64], ps[:, 0:384].rearrange("p (c q) -> p c q", c=6), r=[pk],
                   w=[("MIXT", i) for i in range(6)])
            yield

        def chain(*gens):
            for gq in gens:
                for _ in gq:
                    yield

        P.dma("sync", SCT4, cx.d["sct4b"][ch], w=["SCT4"])
        for _ in chain(phase1(ch * 4, 0), phase1(ch * 4 + 1, 1)):
            pass
        for pi in range(2):
            c0 = ch * 4 + 2 * pi
            g2 = phase2(c0, 2 * pi)
            g1 = chain(phase1(c0 + 2, 2), phase1(c0 + 3, 3)) if pi == 0 else iter(())
            n2 = 4 * ((c0 // 2 + 1) + (c0 // 2 + 1 - max(0, (c0 - 8) // 2)) + 5) + 1
            per = max(1, -(-152 // n2))
            for _ in g2:
                for _k in range(per):
                    next(g1, None)
            for _ in g1:
                pass
        mem_attn_chunk(P, cx, pp, MQ, KmT, Vm, PT, Rb, MIXT, TC)
        out_proj_chunk(P, cx, pp, WO, MIXT, Hc, hkey, TC)
        P.dma("sync", hout_v[:, :, t0:t0 + TC], Hc, r=[hkey])
    P.barrier()


def final_pass(P, cx, h_in, out):
    TC = 256
    A = cx.arena
    A.reset()
    Hb = [A.f32([128, 8, TC]) for _ in range(2)]
    Ob = [A.f32([128, 8, TC]) for _ in range(2)]
    cx.sq = [A.bf16([128, TC]) for _ in range(2)]
    cx.rstd = A.f32([128, TC])
    gn = A.f32([128, 8])
    P.dma("sync", gn, cx.d["final_normT"], w=["gn"])
    hin_v = h_in.rearrange("(kc p) t -> p kc t", p=128)
    P.dma("sync", Hb[0], hin_v[:, :, 0:TC], w=[("H", 0)])
    for ch in range(S // TC):
        t0 = ch * TC
        b = ch % 2
        if ch + 1 < S // TC:
            P.dma("sync", Hb[1 - b], hin_v[:, :, t0 + TC:t0 + 2 * TC], w=[("H", 1 - b)])
        rms_chunk(P, cx, Hb[b], ("H", b), gn, Ob[b], ("O", b), TC)
        o = P.dma("sync", out.rearrange("(kc p) t -> p kc t", p=128)[:, :, t0:t0 + TC], Ob[b],
                  r=[(("O", b), kc) for kc in range(8)])
        P.out_dmas.append(o)
    P.barrier()


IN_SPECS = [
    ("xT", [D, S]), ("memT", [D, MEM]),
    ("norm_mixT", [DEPTH, 128, 8]), ("norm_memT", [DEPTH, 128, 8]), ("norm_ffnT", [DEPTH, 128, 8]),
    ("final_normT", [128, 8]),
    ("w_up", [DEPTH, D, 2 * FFN]), ("w_down", [DEPTH, FFN, D]),
    ("conv_wT", [DEPTH, 128, 44, 3]), ("conv_bT", [DEPTH, 128, 44]),
    ("ones", [128, 128]),
    ("w_mem_kv", [DEPTH, D, 512]), ("w_out", [DEPTH, D, D]),
    ("gla_w_in", [2, D, 2576]), ("gla_w_gate_up", [2, 16, 384]), ("gla_b_gate", [2, 384]), ("gla_out_norm_rep", [2, 128, 192]),
    ("cmat", [6, 128, 128]),
    ("nsa_w_in", [2, D, 1060]), ("kv_normT", [128, 8]), ("w_kv_shared", [D, 1536]),
    ("cmp_posT", [2, 64, 32]), ("cmp_w1", [2, 2048, 128]), ("cmp_b1T", [128, 2]), ("cmp_w2", [2, 128, 64]),
    ("cmp_b2", [2, 64]), ("cmp_b2T", [64, 2]), ("rel_bias", [32, 12]),
    ("ind", [128, S]), ("ovl", [128, 128]), ("sct4b", [16, 64, 4, 2, 64]), ("oh_u", [33, NSA_U]),
]


def build(stages=("ffn0", "final")):
    nc = bass.Bass("TRN2", target_bir_lowering=False)
    cx = Ctx()
    cx.d = {}
    for name, shape in IN_SPECS:
        cx.d[name] = nc.dram_tensor(name, shape, F32, kind="ExternalInput").ap()
    outT = nc.dram_tensor("outT", [D, S], F32, kind="ExternalOutput").ap()
    hA = nc.dram_tensor("hA", [D, S], F32, kind="Internal").ap()
    cx.d_KT = nc.dram_tensor("d_KT", [128, 4 * S], BF16, kind="Internal").ap()
    cx.d_VS = nc.dram_tensor("d_VS", [128, 32 * 4 * 65], BF16, kind="Internal").ap()
    cx.d_VW = nc.dram_tensor("d_VW", [128, 32 * 4 * 65], BF16, kind="Internal").ap()
    cx.d_KC = nc.dram_tensor("d_KC", [64, 4 * 256], BF16, kind="Internal").ap()
    cx.d_VC = nc.dram_tensor("d_VC", [128, 2 * 4 * 65], BF16, kind="Internal").ap()
    cx.d_GV = nc.dram_tensor("d_GV", [12, NSA_U], F32, kind="Internal").ap()
    P = Prog(nc)
    with ExitStack() as st:
        NW = 51200
        arena_t = st.enter_context(nc.sbuf_tensor("arena", [128, NW], F32))
        ones_bf = st.enter_context(nc.sbuf_tensor("ones_bf", [128, 128], BF16))
        ones_f = st.enter_context(nc.sbuf_tensor("ones_f", [128, 128], F32))
        cx.arena = Arena(arena_t, NW)
        cx.ones_bf = ones_bf
        psall = st.enter_context(nc.psum_tensor("psall", [128, 4096], F32))
        cx.psall = psall
        cx.ps = [psall[:, i * 512:(i + 1) * 512] for i in range(8)]
        sems = {e: st.enter_context(nc.semaphore("s_" + e)) for e in Prog.ENGS}
        dsems = [st.enter_context(nc.semaphore("d%d" % i)) for i in range(NDSEM)]
        block = st.enter_context(nc.Block())

        P.dma("sync", ones_f[:, :], cx.d["ones"], w=["ones_f"])
        P.op("dve", lambda e: e.tensor_copy(out=ones_bf[:, :], in_=ones_f[:, :]), r=["ones_f"], w=["ones"])
        cmf = st.enter_context(nc.sbuf_tensor("cmf", [128, 6, 128], F32))
        cmb = st.enter_context(nc.sbuf_tensor("cmb", [128, 6, 128], BF16))
        one_col = st.enter_context(nc.sbuf_tensor("one_col", [128, 2], F32))
        P.dma("sync", cmf[:, :, :], cx.d["cmat"].rearrange("c p n -> p c n"), w=["cmf"])
        P.op("dve", lambda e: e.tensor_copy(out=cmb[:, :, :], in_=cmf[:, :, :]), r=["cmf"], w=["cmb"])
        P.op("dve", lambda e: e.memset(one_col[:, :], 1.0), w=["one_col"])
        cx.ident = cmb[:, 0, :]
        cx.identf = cmf[:, 0, :]
        cx.triu = cmf[:, 1, :]
        cx.strictl = cmf[:, 2, :]
        cx.maskji = cmf[:, 3, :]
        cx.ones1 = cmb[:, 4, :]
        cx.one_col = one_col
        cx.antiid = cmf[:, 5, :]
        P.barrier()

        h = cx.d["xT"]
        for stg in stages:
            if stg.startswith("ffn"):
                ffn_pass(P, cx, int(stg[3:]), h, hA)
                h = hA
            elif stg.startswith("gla"):
                gla_pass(P, cx, int(stg[3:]), h, hA)
                h = hA
            elif stg == "kv":
                kv_pass(P, cx, h)
            elif stg.startswith("nsa"):
                nsa_pass(P, cx, int(stg[3:]), h, hA)
                h = hA
            elif stg == "final":
                final_pass(P, cx, h, outT)
        P.barrier()
        P.emit(block, sems, dsems)
    cx.P = P
    return nc, cx


def prep_common(inp):
    f = np.float32

    def featT(v):
        v = np.asarray(v, f)
        return np.ascontiguousarray(v.reshape(v.shape[:-1] + (8, 128)).swapaxes(-1, -2))

    m = {}
    m["norm_mixT"] = featT(inp["norm_mix"])
    m["norm_memT"] = featT(inp["norm_mem"])
    m["norm_ffnT"] = featT(inp["norm_ffn"])
    m["final_normT"] = featT(inp["final_norm"])
    m["w_up"] = np.ascontiguousarray(np.asarray(inp["w_up"], f))
    m["w_down"] = np.ascontiguousarray(np.asarray(inp["w_down"], f))
    cwv = np.asarray(inp["conv_w"], f)
    m["conv_wT"] = np.ascontiguousarray(cwv.reshape(DEPTH, 3, 44, 128).transpose(0, 3, 2, 1))
    cbv = np.asarray(inp["conv_b"], f)
    m["conv_bT"] = np.ascontiguousarray(cbv.reshape(DEPTH, 44, 128).transpose(0, 2, 1))
    m["ones"] = np.full((128, 128), 1.0 / D, f)
    for k in ("w_mem_kv", "w_out", "gla_w_in", "gla_w_gate_up", "gla_b_gate"):
        m[k] = np.ascontiguousarray(np.asarray(inp[k], f))
    m["gla_out_norm_rep"] = np.ascontiguousarray(np.broadcast_to(np.asarray(inp["gla_out_norm"], f)[:, None, :], (2, 128, 192)))
    jj, ii = np.meshgrid(np.arange(128), np.arange(128), indexing="ij")
    cm = np.zeros((6, 128, 128), f)
    cm[0] = np.eye(128)
    cm[1] = (jj <= ii) * (-1.0 / 16.0)
    cm[2] = (jj > ii) * (-1.0 / 16.0)
    cm[3] = (jj <= ii) * 1.0
    cm[4] = 1.0
    cm[5] = np.eye(128)[::-1]
    m["cmat"] = cm
    for k in ("nsa_w_in", "w_kv_shared", "cmp_w1", "cmp_w2", "cmp_b2", "rel_bias"):
        m[k] = np.ascontiguousarray(np.asarray(inp[k], f))
    m["kv_normT"] = featT(inp["kv_norm"])
    m["cmp_posT"] = np.ascontiguousarray(np.asarray(inp["cmp_pos"], f).transpose(0, 2, 1))
    m["cmp_b1T"] = np.ascontiguousarray(np.asarray(inp["cmp_b1"], f).T)
    m["cmp_b2T"] = np.ascontiguousarray(np.asarray(inp["cmp_b2"], f).T)
    pidx = np.arange(128)[:, None] % 64
    m["ind"] = (np.arange(S)[None, :] // 64 == pidx).astype(f)
    kk = (np.arange(2)[None, :, None] * 128 + np.arange(128)[:, None, None])
    jb = np.arange(64)[None, None, :]
    ov = ((16 * kk < 64 * jb + 64) & (16 * kk + 31 >= 64 * jb) & (kk < 255)).astype(f)
    m["ovl"] = np.ascontiguousarray(ov.reshape(128, 128))
    cb = np.arange(64)[:, None]
    jj2 = np.arange(64)[None, :]
    forced = (jj2 == 0) | (jj2 == cb) | (jj2 == cb - 1)
    valid = (jj2 <= cb) & ~forced
    add = np.where(forced, 1e4, np.where(jj2 <= cb, 0.0, -1.0))
    sct = np.stack([valid.astype(f), add.astype(f)], axis=1).reshape(16, 1, 4, 2, 64)
    m["sct4b"] = np.ascontiguousarray(np.broadcast_to(sct, (16, 64, 4, 2, 64)))
    dd = np.arange(NSA_U) - 127
    dcl = np.maximum(dd, 0)
    large = 16 + (np.log(np.maximum(dcl, 1).astype(f) / f(16)) / f(np.log(128 / 16)) * f(16)).astype(np.int32)
    bucket = np.where(dcl < 16, dcl, np.minimum(large, 31))
    near = (dd >= 0) & (dd < 113)
    oh = np.zeros((33, NSA_U), f)
    for b in range(31):
        oh[b] = (near & (bucket == b))
    oh[31] = -1.0 * near
    oh[32] = ((dd < 0) | (dd >= 512))
    assert not (near & (bucket == 31)).any()
    m["oh_u"] = oh
    return m


def prep_inputs(inp, b, common):
    f = np.float32
    m = dict(common)
    m["xT"] = np.ascontiguousarray(np.asarray(inp["x"][b], f).T)
    m["memT"] = np.ascontiguousarray(np.asarray(inp["mem"][b], f).T)
    return m


def run(inp, stages, ncores=8):
    nc, cx = build(stages)
    common = prep_common(inp)
    per_b = [prep_inputs(inp, b, common) for b in range(4)]
    in_maps = [per_b[c // 2] for c in range(ncores)]
    res = run_bass_kernel_spmd(nc, in_maps, core_ids=list(range(ncores)))
    out = np.stack([np.ascontiguousarray(res.results[2 * b]["outT"].T) for b in range((ncores + 1) // 2)], axis=0)
    return out.astype(np.float32)


FULL = ("gla0", "ffn0", "gla1", "ffn1", "kv", "nsa2", "ffn2", "nsa3", "ffn3", "final")


def kernel(**inputs):
    return run(inputs, FULL)
```

```python
import numpy as np
import concourse.bass as bass
import concourse.mybir as mybir
from concourse.bass_utils import run_bass_kernel_spmd
from contextlib import ExitStack

F32 = mybir.dt.float32
BF16 = mybir.dt.bfloat16
AF = mybir.ActivationFunctionType
ALU = mybir.AluOpType
AX = mybir.AxisListType

D = 1024
S = 4096
DEPTH = 4
FFN = 2816
MEM = 256
EPS = 1e-6
NDSEM = 24
SAME_ENGINE_SYNC = True


class Op:
    __slots__ = ("eng", "fn", "dma", "idx", "waits", "signal", "sigval", "dsem", "dval")


class Prog:
    ENGS = ("pe", "act", "dve", "pool", "sync")

    def __init__(self, nc):
        self.nc = nc
        self.ops = {e: [] for e in self.ENGS}
        self.last_w = {}
        self.readers = {}
        self.waited_c = {e: {x: -1 for x in self.ENGS} for e in self.ENGS}
        self.waited_d = {e: {} for e in self.ENGS}
        self.ndma = 0
        self.dma_since_barrier = {}
        self.out_dmas = []

    def _add_dep(self, o, d):
        if d is None or d is o:
            return
        e = o.eng
        if d.dma:
            if self.waited_d[e].get(d.dsem, 0) >= d.dval:
                return
            self.waited_d[e][d.dsem] = d.dval
            o.waits.append(("d", d.dsem, d.dval))
            return
        if d.eng == e:
            if e == "pe" or not SAME_ENGINE_SYNC:
                return
        if self.waited_c[e][d.eng] >= d.idx:
            return
        self.waited_c[e][d.eng] = d.idx
        d.signal = True
        o.waits.append(("c", d.eng, d))

    def op(self, eng, fn, r=(), w=(), dma=False):
        o = Op()
        o.eng = eng
        o.fn = fn
        o.dma = dma
        o.idx = len(self.ops[eng])
        o.waits = []
        o.signal = False
        o.sigval = 0
        o.dsem = o.dval = None
        if dma:
            n = self.ndma
            self.ndma += 1
            o.dsem = n % NDSEM
            o.dval = 16 * (n // NDSEM + 1)
            if n >= NDSEM:
                prev = 16 * (n // NDSEM)
                if self.waited_d[eng].get(o.dsem, 0) < prev:
                    self.waited_d[eng][o.dsem] = prev
                    o.waits.append(("d", o.dsem, prev))
            self.dma_since_barrier[o.dsem] = o
        for b in r:
            self._add_dep(o, self.last_w.get(b))
            if isinstance(b, tuple) and b[0] == "ps":
                for t in self.readers.get(b, ()):
                    if t.eng != eng:
                        self._add_dep(o, t)
        for b in w:
            self._add_dep(o, self.last_w.get(b))
            for t in self.readers.get(b, ()):
                self._add_dep(o, t)
        for b in r:
            self.readers.setdefault(b, []).append(o)
        for b in w:
            self.last_w[b] = o
            self.readers[b] = []
        self.ops[eng].append(o)
        return o

    def dma(self, eng, out, in_, r=(), w=()):
        return self.op(eng, lambda e: e.dma_start(out=out, in_=in_), r=r, w=w, dma=True)

    def barrier(self):
        lasts = {}
        for e in self.ENGS:
            real = [o for o in self.ops[e][-64:] if o.fn is not None]
            if not real:
                real = [o for o in self.ops[e] if o.fn is not None]
            lasts[e] = real[-1] if real else None
        dmas = list(self.dma_since_barrier.values())
        for e in self.ENGS:
            o = self.op(e, None)
            for x in self.ENGS:
                d = lasts[x]
                if d is not None and x != e and not d.dma:
                    self._add_dep(o, d)
                elif d is not None and d.dma:
                    self._add_dep(o, d)
            for d in dmas:
                self._add_dep(o, d)
        self.dma_since_barrier = {}
        self.last_w = {}
        self.readers = {}

    def emit(self, block, sems, dsems):
        nc = self.nc
        for e in self.ENGS:
            c = 0
            for o in self.ops[e]:
                if o.signal:
                    c += 1
                    o.sigval = c
        self.sig_counts = {e: sum(1 for o in self.ops[e] if o.signal) for e in self.ENGS}

        def run(ename, eng):
            for o in self.ops[ename]:
                for wt in o.waits:
                    if wt[0] == "d":
                        eng.wait_ge(dsems[wt[1]], wt[2])
                    else:
                        eng.wait_ge(sems[wt[1]], wt[2].sigval)
                if o.fn is None:
                    continue
                ins = o.fn(eng)
                if o.dma:
                    ins.then_inc(dsems[o.dsem], 16)
                elif o.signal:
                    ins.then_inc(sems[ename], 1)

        @block.tensor
        def _(eng):
            run("pe", eng)

        @block.scalar
        def _(eng):
            run("act", eng)

        @block.vector
        def _(eng):
            run("dve", eng)

        @block.gpsimd
        def _(eng):
            run("pool", eng)

        @block.sync
        def _(eng):
            run("sync", eng)


class Arena:
    def __init__(self, tens, nwords):
        self.t = tens
        self.n = nwords
        self.off = 0

    def reset(self):
        self.off = 0

    def f32(self, shape):
        n = int(np.prod(shape[1:]))
        assert self.off + n <= self.n, ("arena overflow", self.off, n, self.n)
        ap = self.t[0:shape[0], self.off:self.off + n]
        self.off += n
        return _shape(ap, shape)

    def bf16(self, shape):
        n = int(np.prod(shape[1:]))
        nw = (n + 1) // 2
        assert self.off + nw <= self.n, ("arena overflow", self.off, nw, self.n)
        ap = self.t[0:shape[0], self.off:self.off + nw].bitcast(BF16)
        if 2 * nw != n:
            ap = ap[:, 0:n]
        self.off += nw
        return _shape(ap, shape)


def _shape(ap, shape):
    if len(shape) == 2:
        return ap
    if len(shape) == 3:
        return ap.rearrange("p (a b) -> p a b", a=shape[1], b=shape[2])
    if len(shape) == 4:
        return ap.rearrange("p (a b c) -> p a b c", a=shape[1], b=shape[2], c=shape[3])
    raise ValueError(shape)


class Ctx:
    pass


def load_cast(P, cx, dst, src, key, shape, cast_i):
    sb = cast_i % 2
    n = int(np.prod(shape[1:]))
    stg = _shape(cx.stg[sb][0:shape[0], 0:n], shape)
    P.dma("sync", stg, src, w=[("stg", sb)])
    eng = ("dve", "pool", "act")[cast_i % 3]
    if eng == "act":
        P.op("act", lambda e: e.copy(out=dst, in_=stg), r=[("stg", sb)], w=[key])
    else:
        P.op(eng, lambda e: e.tensor_copy(out=dst, in_=stg), r=[("stg", sb)], w=[key])


def rms_chunk(P, cx, Hc, hkey, gcol, xnT, xkey, TC, pool_share=False):
    ps = cx.ps[0][:, 0:TC]
    for kc in range(8):
        sq = cx.sq[kc % 2][:, 0:TC]
        P.op("act", lambda e, sq=sq, kc=kc: e.activation(out=sq, in_=Hc[:, kc, :], func=AF.Square),
             r=[hkey], w=[("sq", kc % 2)])
        P.op("pe", lambda e, sq=sq, kc=kc: e.matmul(ps, lhsT=cx.ones_bf[:, :], rhs=sq, start=(kc == 0), stop=(kc == 7)),
             r=[("sq", kc % 2)], w=[("ps", 0)])
    rstd = cx.rstd[:, 0:TC]
    P.op("dve", lambda e: e.tensor_scalar(out=rstd, in0=ps, scalar1=EPS, scalar2=None, op0=ALU.add),
         r=[("ps", 0)], w=["rstd"])
    P.op("act", lambda e: e.activation(out=rstd, in_=rstd, func=AF.Sqrt), r=["rstd"], w=["rstd"])
    P.op("dve", lambda e: e.reciprocal(out=rstd, in_=rstd), r=["rstd"], w=["rstd"])
    for kc in range(8):
        eng = "pool" if (pool_share and kc % 2 == 1) else "dve"
        P.op(eng, lambda e, kc=kc: e.scalar_tensor_tensor(out=xnT[:, kc, :], in0=Hc[:, kc, :], scalar=gcol[:, kc:kc + 1],
                                                          in1=rstd, op0=ALU.mult, op1=ALU.mult),
             r=[hkey, "rstd", "gn"], w=[(xkey, kc)])


def MM(P, out, lhsT, rhs, start, stop, r, w, **kw):
    return P.op("pe", lambda e: e.matmul(out, lhsT=lhsT, rhs=rhs, start=start, stop=stop, **kw), r=r, w=w)


def ACT(P, out, in_, func, r, w, **kw):
    return P.op("act", lambda e: e.activation(out=out, in_=in_, func=func, **kw), r=r, w=w)


def TT(P, eng, out, in0, in1, op, r, w):
    return P.op(eng, lambda e: e.tensor_tensor(out=out, in0=in0, in1=in1, op=op), r=r, w=w)


def STT(P, out, in0, scalar, in1, op0, op1, r, w):
    return P.op("dve", lambda e: e.scalar_tensor_tensor(out=out, in0=in0, scalar=scalar, in1=in1, op0=op0, op1=op1), r=r, w=w)


def TS(P, eng, out, in0, s1, s2, op0, op1, r, w):
    if s2 is None:
        return P.op(eng, lambda e: e.tensor_scalar(out=out, in0=in0, scalar1=s1, scalar2=None, op0=op0), r=r, w=w)
    return P.op(eng, lambda e: e.tensor_scalar(out=out, in0=in0, scalar1=s1, scalar2=s2, op0=op0, op1=op1), r=r, w=w)


def CP(P, eng, out, in_, r, w):
    if eng == "act":
        return P.op("act", lambda e: e.copy(out=out, in_=in_), r=r, w=w)
    return P.op(eng, lambda e: e.tensor_copy(out=out, in_=in_), r=r, w=w)


def MEMSET(P, eng, out, val, w):
    return P.op(eng, lambda e: e.memset(out, val), w=w)


def RECIP(P, out, in_, r, w):
    return P.op("dve", lambda e: e.reciprocal(out=out, in_=in_), r=r, w=w)


def ffn_pass(P, cx, l, h_in, h_out):
    TC = 512
    NCH = S // TC
    A = cx.arena
    A.reset()
    WU = A.bf16([128, 8, 2 * FFN])
    WD = A.bf16([128, 22, D])
    Hst = A.f32([128, 4096])
    cx.stg = [Hst[:, 0:2048], Hst[:, 2048:4096]]
    Hc = _shape(Hst, [128, 8, TC])
    hkey = "H"
    xnT = A.bf16([128, 8, TC])
    cx.sq = [A.bf16([128, TC]) for _ in range(2)]
    cx.rstd = A.f32([128, TC])
    U = [A.f32([128, TC + 2]) for _ in range(2)]
    T1 = [A.f32([128, TC]) for _ in range(3)]
    SA = [A.bf16([128, TC]) for _ in range(2)]
    Rb = [A.f32([128, TC]) for _ in range(2)]
    actT = A.bf16([128, 22, TC])
    HALO = A.f32([128, 44, 2])
    cw = A.f32([128, 44, 3])
    cb = A.f32([128, 44])
    gn = A.f32([128, 8])

    P.dma("sync", cw, cx.d["conv_wT"][l], w=["cw"])
    P.dma("sync", cb, cx.d["conv_bT"][l], w=["cb"])
    P.dma("sync", gn, cx.d["norm_ffnT"][l], w=["gn"])
    wu_src = cx.d["w_up"][l].rearrange("(kc p) c -> p kc c", p=128)
    ci = 0
    for c0 in range(0, 2 * FFN, 256):
        load_cast(P, cx, WU[:, :, c0:c0 + 256], wu_src[:, :, c0:c0 + 256], "WU", [128, 8, 256], ci)
        ci += 1
    wd_src = cx.d["w_down"][l].rearrange("(fc p) n -> p fc n", p=128)
    for f0 in range(0, 22, 2):
        load_cast(P, cx, WD[:, f0:f0 + 2, :], wd_src[:, f0:f0 + 2, :], "WD", [128, 2, D], ci)
        ci += 1
    P.op("pool", lambda e: e.memset(HALO, 0.0), w=["halo"])

    hin_v = h_in.rearrange("(kc p) t -> p kc t", p=128)
    hout_v = h_out.rearrange("(kc p) t -> p kc t", p=128)
    P.dma("sync", Hc, hin_v[:, :, 0:TC], w=[hkey, ("stg", 0), ("stg", 1)])
    nu = 0
    for ch in range(NCH):
        t0 = ch * TC
        rms_chunk(P, cx, Hc, hkey, gn, xnT, "xnT", TC)
        if ch + 1 < NCH:
            P.dma("sync", Hc, hin_v[:, :, t0 + TC:t0 + 2 * TC], w=[hkey])
        P.dma("sync", Rb[0], hin_v[:, 0, t0:t0 + TC], w=[("R", 0)])
        for fc in range(22):
            tt = []
            for half in range(2):
                cc = fc + 22 * half
                pi = nu % 4
                ui = nu % 2
                ti = nu % 3
                nu += 1
                pst = cx.ps[1 + pi][:, 0:TC]
                pk = ("ps", 1 + pi)
                for kc in range(8):
                    MM(P, pst, WU[:, kc, cc * 128:(cc + 1) * 128], xnT[:, kc, :], kc == 0, kc == 7, r=[("xnT", kc), "WU"], w=[pk])
                Ut = U[ui]
                uk = ("U", ui)
                t1 = T1[ti]
                tk = ("T1", ti)
                CP(P, "pool", Ut[:, 0:2], HALO[:, cc, :], r=["halo"], w=[uk])
                CP(P, "act", Ut[:, 2:TC + 2], pst, r=[pk], w=[uk])
                ACT(P, t1, pst, AF.Identity, r=[pk, "cw", "cb"], w=[tk], bias=cb[:, cc:cc + 1], scale=cw[:, cc, 2:3])
                STT(P, t1, Ut[:, 1:TC + 1], cw[:, cc, 1:2], t1, ALU.mult, ALU.add, r=[uk, tk, "cw"], w=[tk])
                STT(P, t1, Ut[:, 0:TC], cw[:, cc, 0:1], t1, ALU.mult, ALU.add, r=[uk, tk, "cw"], w=[tk])
                CP(P, "pool", HALO[:, cc, :], Ut[:, TC:TC + 2], r=[uk], w=["halo"])
                tt.append((t1, tk))
            sa = SA[fc % 2]
            sk = ("SA", fc % 2)
            ACT(P, sa, tt[0][0], AF.Silu, r=[tt[0][1]], w=[sk])
            TT(P, "pool", actT[:, fc, :], sa, tt[1][0], ALU.mult, r=[sk, tt[1][1]], w=[("actT", fc)])
        for n in range(8):
            pd = cx.ps[5 + n % 2][:, 0:TC]
            pk = ("ps", 5 + n % 2)
            for fc in range(22):
                MM(P, pd, WD[:, fc, n * 128:(n + 1) * 128], actT[:, fc, :], fc == 0, fc == 21, r=[("actT", fc), "WD"], w=[pk])
            if n + 1 < 8:
                P.dma("sync", Rb[(n + 1) % 2], hin_v[:, n + 1, t0:t0 + TC], w=[("R", (n + 1) % 2)])
            rb = Rb[n % 2]
            TT(P, "dve", rb, rb, pd, ALU.add, r=[pk, ("R", n % 2)], w=[("R", n % 2)])
            P.dma("sync", hout_v[:, n, t0:t0 + TC], rb, r=[("R", n % 2)])
    P.barrier()


class PsPool:
    def __init__(self, cx, banks):
        self.cx = cx
        self.banks = banks
        self.i = 0

    def get(self):
        b = self.banks[self.i % len(self.banks)]
        self.i += 1
        return self.cx.ps[b], ("ps", b)


def mem_setup(P, cx, l, WMK, xmT, KmT, Vm, gm, Hstage):
    pp = PsPool(cx, [1, 2, 3])
    P.dma("sync", gm, cx.d["norm_memT"][l], w=["gn"])
    src = cx.d["w_mem_kv"][l].rearrange("(kc p) c -> p kc c", p=128)
    for i, c0 in enumerate(range(0, 512, 256)):
        load_cast(P, cx, WMK[:, :, c0:c0 + 256], src[:, :, c0:c0 + 256], "WMK", [128, 8, 256], i)
    Hm = _shape(Hstage[:, 0:8 * MEM], [128, 8, MEM])
    P.dma("sync", Hm, cx.d["memT"].rearrange("(kc p) t -> p kc t", p=128), w=[("stg", 0)])
    rms_chunk(P, cx, Hm, ("stg", 0), gm, xmT, "xmT", MEM)
    for h in range(4):
        ps, pk = pp.get()
        for kc in range(8):
            MM(P, ps[0:64, 0:MEM], WMK[:, kc, h * 64:(h + 1) * 64], xmT[:, kc, :], kc == 0, kc == 7,
               r=[("xmT", kc), "WMK"], w=[pk])
        CP(P, "act", KmT[0:64, h, :], ps[0:64, 0:MEM], r=[pk], w=["KmT"])
    for mt in range(2):
        ps, pk = pp.get()
        for kc in range(8):
            MM(P, ps[:, 0:256], xmT[:, kc, mt * 128:(mt + 1) * 128], WMK[:, kc, 256:512], kc == 0, kc == 7,
               r=[("xmT", kc), "WMK"], w=[pk])
        CP(P, "dve", Vm[:, mt, :], ps[:, 0:256], r=[pk], w=["Vm"])


def mem_attn_chunk(P, cx, pp, MQ, KmT, Vm, PT, Rb, MIXT, TC):
    for h in range(4):
        po = (h % 2) * 64
        for mt in range(2):
            ps, pk = pp.get()
            MM(P, ps[:, 0:TC], KmT[0:64, h, mt * 128:(mt + 1) * 128], MQ[0:64, h, :], True, True,
               r=["KmT", ("MQ", h)], w=[pk])
            ACT(P, PT[mt][:, 0:TC], ps[:, 0:TC], AF.Exp, r=[pk], w=[("PT", mt)], scale=0.125)
        pso = cx.ps[4]
        pss = cx.ps[5]
        for mt in range(2):
            MM(P, pso[po:po + 64, 0:TC], Vm[:, mt, h * 64:(h + 1) * 64], PT[mt][:, 0:TC], mt == 0, mt == 1,
               r=["Vm", ("PT", mt)], w=[("ps", 4)])
        for mt in range(2):
            MM(P, pss[:, 0:TC], cx.ones1[:, :], PT[mt][:, 0:TC], mt == 0, mt == 1,
               r=[("PT", mt)], w=[("ps", 5)])
        RECIP(P, Rb[:, 0:TC], pss[:, 0:TC], r=[("ps", 5)], w=["Rb"])
        TT(P, "dve", MIXT[po:po + 64, 6 + h // 2, :], pso[po:po + 64, 0:TC], Rb[po:po + 64, 0:TC], ALU.mult,
           r=[("ps", 4), "Rb"], w=[("MIXT", 6 + h // 2)])


def out_proj_chunk(P, cx, pp, WO, MIXT, Hc, hkey, TC):
    for n in range(8):
        ps, pk = pp.get()
        for c in range(8):
            MM(P, ps[:, 0:TC], WO[:, c, n * 128:(n + 1) * 128], MIXT[:, c, :], c == 0, c == 7,
               r=[("MIXT", c), "WO"], w=[pk])
        TT(P, "dve", Hc[:, n, :], Hc[:, n, :], ps[:, 0:TC], ALU.add, r=[pk, hkey], w=[hkey])


def gla_pass(P, cx, l, h_in, h_out):
    TC = 512
    NCH = S // TC
    A = cx.arena
    A.reset()
    WIN = A.bf16([128, 8, 2576])
    WO = A.bf16([128, 8, D])
    Hst = A.f32([128, 4096])
    cx.stg = [Hst[:, 0:2048], Hst[:, 2048:4096]]
    Hc = _shape(Hst, [128, 8, TC])
    hkey = "H"
    xnT = A.bf16([128, 8, TC])
    cx.sq = [A.bf16([128, TC]) for _ in range(2)]
    cx.rstd = A.f32([128, TC])
    gn = A.f32([128, 8])
    gm = A.f32([128, 8])
    LR = A.bf16([32, TC])
    WGf = A.f32([32, 384])
    WGa = A.bf16([32, 384])
    LA = A.f32([128, 4, 384])
    E1 = A.f32([128, 384])
    EQ = A.f32([96, 4, TC])
    EK = A.f32([96, 4, TC])
    EKO = A.f32([128, 4, 384])
    QIN = A.bf16([96, 4, TC])
    KDEC = A.bf16([96, 4, TC])
    V = A.bf16([128, 4, 768])
    GS = A.f32([128, 768])
    GG = A.bf16([128, 4, 768])
    KOUT = A.bf16([128, 4, 384])
    MQ = A.bf16([64, 4, TC])
    Sst = A.f32([96, 4, 192])
    Sbf = A.bf16([96, 4, 192])
    ATm = [A.bf16([128, 128]) for _ in range(2)]
    MAIN = [A.bf16([128, 768]) for _ in range(2)]
    MIXT = A.bf16([128, 8, TC])
    PT = [A.bf16([128, TC]) for _ in range(2)]
    Rb = A.f32([128, TC])
    KmT = A.bf16([64, 4, MEM])
    Vm = A.bf16([128, 2, MEM])
    ON = A.f32([128, 192])
    SS = A.f32([128, 4])
    JUNK = A.f32([128, 192])
    xmT = A.bf16([128, 8, MEM])
    WMK = A.bf16([128, 8, 512])

    pp = PsPool(cx, [1, 2, 3])
    P.dma("sync", gn, cx.d["norm_mixT"][l], w=["gn"])
    mem_setup(P, cx, l, WMK, xmT, KmT, Vm, gm, Hst)
    P.dma("sync", gn, cx.d["norm_mixT"][l], w=["gn"])
    P.dma("sync", ON, cx.d["gla_out_norm_rep"][l], w=["ON"])
    src = cx.d["gla_w_in"][l].rearrange("(kc p) c -> p kc c", p=128)
    ci = 0
    for c0 in range(0, 2576, 256):
        c1 = min(2576, c0 + 256)
        load_cast(P, cx, WIN[:, :, c0:c1], src[:, :, c0:c1], "WIN", [128, 8, c1 - c0], ci)
        ci += 1
    src = cx.d["w_out"][l].rearrange("(kc p) c -> p kc c", p=128)
    for c0 in range(0, D, 256):
        load_cast(P, cx, WO[:, :, c0:c0 + 256], src[:, :, c0:c0 + 256], "WO", [128, 8, 256], ci)
        ci += 1
    MEMSET(P, "dve", WGf, 0.0, w=["WGf"])
    P.dma("sync", WGf[0:16, :], cx.d["gla_w_gate_up"][l], w=["WGf"])
    P.dma("sync", WGf[16:17, :], cx.d["gla_b_gate"][l:l + 1, :], w=["WGf"])
    CP(P, "dve", WGa, WGf, r=["WGf"], w=["WGa"])
    MEMSET(P, "dve", LR, 1.0, w=["LR"])
    MEMSET(P, "dve", Sst, 0.0, w=["S"])
    MEMSET(P, "dve", Sbf, 0.0, w=["Sbf"])

    hin_v = h_in.rearrange("(kc p) t -> p kc t", p=128)
    hout_v = h_out.rearrange("(kc p) t -> p kc t", p=128)
    for ch in range(NCH):
        t0 = ch * TC
        P.dma("sync", Hc, hin_v[:, :, t0:t0 + TC], w=[hkey, ("stg", 0), ("stg", 1)])
        rms_chunk(P, cx, Hc, hkey, gn, xnT, "xnT", TC)
        xr = [("xnT", kc) for kc in range(8)]
        ps, pk = pp.get()
        for kc in range(8):
            MM(P, ps[0:16, 0:TC], WIN[:, kc, 2304:2320], xnT[:, kc, :], kc == 0, kc == 7, r=[("xnT", kc), "WIN"], w=[pk])
        CP(P, "act", LR[0:16, :], ps[0:16, 0:TC], r=[pk], w=["LR"])
        for s in range(4):
            ts = slice(s * 128, (s + 1) * 128)
            ps, pk = pp.get()
            MM(P, ps[:, 0:384], LR[0:32, ts], WGa[0:32, :], True, True, r=["LR", "WGa"], w=[pk])
            ACT(P, E1, ps[:, 0:384], AF.Exp, r=[pk], w=["E1"], scale=-1.0)
            ACT(P, LA[:, s, :], E1, AF.Ln, r=["E1"], w=[("LA", s)], bias=cx.one_col[:, 0:1])
            psb = cx.ps[6]
            for h in range(4):
                MM(P, psb[0:96, h * 128:(h + 1) * 128], LA[:, s, h * 96:(h + 1) * 96], cx.triu[:, :], True, True,
                   r=[("LA", s)], w=[("ps", 6)])
            ACT(P, EQ[:, :, ts], psb[0:96, :].rearrange("p (h t) -> p h t", h=4), AF.Exp, r=[("ps", 6)], w=[("EQ", s)])
            ACT(P, EK[:, :, ts], psb[0:96, :].rearrange("p (h t) -> p h t", h=4), AF.Exp, r=[("ps", 6)], w=[("EK", s)], scale=-1.0)
            psl = cx.ps[7]
            MM(P, psl[:, 0:384], cx.strictl[:, :], LA[:, s, :], True, True, r=[("LA", s)], w=[("ps", 7)])
            ACT(P, EKO[:, s, :], psl[:, 0:384], AF.Exp, r=[("ps", 7)], w=[("EKO", s)])
        eqr = [("EQ", s) for s in range(4)]
        ekr = [("EK", s) for s in range(4)]
        for h in range(4):
            ps, pk = pp.get()
            for kc in range(8):
                MM(P, ps[0:96, 0:TC], WIN[:, kc, h * 96:(h + 1) * 96], xnT[:, kc, :], kc == 0, kc == 7, r=[("xnT", kc), "WIN"], w=[pk])
            STT(P, QIN[:, h, :], ps[0:96, 0:TC], float(96 ** -0.5), EQ[:, h, :], ALU.mult, ALU.mult, r=[pk] + eqr, w=[("QIN", h)])
            ps, pk = pp.get()
            for kc in range(8):
                MM(P, ps[0:96, 0:TC], WIN[:, kc, 384 + h * 96:384 + (h + 1) * 96], xnT[:, kc, :], kc == 0, kc == 7, r=[("xnT", kc), "WIN"], w=[pk])
            TT(P, "dve", KDEC[:, h, :], ps[0:96, 0:TC], EK[:, h, :], ALU.mult, r=[pk] + ekr, w=[("KDEC", h)])
            ps, pk = pp.get()
            for kc in range(8):
                MM(P, ps[0:64, 0:TC], WIN[:, kc, 2320 + h * 64:2320 + (h + 1) * 64], xnT[:, kc, :], kc == 0, kc == 7, r=[("xnT", kc), "WIN"], w=[pk])
            CP(P, "act", MQ[:, h, :], ps[0:64, 0:TC], r=[pk], w=[("MQ", h)])
        for s in range(4):
            ts = slice(s * 128, (s + 1) * 128)
            for half in range(2):
                ps, pk = pp.get()
                for kc in range(8):
                    MM(P, ps[:, 0:384], xnT[:, kc, ts], WIN[:, kc, 768 + half * 384:768 + (half + 1) * 384], kc == 0, kc == 7,
                       r=[("xnT", kc), "WIN"], w=[pk])
                CP(P, "act", V[:, s, half * 384:(half + 1) * 384], ps[:, 0:384], r=[pk], w=[("V", s)])
            for half in range(2):
                ps, pk = pp.get()
                for kc in range(8):
                    MM(P, ps[:, 0:384], xnT[:, kc, ts], WIN[:, kc, 1536 + half * 384:1536 + (half + 1) * 384], kc == 0, kc == 7,
                       r=[("xnT", kc), "WIN"], w=[pk])
                ACT(P, GS[:, half * 384:(half + 1) * 384], ps[:, 0:384], AF.Silu, r=[pk], w=[("GS", half)])
            TT(P, "pool", GG[:, s, :].rearrange("p (h v) -> p h v", h=4), GS.rearrange("p (h v) -> p h v", h=4),
               ON[:, None, :].to_broadcast([128, 4, 192]), ALU.mult, r=[("GS", 0), ("GS", 1), "ON"], w=[("GG", s)])
            ps, pk = pp.get()
            for kc in range(8):
                MM(P, ps[:, 0:384], xnT[:, kc, ts], WIN[:, kc, 384:768], kc == 0, kc == 7, r=[("xnT", kc), "WIN"], w=[pk])
            TT(P, "dve", KOUT[:, s, :], ps[:, 0:384], EKO[:, s, :], ALU.mult, r=[pk, ("EKO", s)], w=[("KOUT", s)])
            MEMSET(P, "dve", SS, 0.0, w=["SS"])
            for h in range(4):
                ps, pk = pp.get()
                MM(P, ps[:, 0:128], KDEC[0:96, h, ts], QIN[0:96, h, ts], True, True, r=[("KDEC", h), ("QIN", h)], w=[pk])
                at = ATm[h % 2]
                TT(P, "dve", at, ps[:, 0:128], cx.maskji[:, :], ALU.mult, r=[pk], w=[("ATm", h % 2)])
                pso = cx.ps[4 + h]
                ok = ("ps", 4 + h)
                osl = slice(0, 192)
                MM(P, pso[:, osl], at, V[:, s, h * 192:(h + 1) * 192], True, False, r=[("ATm", h % 2), ("V", s)], w=[ok])
                MM(P, pso[:, osl], QIN[0:96, h, ts], Sbf[0:96, h, :], False, True, r=[("QIN", h), "Sbf"], w=[ok])
                ps, pk = pp.get()
                MM(P, ps[0:96, 0:192], KOUT[:, s, h * 96:(h + 1) * 96], V[:, s, h * 192:(h + 1) * 192], True, True,
                   r=[("KOUT", s), ("V", s)], w=[pk])
                STT(P, Sst[:, h, :], Sst[:, h, :], EQ[:, h, s * 128 + 127:s * 128 + 128], ps[0:96, 0:192], ALU.mult, ALU.add,
                    r=[pk, "S", ("EQ", s)], w=["S"])
                CP(P, "act", Sbf[:, h, :], Sst[:, h, :], r=["S"], w=["Sbf"])
                ACT(P, JUNK, pso[:, osl], AF.Square, r=[ok], w=["JUNK", "SS"], accum_out=SS[:, h:h + 1])
            TS(P, "dve", SS, SS, 1.0 / 192, EPS, ALU.mult, ALU.add, r=["SS"], w=["SS"])
            ACT(P, SS, SS, AF.Sqrt, r=["SS"], w=["SS"])
            RECIP(P, SS, SS, r=["SS"], w=["SS"])
            mn = MAIN[s % 2]
            mk = ("MAIN", s % 2)
            for h in range(4):
                pso = cx.ps[4 + h]
                ok = ("ps", 4 + h)
                osl = slice(0, 192)
                STT(P, mn[:, h * 192:(h + 1) * 192], pso[:, osl], SS[:, h:h + 1], GG[:, s, h * 192:(h + 1) * 192], ALU.mult, ALU.mult,
                    r=[ok, "SS", ("GG", s)], w=[mk])
            for c in range(6):
                ps, pk = pp.get()
                MM(P, ps[:, 0:128], mn[:, c * 128:(c + 1) * 128], cx.ident[:, :], True, True, r=[mk], w=[pk])
                CP(P, "act" if c % 2 == 0 else "dve", MIXT[:, c, ts], ps[:, 0:128], r=[pk], w=[("MIXT", c)])
        mem_attn_chunk(P, cx, pp, MQ, KmT, Vm, PT, Rb, MIXT, TC)
        out_proj_chunk(P, cx, pp, WO, MIXT, Hc, hkey, TC)
        P.dma("sync", hout_v[:, :, t0:t0 + TC], Hc, r=[hkey])
    P.barrier()


def kv_pass(P, cx, h_in):
    TC = 512
    NCH = S // TC
    A = cx.arena
    A.reset()
    WKV = A.bf16([128, 8, 1536])
    Hst = A.f32([128, 4096])
    cx.stg = [Hst[:, 0:2048], Hst[:, 2048:4096]]
    Hc = _shape(Hst, [128, 8, TC])
    xnT = A.bf16([128, 8, TC])
    cx.sq = [A.bf16([128, TC]) for _ in range(2)]
    cx.rstd = A.f32([128, TC])
    gn = A.f32([128, 8])
    KT = A.bf16([128, 4, S])
    CF = A.bf16([128, 4, S])
    VS = A.bf16([128, 32, 4, 65])
    VW = A.bf16([128, 32, 4, 65])
    W1f = A.f32([128, 32 * 128])
    W1 = A.bf16([128, 32, 128])
    W2f = A.f32([128, 2, 64])
    W2 = A.bf16([128, 2, 64])
    POSf = A.f32([128, 32])
    POS = A.bf16([128, 32])
    B1 = A.f32([128, 2])
    B2c = A.f32([64, 2])
    B2rf = A.f32([1, 64])
    B2r = A.bf16([1, 64])
    CST = A.f32([128, 2])
    HID = A.bf16([128, 256])
    KC = A.bf16([64, 4, 256])
    VC = A.bf16([128, 2, 4, 65])
    pp = PsPool(cx, [1, 2, 3, 4, 5, 6, 7])

    P.dma("sync", gn, cx.d["kv_normT"], w=["gn"])
    src = cx.d["w_kv_shared"].rearrange("(kc p) c -> p kc c", p=128)
    for i, c0 in enumerate(range(0, 1536, 256)):
        load_cast(P, cx, WKV[:, :, c0:c0 + 256], src[:, :, c0:c0 + 256], "WKV", [128, 8, 256], i)
    MEMSET(P, "pool", VS, 1.0, w=["VS"])
    MEMSET(P, "pool", VW, 1.0, w=["VW"])
    hin_v = h_in.rearrange("(kc p) t -> p kc t", p=128)
    for ch in range(NCH):
        t0 = ch * TC
        P.dma("sync", Hc, hin_v[:, :, t0:t0 + TC], w=["H", ("stg", 0), ("stg", 1)])
        rms_chunk(P, cx, Hc, "H", gn, xnT, "xnT", TC)
        for (j, dst, po, key) in ((0, CF, 0, "CF"), (1, CF, 64, "CF"), (2, KT, 0, "KT"), (4, KT, 64, "KT")):
            for g in range(4):
                ps, pk = pp.get()
                c0 = j * 256 + g * 64
                for kc in range(8):
                    MM(P, ps[po:po + 64, 0:TC], WKV[:, kc, c0:c0 + 64], xnT[:, kc, :], kc == 0, kc == 7,
                       r=[("xnT", kc), "WKV"], w=[pk])
                CP(P, "act" if g % 2 == 0 else "dve", dst[po:po + 64, g, t0:t0 + TC], ps[po:po + 64, 0:TC], r=[pk], w=[key])
        for s4 in range(4):
            tile = ch * 4 + s4
            ts = slice(s4 * 128, (s4 + 1) * 128)
            for (j, dst, key) in ((3, VS, "VS"), (5, VW, "VW")):
                ps, pk = pp.get()
                for kc in range(8):
                    MM(P, ps[:, 0:256], xnT[:, kc, ts], WKV[:, kc, j * 256:(j + 1) * 256], kc == 0, kc == 7,
                       r=[("xnT", kc), "WKV"], w=[pk])
                CP(P, "act" if j == 3 else "dve", dst[:, tile, :, 0:64], ps[:, 0:256].rearrange("p (g d) -> p g d", g=4),
                   r=[pk], w=[key])
    P.dma("sync", W1f[0:64, :].rearrange("p (l n) -> p l n", l=32), cx.d["cmp_w1"][0].rearrange("(l d) n -> d l n", d=64), w=["W1f"])
    P.dma("sync", W1f[64:128, :].rearrange("p (l n) -> p l n", l=32), cx.d["cmp_w1"][1].rearrange("(l d) n -> d l n", d=64), w=["W1f"])
    CP(P, "dve", W1.rearrange("p l n -> p (l n)"), W1f, r=["W1f"], w=["W1"])
    P.dma("sync", W2f, cx.d["cmp_w2"].rearrange("j n d -> n j d"), w=["W2f"])
    CP(P, "dve", W2, W2f, r=["W2f"], w=["W2"])
    P.dma("sync", POSf[0:64, :], cx.d["cmp_posT"][0], w=["POSf"])
    P.dma("sync", POSf[64:128, :], cx.d["cmp_posT"][1], w=["POSf"])
    CP(P, "dve", POS, POSf, r=["POSf"], w=["POS"])
    P.dma("sync", B1, cx.d["cmp_b1T"], w=["B1"])
    P.dma("sync", B2c, cx.d["cmp_b2T"], w=["B2c"])
    P.dma("sync", B2rf, cx.d["cmp_b2"][1:2, :], w=["B2rf"])
    CP(P, "dve", B2r, B2rf, r=["B2rf"], w=["B2r"])
    MEMSET(P, "dve", KC, 0.0, w=["KC"])
    MEMSET(P, "dve", VC, 0.0, w=["VC"])
    MEMSET(P, "dve", HID, 0.0, w=["HID"])
    for j in range(2):
        po = j * 64
        ps, pk = pp.get()
        for l in range(32):
            MM(P, ps[:, 0:1], W1[po:po + 64, l, :], POS[po:po + 64, l:l + 1], l == 0, l == 31, r=["W1", "POS"], w=[pk])
        TT(P, "dve", CST[:, j:j + 1], ps[:, 0:1], B1[:, j:j + 1], ALU.add, r=[pk, "B1"], w=["CST"])
        for g in range(4):
            ps, pk = pp.get()
            for l in range(32):
                MM(P, ps[:, 0:255], W1[po:po + 64, l, :], CF[po:po + 64, g, l:l + 16 * 254 + 1:16], l == 0, l == 31,
                   r=["W1", "CF"], w=[pk])
            ACT(P, HID[:, 0:255], ps[:, 0:255], AF.Silu, r=[pk, "CST"], w=["HID"], bias=CST[:, j:j + 1])
            if j == 0:
                ps, pk = pp.get()
                MM(P, ps[0:64, 0:255], W2[:, 0, :], HID[:, 0:255], True, True, r=["W2", "HID"], w=[pk])
                ACT(P, KC[:, g, 0:255], ps[0:64, 0:255], AF.Identity, r=[pk, "B2c"], w=["KC"], bias=B2c[:, 0:1])
            else:
                for kt in range(2):
                    n = 128 if kt == 0 else 127
                    ps, pk = pp.get()
                    MM(P, ps[0:n, 0:64], HID[:, kt * 128:kt * 128 + n], W2[:, 1, :], True, False, r=["W2", "HID"], w=[pk])
                    MM(P, ps[0:n, 0:64], cx.ones1[0:1, 0:n], B2r[0:1, :], False, True, r=["B2r"], w=[pk])
                    CP(P, "dve", VC[0:n, kt, g, 0:64], ps[0:n, 0:64], r=[pk], w=["VC"])
    MEMSET(P, "dve", VC[:, :, :, 64:65], 1.0, w=["VC"])
    P.dma("sync", cx.d_KT, KT.rearrange("p g t -> p (g t)"), r=["KT"])
    P.dma("sync", cx.d_VS, VS.rearrange("p a g d -> p (a g d)"), r=["VS"])
    P.dma("sync", cx.d_VW, VW.rearrange("p a g d -> p (a g d)"), r=["VW"])
    P.dma("sync", cx.d_KC, KC.rearrange("p g t -> p (g t)"), r=["KC"])
    P.dma("sync", cx.d_VC, VC.rearrange("p a g d -> p (a g d)"), r=["VC"])
    P.barrier()


NSA_U = 768
import os
NSA_DBG = int(os.environ.get('NSA_DBG', '9'))


def nsa_pass(P, cx, l, h_in, h_out):
    li = l - 2
    TC = 256
    NCH = S // TC
    A = cx.arena
    A.reset()
    Hst = A.f32([128, 4096])
    cx.stg = [Hst[:, 0:2048], Hst[:, 2048:4096]]
    Hc = _shape(Hst[:, 0:2048], [128, 8, TC])
    hkey = ("stg", 0)
    cx.sq = [A.bf16([128, TC]) for _ in range(2)]
    cx.rstd = A.f32([128, TC])
    gn = A.f32([128, 8])
    gm = A.f32([128, 8])
    KmT = A.bf16([64, 4, MEM])
    Vm = A.bf16([128, 2, MEM])
    mark = A.off
    xmT = A.bf16([128, 8, MEM])
    WMK = A.bf16([128, 8, 512])
    pp = PsPool(cx, [1, 2])
    pp1 = PsPool(cx, [3])
    psall = cx.psall
    mem_setup(P, cx, l, WMK, xmT, KmT, Vm, gm, Hst)
    P.barrier()
    A.off = mark
    WIN = A.bf16([128, 8, 1060])
    WO = A.bf16([128, 8, D])
    xnT = A.bf16([128, 8, TC])
    KT = A.bf16([128, 4, S])
    IND = A.bf16([128, S])
    VS = A.bf16([128, 32, 4, 65])
    VW = A.bf16([128, 32, 4, 65])
    KcT = A.bf16([64, 4, 256])
    VC = A.bf16([128, 2, 4, 65])
    OV = A.bf16([128, 2, 64])
    CPT = [A.f32([128, 12, 128]) for _ in range(2)]
    WMP = A.f32([128, 128])
    Fn = A.f32([64, 12, 12])
    RB = A.f32([33, 12])
    OHs = Hst[0:33, 2048:2048 + NSA_U]
    GVs = Hst[0:12, 2816:2816 + NSA_U]
    QS = A.bf16([128, 12, TC])
    QMP = [A.bf16([64, 12, 128]) for _ in range(2)]
    OT = [A.f32([65, 384]) for _ in range(2)]
    GT = A.f32([64, 4, 36])
    MQ = A.bf16([64, 4, TC])
    EC = A.f32([64, 3, 256])
    PC = A.bf16([64, 3, 256])
    PCT = A.bf16([128, 2, 192])
    SUMC = A.f32([64, 4])
    SC = A.f32([64, 4, 64])
    SC2 = A.f32([64, 64])
    M8 = A.f32([64, 16])
    SCT4 = A.f32([64, 4, 2, 64])
    CMPC = [A.bf16([64, 2, 12, 64]) for _ in range(2)]
    SMpad = A.bf16([64, 4, 128])
    SB = [A.f32([128, 384]) for _ in range(2)]
    PTs = [A.bf16([128, 384]) for _ in range(3)]
    MAINQ = A.bf16([64, 2, 768])
    TMP = [A.f32([64, 2, 3, 64]) for _ in range(2)]
    WX = A.f32([64, 2, 3, 4])
    MIXT = A.bf16([128, 8, TC])
    PT = [A.bf16([128, TC]) for _ in range(2)]
    Rb = A.f32([128, TC])
    P.dma("sync", gn, cx.d["norm_mixT"][l], w=["gn"])
    src = cx.d["nsa_w_in"][li].rearrange("(kc p) c -> p kc c", p=128)
    ci = 0
    for c0 in range(0, 1060, 256):
        c1 = min(1060, c0 + 256)
        load_cast(P, cx, WIN[:, :, c0:c1], src[:, :, c0:c1], "WIN", [128, 8, c1 - c0], ci)
        ci += 1
    src = cx.d["w_out"][l].rearrange("(kc p) c -> p kc c", p=128)
    for c0 in range(0, D, 256):
        load_cast(P, cx, WO[:, :, c0:c0 + 256], src[:, :, c0:c0 + 256], "WO", [128, 8, 256], ci)
        ci += 1
    for c0 in range(0, S, 2048):
        load_cast(P, cx, IND[:, c0:c0 + 2048], cx.d["ind"][:, c0:c0 + 2048], "IND", [128, 2048], ci)
        ci += 1
    load_cast(P, cx, OV.rearrange("p a j -> p (a j)"), cx.d["ovl"], "OV", [128, 128], ci)
    ci += 1
    P.dma("sync", KT.rearrange("p g t -> p (g t)"), cx.d_KT, w=["KT"])
    P.dma("sync", VS.rearrange("p a g d -> p (a g d)"), cx.d_VS, w=["VS"])
    P.dma("sync", VW.rearrange("p a g d -> p (a g d)"), cx.d_VW, w=["VW"])
    P.dma("sync", VC.rearrange("p a g d -> p (a g d)"), cx.d_VC, w=["VC"])
    P.dma("sync", KcT.rearrange("p g t -> p (g t)"), cx.d_KC, w=["KcT"])
    MEMSET(P, "dve", RB, -30000.0, w=["RB"])
    P.dma("sync", RB[0:32, :], cx.d["rel_bias"], w=["RB"])
    P.dma("sync", OHs, cx.d["oh_u"], w=[("stg", 1)])
    for u0 in range(0, NSA_U, 384):
        ps, pk = pp.get()
        MM(P, ps[0:12, 0:384], RB[0:33, :], OHs[0:33, u0:u0 + 384], True, True, r=["RB", ("stg", 1)], w=[pk])
        CP(P, "dve", GVs[:, u0:u0 + 384], ps[0:12, 0:384], r=[pk], w=[("stg", 1)])
    gv_w = P.dma("sync", cx.d_GV, GVs, r=[("stg", 1)], w=["dGV"])
    XH = _shape(Hst[:, 0:768], [128, 12, 64])
    for mi, m in enumerate((0, 64, 128, 192, 512, 576)):
        src = bass.AP(tensor=cx.d_GV.tensor, offset=m, ap=[[1, 128], [NSA_U, 12], [1, 64]])
        P.dma("sync", XH, src, r=["dGV"], w=[("stg", 0)])
        for half in range(2):
            ps, pk = pp.get()
            MM(P, ps[:, 0:384], cx.antiid[:, :], XH.rearrange("p h i -> p (h i)")[:, half * 384:(half + 1) * 384], True, True,
               r=[("stg", 0)], w=[pk])
            if mi < 4:
                CP(P, "dve", CPT[mi // 2][:, 6 * half:6 * half + 6, (mi % 2) * 64:(mi % 2) * 64 + 64],
                   ps[:, 0:384].rearrange("p (h i) -> p h i", h=6), r=[pk], w=[("CPT", mi // 2)])
            elif half == 0:
                CP(P, "dve", WMP[:, (mi - 4) * 64:(mi - 4) * 64 + 64], ps[:, 0:64], r=[pk], w=["WMP"])
    for xq in range(12):
        src = bass.AP(tensor=cx.d_GV.tensor, offset=240 - 16 * xq, ap=[[1, 64], [NSA_U, 12], [1, 1]])
        P.op("sync", lambda e, xq=xq, src=src: e.dma_start(out=Fn[:, :, xq:xq + 1], in_=src, allow_slow_non_contiguous=True),
             r=["dGV"], w=["Fn"], dma=True)
    MEMSET(P, "dve", PC, 0.0, w=["PC"])
    MEMSET(P, "dve", PCT, 0.0, w=["PCT"])
    MEMSET(P, "dve", SMpad, 0.0, w=["SM"])
    MEMSET(P, "dve", EC, 0.0, w=["EC"])

    hin_v = h_in.rearrange("(kc p) t -> p kc t", p=128)
    hout_v = h_out.rearrange("(kc p) t -> p kc t", p=128)
    for ch in range(NCH):
        t0 = ch * TC
        P.dma("sync", Hc, hin_v[:, :, t0:t0 + TC], w=[hkey])
        rms_chunk(P, cx, Hc, hkey, gn, xnT, "xnT", TC)
        for h in range(12):
            ps, pk = pp.get()
            for kc in range(8):
                MM(P, ps[0:64, 0:TC], WIN[:, kc, h * 64:(h + 1) * 64], xnT[:, kc, :], kc == 0, kc == 7, r=[("xnT", kc), "WIN"], w=[pk])
            ACT(P, QS[0:64, h, :], ps[0:64, 0:TC], AF.Identity, r=[pk], w=[("QS", h)], scale=0.125)
            CP(P, "dve", QS[64:128, h, :], QS[0:64, h, :], r=[("QS", h)], w=[("QSw", h)])
        for h in range(4):
            ps, pk = pp.get()
            for kc in range(8):
                MM(P, ps[0:64, 0:TC], WIN[:, kc, 804 + h * 64:804 + (h + 1) * 64], xnT[:, kc, :], kc == 0, kc == 7, r=[("xnT", kc), "WIN"], w=[pk])
            CP(P, "act", MQ[:, h, :], ps[0:64, 0:TC], r=[pk], w=[("MQ", h)])
        for cl in range(4):
            ps, pk = pp.get()
            for kc in range(8):
                MM(P, ps[0:64, 0:36], xnT[:, kc, cl * 64:(cl + 1) * 64], WIN[:, kc, 768:804], kc == 0, kc == 7, r=[("xnT", kc), "WIN"], w=[pk])
            ACT(P, GT[:, cl, :], ps[0:64, 0:36], AF.Sigmoid, r=[pk], w=[("GT", cl)])
        def phase1(c, cl):
            tsl = slice(cl * 64, (cl + 1) * 64)
            hf = c % 2
            pr = (c // 2) % 2
            ncol = min(255, 4 * c + 3)
            nkt = 1 if ncol <= 128 else 2
            sct = SCT4[:, cl, :, :]
            sctk = "SCT4"
            kn0, kn1 = max(0, 4 * c - 9), min(ncol, 4 * c + 3)
            x0 = kn0 - (4 * c - 9)
            qm = QMP[pr]
            for g in range(4):
                hs = slice(3 * g, 3 * g + 3)
                qmk = ("QM", pr, g, hf)
                pcs = psall[0:64, 6 * 512:6 * 512 + 768].rearrange("p (r k) -> p r k", r=3)
                ck = [("ps", 6), ("ps", 7)]
                for r in range(3):
                    h = 3 * g + r
                    MM(P, pcs[:, r, 0:ncol], QS[0:64, h, tsl], KcT[0:64, g, 0:ncol], True, True, r=[("QS", h), "KcT"], w=ck)
                yield
                TT(P, "dve", pcs[:, :, kn0:kn1], pcs[:, :, kn0:kn1], Fn[:, hs, x0:x0 + (kn1 - kn0)], ALU.add, r=ck + ["Fn"], w=ck)
                yield
                ACT(P, EC[:, :, 0:ncol], pcs[:, :, 0:ncol], AF.Exp, r=ck, w=["EC"])
                yield
                P.op("dve", lambda e, ncol=ncol: e.reduce_sum(out=SUMC[:, 0:3], in_=EC[:, :, 0:ncol], axis=AX.X), r=["EC"], w=["SUMC"])
                yield
                TS(P, "dve", SUMC[:, 0:3], SUMC[:, 0:3], 1e-30, None, ALU.max, None, r=["SUMC"], w=["SUMC"])
                yield
                RECIP(P, SUMC[:, 0:3], SUMC[:, 0:3], r=["SUMC"], w=["SUMC"])
                yield
                TT(P, "dve", PC[:, :, 0:ncol], EC[:, :, 0:ncol], SUMC[:, 0:3, None].to_broadcast([64, 3, ncol]), ALU.mult,
                   r=["EC", "SUMC"], w=["PC"])
                yield
                ps, pk = pp1.get()
                for kt in range(nkt):
                    for r in range(3):
                        MM(P, ps[:, (kt * 3 + r) * 64:(kt * 3 + r + 1) * 64], PC[0:64, r, kt * 128:(kt + 1) * 128], cx.ident[0:64, 0:64],
                           True, True, r=["PC"], w=[pk])
                CP(P, "act", PCT[:, 0:nkt, :].rearrange("p a b -> p (a b)"), ps[:, 0:nkt * 192], r=[pk], w=["PCT"])
                yield
                poc = cx.ps[5]
                for r in range(3):
                    for kt in range(nkt):
                        MM(P, poc[0:64, r * 65:(r + 1) * 65], PCT[:, kt, r * 64:(r + 1) * 64], VC[:, kt, g, :], kt == 0, kt == nkt - 1,
                           r=["PCT", "VC"], w=[("ps", 5)])
                n = 0
                for r in range(3):
                    for kt in range(nkt):
                        MM(P, poc[0:64, 256:320], PCT[:, kt, r * 64:(r + 1) * 64], OV[:, kt, :], n == 0, n == 3 * nkt - 1,
                           r=["PCT", "OV"], w=[("ps", 5)])
                        n += 1
                yield
                gv = GT[:, cl, 9 * g:9 * g + 9].rearrange("p (r x) -> p r x", r=3)[:, :, 0:1]
                ov = poc[0:64, 0:195].rearrange("p (r d) -> p r d", r=3)
                TT(P, "dve", CMPC[pr][:, hf, hs, :], ov[:, :, 0:64], gv.to_broadcast([64, 3, 64]), ALU.mult,
                   r=[("ps", 5), ("GT", cl)], w=[("CMPC", pr, g, hf)])
                yield
                TT(P, "dve", SC[:, g, :], poc[0:64, 256:320], sct[:, 0, :], ALU.mult, r=[("ps", 5), sctk], w=[("SC", g)])
                yield
                TT(P, "dve", SC[:, g, :], SC[:, g, :], sct[:, 1, :], ALU.add, r=[("SC", g), sctk], w=[("SC", g)])
                yield
                P.op("dve", lambda e, g=g: e.max(out=M8[:, 0:8], in_=SC[:, g, :]), r=[("SC", g)], w=["M8"])
                yield
                P.op("dve", lambda e, g=g: e.match_replace(out=SC2, in_to_replace=M8[:, 0:8], in_values=SC[:, g, :], imm_value=-1e9),
                     r=["M8", ("SC", g)], w=["SC2"])
                yield
                P.op("dve", lambda e: e.max(out=M8[:, 8:16], in_=SC2), r=["SC2"], w=["M8"])
                yield
                TS(P, "dve", SMpad[:, g, 64:128], SC[:, g, :], M8[:, 15:16], None, ALU.is_ge, None, r=[("SC", g), "M8"], w=["SM"])
                yield
                ps, pk = pp1.get()
                MM(P, ps[0:64, 0:64], SMpad[0:64, g, 64:128], cx.ident[0:64, 0:64], True, True, r=["SM"], w=[pk])
                TS(P, "dve", qm[0:64, hs, hf * 64:(hf + 1) * 64], ps[0:64, None, 0:64].to_broadcast([64, 3, 64]), 30000.0, -30000.0,
                   ALU.mult, ALU.add, r=[pk], w=[qmk])
                yield

        def phase2(c0, cl0):
            tsl2 = slice(cl0 * 64, cl0 * 64 + 128)
            kt_c = c0 // 2
            pr = (c0 // 2) % 2
            qm = QMP[pr]
            tok = cx.ps[3]
            tkey = ("ps", 3)
            for g in range(4):
                hs = slice(3 * g, 3 * g + 3)
                for br in range(2):
                    if br == 0:
                        kts = list(range(0, kt_c + 1))
                        qkey = [("QS", 3 * g + r) for r in range(3)] + [("QM", pr, g, 0), ("QM", pr, g, 1)]
                        VA = VS
                        vkey = "VS"
                        pob = cx.ps[4]
                        okey = ("ps", 4)
                    else:
                        kts = list(range(max(0, (c0 - 8) // 2), kt_c + 1))
                        qkey = [("QSw", 3 * g + r) for r in range(3)]
                        VA = VW
                        vkey = "VW"
                        pob = cx.ps[0]
                        okey = ("ps", 0)
                    def score(ki, kt):
                        ps, pk = pp.get()
                        if br == 0:
                            MM(P, ps[:, 0:384], KT[0:64, g, kt * 128:(kt + 1) * 128], QS[0:64, hs, tsl2], True, False,
                               r=["KT"] + qkey, w=[pk])
                            MM(P, ps[:, 0:384], IND[0:64, kt * 128:(kt + 1) * 128], qm[0:64, hs, :], False, True,
                               r=["IND"] + qkey, w=[pk])
                        else:
                            MM(P, ps[:, 0:384], KT[64:128, g, kt * 128:(kt + 1) * 128], QS[64:128, hs, tsl2], True, True,
                               r=["KT"] + qkey, w=[pk])
                        return ps, pk

                    def finish(ki, kt, ps, pk):
                        pt = PTs[(ki + br) % 3]
                        ptk = ("PTs", (ki + br) % 3)
                        corr = None
                        if kt == kt_c:
                            corr, corrk = CPT[0][:, hs, :], ("CPT", 0)
                        elif kt == kt_c - 1:
                            corr, corrk = CPT[1][:, hs, :], ("CPT", 1)
                        elif br == 1 and c0 >= 8 and ki == 0:
                            corr, corrk = WMP[:, None, :].to_broadcast([128, 3, 128]), "WMP"
                        if corr is not None:
                            sb = SB[ki % 2]
                            TT(P, "dve", sb.rearrange("p (r q) -> p r q", r=3), ps[:, 0:384].rearrange("p (r q) -> p r q", r=3),
                               corr, ALU.add, r=[pk, corrk], w=[("SB", ki % 2)])
                            ACT(P, pt, sb, AF.Exp, r=[("SB", ki % 2)], w=[ptk])
                        else:
                            ACT(P, pt, ps[:, 0:384], AF.Exp, r=[pk], w=[ptk])
                        MM(P, pob[0:65, 0:384], VA[:, kt, g, :], pt, ki == 0, ki == len(kts) - 1, r=[ptk, vkey], w=[okey])

                    prev = None
                    for ki, kt in enumerate(kts):
                        cur = (ki, kt) + score(ki, kt)
                        if prev is not None:
                            finish(*prev)
                            yield
                        prev = cur
                    finish(*prev)
                    yield
                    CP(P, "act" if br == 0 else "dve", OT[br], pob[0:65, 0:384], r=[okey], w=[("OT", br)])
                    for hf in range(2):
                        for r in range(3):
                            MM(P, tok[0:64, (hf * 3 + r) * 65:(hf * 3 + r + 1) * 65], OT[br][0:65, r * 128 + hf * 64:r * 128 + hf * 64 + 64],
                               cx.identf[0:65, 0:65], True, True, r=[("OT", br)], w=[tkey])
                    x = br + 1
                    ov = tok[0:64, 0:390].rearrange("p (h r d) -> p h r d", h=2, r=3)
                    wx = WX[:, :, :, x:x + 1]
                    TS(P, "dve", wx, ov[:, :, :, 64:65], 1e-30, None, ALU.max, None, r=[tkey], w=[("WX", x)])
                    RECIP(P, wx, wx, r=[("WX", x)], w=[("WX", x)])
                    gv = GT[:, cl0:cl0 + 2, 9 * g:9 * g + 9].rearrange("p h (r x) -> p h r x", r=3)[:, :, :, x:x + 1]
                    TT(P, "dve", wx, wx, gv, ALU.mult, r=[("WX", x), ("GT", cl0), ("GT", cl0 + 1)], w=[("WX", x)])
                    TT(P, "dve", TMP[br], ov[:, :, :, 0:64], wx.to_broadcast([64, 2, 3, 64]), ALU.mult,
                       r=[tkey, ("WX", x)], w=[("TMP", br)])
                    yield
                mq = MAINQ[:, :, 192 * g:192 * (g + 1)].rearrange("p h (r d) -> p h r d", r=3)
                TT(P, "dve", TMP[0], TMP[0], CMPC[pr][:, :, hs, :], ALU.add,
                   r=[("TMP", 0), ("CMPC", pr, g, 0), ("CMPC", pr, g, 1)], w=[("TMP", 0)])
                TT(P, "dve", mq, TMP[0], TMP[1], ALU.add, r=[("TMP", 0), ("TMP", 1)], w=["MAINQ"])
                yield
            for hf in range(2):
                ps, pk = pp.get()
                for c6 in range(6):
                    MM(P, ps[:, c6 * 64:(c6 + 1) * 64], MAINQ[0:64, hf, c6 * 128:(c6 + 1) * 128], cx.ident[0:64, 0:64], True, True, r=["MAINQ"], w=[pk])
                CP(P, "act", MIXT[:, 0:6, (cl0 + hf) * 64:(cl0 + hf + 1) * 64], ps[:, 0:384].rearrange("p (c q) -> p c q", c=6), r=[pk],
                   w=[("MIXT", i) for i in range(6)])
            yield

        def chain(*gens):
            for gq in gens:
                for _ in gq:
                    yield

        P.dma("sync", SCT4, cx.d["sct4b"][ch], w=["SCT4"])
        for _ in chain(phase1(ch * 4, 0), phase1(ch * 4 + 1, 1)):
            pass
        for pi in range(2):
            c0 = ch * 4 + 2 * pi
            g2 = phase2(c0, 2 * pi)
            g1 = chain(phase1(c0 + 2, 2), phase1(c0 + 3, 3)) if pi == 0 else iter(())
            n2 = 4 * ((c0 // 2 + 1) + (c0 // 2 + 1 - max(0, (c0 - 8) // 2)) + 5) + 1
            per = max(1, -(-152 // n2))
            for _ in g2:
                for _k in range(per):
                    next(g1, None)
            for _ in g1:
                pass
        mem_attn_chunk(P, cx, pp, MQ, KmT, Vm, PT, Rb, MIXT, TC)
        out_proj_chunk(P, cx, pp, WO, MIXT, Hc, hkey, TC)
        P.dma("sync", hout_v[:, :, t0:t0 + TC], Hc, r=[hkey])
    P.barrier()


def final_pass(P, cx, h_in, out):
    TC = 256
    A = cx.arena
    A.reset()
    Hb = [A.f32([128, 8, TC]) for _ in range(2)]
    Ob = [A.f32([128, 8, TC]) for _ in range(2)]
    cx.sq = [A.bf16([128, TC]) for _ in range(2)]
    cx.rstd = A.f32([128, TC])
    gn = A.f32([128, 8])
    P.dma("sync", gn, cx.d["final_normT"], w=["gn"])
    hin_v = h_in.rearrange("(kc p) t -> p kc t", p=128)
    P.dma("sync", Hb[0], hin_v[:, :, 0:TC], w=[("H", 0)])
    for ch in range(S // TC):
        t0 = ch * TC
        b = ch % 2
        if ch + 1 < S // TC:
            P.dma("sync", Hb[1 - b], hin_v[:, :, t0 + TC:t0 + 2 * TC], w=[("H", 1 - b)])
        rms_chunk(P, cx, Hb[b], ("H", b), gn, Ob[b], ("O", b), TC)
        o = P.dma("sync", out.rearrange("(kc p) t -> p kc t", p=128)[:, :, t0:t0 + TC], Ob[b],
                  r=[(("O", b), kc) for kc in range(8)])
        P.out_dmas.append(o)
    P.barrier()


IN_SPECS = [
    ("xT", [D, S]), ("memT", [D, MEM]),
    ("norm_mixT", [DEPTH, 128, 8]), ("norm_memT", [DEPTH, 128, 8]), ("norm_ffnT", [DEPTH, 128, 8]),
    ("final_normT", [128, 8]),
    ("w_up", [DEPTH, D, 2 * FFN]), ("w_down", [DEPTH, FFN, D]),
    ("conv_wT", [DEPTH, 128, 44, 3]), ("conv_bT", [DEPTH, 128, 44]),
    ("ones", [128, 128]),
    ("w_mem_kv", [DEPTH, D, 512]), ("w_out", [DEPTH, D, D]),
    ("gla_w_in", [2, D, 2576]), ("gla_w_gate_up", [2, 16, 384]), ("gla_b_gate", [2, 384]), ("gla_out_norm_rep", [2, 128, 192]),
    ("cmat", [6, 128, 128]),
    ("nsa_w_in", [2, D, 1060]), ("kv_normT", [128, 8]), ("w_kv_shared", [D, 1536]),
    ("cmp_posT", [2, 64, 32]), ("cmp_w1", [2, 2048, 128]), ("cmp_b1T", [128, 2]), ("cmp_w2", [2, 128, 64]),
    ("cmp_b2", [2, 64]), ("cmp_b2T", [64, 2]), ("rel_bias", [32, 12]),
    ("ind", [128, S]), ("ovl", [128, 128]), ("sct4b", [16, 64, 4, 2, 64]), ("oh_u", [33, NSA_U]),
]


def build(stages=("ffn0", "final")):
    nc = bass.Bass("TRN2", target_bir_lowering=False)
    cx = Ctx()
    cx.d = {}
    for name, shape in IN_SPECS:
        cx.d[name] = nc.dram_tensor(name, shape, F32, kind="ExternalInput").ap()
    outT = nc.dram_tensor("outT", [D, S], F32, kind="ExternalOutput").ap()
    hA = nc.dram_tensor("hA", [D, S], F32, kind="Internal").ap()
    cx.d_KT = nc.dram_tensor("d_KT", [128, 4 * S], BF16, kind="Internal").ap()
    cx.d_VS = nc.dram_tensor("d_VS", [128, 32 * 4 * 65], BF16, kind="Internal").ap()
    cx.d_VW = nc.dram_tensor("d_VW", [128, 32 * 4 * 65], BF16, kind="Internal").ap()
    cx.d_KC = nc.dram_tensor("d_KC", [64, 4 * 256], BF16, kind="Internal").ap()
    cx.d_VC = nc.dram_tensor("d_VC", [128, 2 * 4 * 65], BF16, kind="Internal").ap()
    cx.d_GV = nc.dram_tensor("d_GV", [12, NSA_U], F32, kind="Internal").ap()
    P = Prog(nc)
    with ExitStack() as st:
        NW = 51200
        arena_t = st.enter_context(nc.sbuf_tensor("arena", [128, NW], F32))
        ones_bf = st.enter_context(nc.sbuf_tensor("ones_bf", [128, 128], BF16))
        ones_f = st.enter_context(nc.sbuf_tensor("ones_f", [128, 128], F32))
        cx.arena = Arena(arena_t, NW)
        cx.ones_bf = ones_bf
        psall = st.enter_context(nc.psum_tensor("psall", [128, 4096], F32))
        cx.psall = psall
        cx.ps = [psall[:, i * 512:(i + 1) * 512] for i in range(8)]
        sems = {e: st.enter_context(nc.semaphore("s_" + e)) for e in Prog.ENGS}
        dsems = [st.enter_context(nc.semaphore("d%d" % i)) for i in range(NDSEM)]
        block = st.enter_context(nc.Block())

        P.dma("sync", ones_f[:, :], cx.d["ones"], w=["ones_f"])
        P.op("dve", lambda e: e.tensor_copy(out=ones_bf[:, :], in_=ones_f[:, :]), r=["ones_f"], w=["ones"])
        cmf = st.enter_context(nc.sbuf_tensor("cmf", [128, 6, 128], F32))
        cmb = st.enter_context(nc.sbuf_tensor("cmb", [128, 6, 128], BF16))
        one_col = st.enter_context(nc.sbuf_tensor("one_col", [128, 2], F32))
        P.dma("sync", cmf[:, :, :], cx.d["cmat"].rearrange("c p n -> p c n"), w=["cmf"])
        P.op("dve", lambda e: e.tensor_copy(out=cmb[:, :, :], in_=cmf[:, :, :]), r=["cmf"], w=["cmb"])
        P.op("dve", lambda e: e.memset(one_col[:, :], 1.0), w=["one_col"])
        cx.ident = cmb[:, 0, :]
        cx.identf = cmf[:, 0, :]
        cx.triu = cmf[:, 1, :]
        cx.strictl = cmf[:, 2, :]
        cx.maskji = cmf[:, 3, :]
        cx.ones1 = cmb[:, 4, :]
        cx.one_col = one_col
        cx.antiid = cmf[:, 5, :]
        P.barrier()

        h = cx.d["xT"]
        for stg in stages:
            if stg.startswith("ffn"):
                ffn_pass(P, cx, int(stg[3:]), h, hA)
                h = hA
            elif stg.startswith("gla"):
                gla_pass(P, cx, int(stg[3:]), h, hA)
                h = hA
            elif stg == "kv":
                kv_pass(P, cx, h)
            elif stg.startswith("nsa"):
                nsa_pass(P, cx, int(stg[3:]), h, hA)
                h = hA
            elif stg == "final":
                final_pass(P, cx, h, outT)
        P.barrier()
        P.emit(block, sems, dsems)
    cx.P = P
    return nc, cx


def prep_common(inp):
    f = np.float32

    def featT(v):
        v = np.asarray(v, f)
        return np.ascontiguousarray(v.reshape(v.shape[:-1] + (8, 128)).swapaxes(-1, -2))

    m = {}
    m["norm_mixT"] = featT(inp["norm_mix"])
    m["norm_memT"] = featT(inp["norm_mem"])
    m["norm_ffnT"] = featT(inp["norm_ffn"])
    m["final_normT"] = featT(inp["final_norm"])
    m["w_up"] = np.ascontiguousarray(np.asarray(inp["w_up"], f))
    m["w_down"] = np.ascontiguousarray(np.asarray(inp["w_down"], f))
    cwv = np.asarray(inp["conv_w"], f)
    m["conv_wT"] = np.ascontiguousarray(cwv.reshape(DEPTH, 3, 44, 128).transpose(0, 3, 2, 1))
    cbv = np.asarray(inp["conv_b"], f)
    m["conv_bT"] = np.ascontiguousarray(cbv.reshape(DEPTH, 44, 128).transpose(0, 2, 1))
    m["ones"] = np.full((128, 128), 1.0 / D, f)
    for k in ("w_mem_kv", "w_out", "gla_w_in", "gla_w_gate_up", "gla_b_gate"):
        m[k] = np.ascontiguousarray(np.asarray(inp[k], f))
    m["gla_out_norm_rep"] = np.ascontiguousarray(np.broadcast_to(np.asarray(inp["gla_out_norm"], f)[:, None, :], (2, 128, 192)))
    jj, ii = np.meshgrid(np.arange(128), np.arange(128), indexing="ij")
    cm = np.zeros((6, 128, 128), f)
    cm[0] = np.eye(128)
    cm[1] = (jj <= ii) * (-1.0 / 16.0)
    cm[2] = (jj > ii) * (-1.0 / 16.0)
    cm[3] = (jj <= ii) * 1.0
    cm[4] = 1.0
    cm[5] = np.eye(128)[::-1]
    m["cmat"] = cm
    for k in ("nsa_w_in", "w_kv_shared", "cmp_w1", "cmp_w2", "cmp_b2", "rel_bias"):
        m[k] = np.ascontiguousarray(np.asarray(inp[k], f))
    m["kv_normT"] = featT(inp["kv_norm"])
    m["cmp_posT"] = np.ascontiguousarray(np.asarray(inp["cmp_pos"], f).transpose(0, 2, 1))
    m["cmp_b1T"] = np.ascontiguousarray(np.asarray(inp["cmp_b1"], f).T)
    m["cmp_b2T"] = np.ascontiguousarray(np.asarray(inp["cmp_b2"], f).T)
    pidx = np.arange(128)[:, None] % 64
    m["ind"] = (np.arange(S)[None, :] // 64 == pidx).astype(f)
    kk = (np.arange(2)[None, :, None] * 128 + np.arange(128)[:, None, None])
    jb = np.arange(64)[None, None, :]
    ov = ((16 * kk < 64 * jb + 64) & (16 * kk + 31 >= 64 * jb) & (kk < 255)).astype(f)
    m["ovl"] = np.ascontiguousarray(ov.reshape(128, 128))
    cb = np.arange(64)[:, None]
    jj2 = np.arange(64)[None, :]
    forced = (jj2 == 0) | (jj2 == cb) | (jj2 == cb - 1)
    valid = (jj2 <= cb) & ~forced
    add = np.where(forced, 1e4, np.where(jj2 <= cb, 0.0, -1.0))
    sct = np.stack([valid.astype(f), add.astype(f)], axis=1).reshape(16, 1, 4, 2, 64)
    m["sct4b"] = np.ascontiguousarray(np.broadcast_to(sct, (16, 64, 4, 2, 64)))
    dd = np.arange(NSA_U) - 127
    dcl = np.maximum(dd, 0)
    large = 16 + (np.log(np.maximum(dcl, 1).astype(f) / f(16)) / f(np.log(128 / 16)) * f(16)).astype(np.int32)
    bucket = np.where(dcl < 16, dcl, np.minimum(large, 31))
    near = (dd >= 0) & (dd < 113)
    oh = np.zeros((33, NSA_U), f)
    for b in range(31):
        oh[b] = (near & (bucket == b))
    oh[31] = -1.0 * near
    oh[32] = ((dd < 0) | (dd >= 512))
    assert not (near & (bucket == 31)).any()
    m["oh_u"] = oh
    return m


def prep_inputs(inp, b, common):
    f = np.float32
    m = dict(common)
    m["xT"] = np.ascontiguousarray(np.asarray(inp["x"][b], f).T)
    m["memT"] = np.ascontiguousarray(np.asarray(inp["mem"][b], f).T)
    return m


def run(inp, stages, ncores=8):
    nc, cx = build(stages)
    common = prep_common(inp)
    per_b = [prep_inputs(inp, b, common) for b in range(4)]
    in_maps = [per_b[c // 2] for c in range(ncores)]
    res = run_bass_kernel_spmd(nc, in_maps, core_ids=list(range(ncores)))
    out = np.stack([np.ascontiguousarray(res.results[2 * b]["outT"].T) for b in range((ncores + 1) // 2)], axis=0)
    return out.astype(np.float32)


FULL = ("gla0", "ffn0", "gla1", "ffn1", "kv", "nsa2", "ffn2", "nsa3", "ffn3", "final")


def kernel(**inputs):
    return run(inputs, FULL)
```

```python
import numpy as np
import concourse.bass as bass
import concourse.mybir as mybir
from concourse.bass_utils import run_bass_kernel_spmd
from contextlib import ExitStack

F32 = mybir.dt.float32
BF16 = mybir.dt.bfloat16
AF = mybir.ActivationFunctionType
ALU = mybir.AluOpType
AX = mybir.AxisListType

D = 1024
S = 4096
DEPTH = 4
FFN = 2816
MEM = 256
EPS = 1e-6
NDSEM = 24
SAME_ENGINE_SYNC = True


class Op:
    __slots__ = ("eng", "fn", "dma", "idx", "waits", "signal", "sigval", "dsem", "dval")


class Prog:
    ENGS = ("pe", "act", "dve", "pool", "sync")

    def __init__(self, nc):
        self.nc = nc
        self.ops = {e: [] for e in self.ENGS}
        self.last_w = {}
        self.readers = {}
        self.waited_c = {e: {x: -1 for x in self.ENGS} for e in self.ENGS}
        self.waited_d = {e: {} for e in self.ENGS}
        self.ndma = 0
        self.dma_since_barrier = {}
        self.out_dmas = []

    def _add_dep(self, o, d):
        if d is None or d is o:
            return
        e = o.eng
        if d.dma:
            if self.waited_d[e].get(d.dsem, 0) >= d.dval:
                return
            self.waited_d[e][d.dsem] = d.dval
            o.waits.append(("d", d.dsem, d.dval))
            return
        if d.eng == e:
            if e == "pe" or not SAME_ENGINE_SYNC:
                return
        if self.waited_c[e][d.eng] >= d.idx:
            return
        self.waited_c[e][d.eng] = d.idx
        d.signal = True
        o.waits.append(("c", d.eng, d))

    def op(self, eng, fn, r=(), w=(), dma=False):
        o = Op()
        o.eng = eng
        o.fn = fn
        o.dma = dma
        o.idx = len(self.ops[eng])
        o.waits = []
        o.signal = False
        o.sigval = 0
        o.dsem = o.dval = None
        if dma:
            n = self.ndma
            self.ndma += 1
            o.dsem = n % NDSEM
            o.dval = 16 * (n // NDSEM + 1)
            if n >= NDSEM:
                prev = 16 * (n // NDSEM)
                if self.waited_d[eng].get(o.dsem, 0) < prev:
                    self.waited_d[eng][o.dsem] = prev
                    o.waits.append(("d", o.dsem, prev))
            self.dma_since_barrier[o.dsem] = o
        for b in r:
            self._add_dep(o, self.last_w.get(b))
            if isinstance(b, tuple) and b[0] == "ps":
                for t in self.readers.get(b, ()):
                    if t.eng != eng:
                        self._add_dep(o, t)
        for b in w:
            self._add_dep(o, self.last_w.get(b))
            for t in self.readers.get(b, ()):
                self._add_dep(o, t)
        for b in r:
            self.readers.setdefault(b, []).append(o)
        for b in w:
            self.last_w[b] = o
            self.readers[b] = []
        self.ops[eng].append(o)
        return o

    def dma(self, eng, out, in_, r=(), w=()):
        return self.op(eng, lambda e: e.dma_start(out=out, in_=in_), r=r, w=w, dma=True)

    def barrier(self):
        lasts = {}
        for e in self.ENGS:
            real = [o for o in self.ops[e][-64:] if o.fn is not None]
            if not real:
                real = [o for o in self.ops[e] if o.fn is not None]
            lasts[e] = real[-1] if real else None
        dmas = list(self.dma_since_barrier.values())
        for e in self.ENGS:
            o = self.op(e, None)
            for x in self.ENGS:
                d = lasts[x]
                if d is not None and x != e and not d.dma:
                    self._add_dep(o, d)
                elif d is not None and d.dma:
                    self._add_dep(o, d)
            for d in dmas:
                self._add_dep(o, d)
        self.dma_since_barrier = {}
        self.last_w = {}
        self.readers = {}

    def emit(self, block, sems, dsems):
        nc = self.nc
        for e in self.ENGS:
            c = 0
            for o in self.ops[e]:
                if o.signal:
                    c += 1
                    o.sigval = c
        self.sig_counts = {e: sum(1 for o in self.ops[e] if o.signal) for e in self.ENGS}

        def run(ename, eng):
            for o in self.ops[ename]:
                for wt in o.waits:
                    if wt[0] == "d":
                        eng.wait_ge(dsems[wt[1]], wt[2])
                    else:
                        eng.wait_ge(sems[wt[1]], wt[2].sigval)
                if o.fn is None:
                    continue
                ins = o.fn(eng)
                if o.dma:
                    ins.then_inc(dsems[o.dsem], 16)
                elif o.signal:
                    ins.then_inc(sems[ename], 1)

        @block.tensor
        def _(eng):
            run("pe", eng)

        @block.scalar
        def _(eng):
            run("act", eng)

        @block.vector
        def _(eng):
            run("dve", eng)

        @block.gpsimd
        def _(eng):
            run("pool", eng)

        @block.sync
        def _(eng):
            run("sync", eng)


class Arena:
    def __init__(self, tens, nwords):
        self.t = tens
        self.n = nwords
        self.off = 0

    def reset(self):
        self.off = 0

    def f32(self, shape):
        n = int(np.prod(shape[1:]))
        assert self.off + n <= self.n, ("arena overflow", self.off, n, self.n)
        ap = self.t[0:shape[0], self.off:self.off + n]
        self.off += n
        return _shape(ap, shape)

    def bf16(self, shape):
        n = int(np.prod(shape[1:]))
        nw = (n + 1) // 2
        assert self.off + nw <= self.n, ("arena overflow", self.off, nw, self.n)
        ap = self.t[0:shape[0], self.off:self.off + nw].bitcast(BF16)
        if 2 * nw != n:
            ap = ap[:, 0:n]
        self.off += nw
        return _shape(ap, shape)


def _shape(ap, shape):
    if len(shape) == 2:
        return ap
    if len(shape) == 3:
        return ap.rearrange("p (a b) -> p a b", a=shape[1], b=shape[2])
    if len(shape) == 4:
        return ap.rearrange("p (a b c) -> p a b c", a=shape[1], b=shape[2], c=shape[3])
    raise ValueError(shape)


class Ctx:
    pass


def load_cast(P, cx, dst, src, key, shape, cast_i):
    sb = cast_i % 2
    n = int(np.prod(shape[1:]))
    stg = _shape(cx.stg[sb][0:shape[0], 0:n], shape)
    P.dma("sync", stg, src, w=[("stg", sb)])
    eng = ("dve", "pool", "act")[cast_i % 3]
    if eng == "act":
        P.op("act", lambda e: e.copy(out=dst, in_=stg), r=[("stg", sb)], w=[key])
    else:
        P.op(eng, lambda e: e.tensor_copy(out=dst, in_=stg), r=[("stg", sb)], w=[key])


def rms_chunk(P, cx, Hc, hkey, gcol, xnT, xkey, TC, pool_share=False):
    ps = cx.ps[0][:, 0:TC]
    for kc in range(8):
        sq = cx.sq[kc % 2][:, 0:TC]
        P.op("act", lambda e, sq=sq, kc=kc: e.activation(out=sq, in_=Hc[:, kc, :], func=AF.Square),
             r=[hkey], w=[("sq", kc % 2)])
        P.op("pe", lambda e, sq=sq, kc=kc: e.matmul(ps, lhsT=cx.ones_bf[:, :], rhs=sq, start=(kc == 0), stop=(kc == 7)),
             r=[("sq", kc % 2)], w=[("ps", 0)])
    rstd = cx.rstd[:, 0:TC]
    P.op("dve", lambda e: e.tensor_scalar(out=rstd, in0=ps, scalar1=EPS, scalar2=None, op0=ALU.add),
         r=[("ps", 0)], w=["rstd"])
    P.op("act", lambda e: e.activation(out=rstd, in_=rstd, func=AF.Sqrt), r=["rstd"], w=["rstd"])
    P.op("dve", lambda e: e.reciprocal(out=rstd, in_=rstd), r=["rstd"], w=["rstd"])
    for kc in range(8):
        eng = "pool" if (pool_share and kc % 2 == 1) else "dve"
        P.op(eng, lambda e, kc=kc: e.scalar_tensor_tensor(out=xnT[:, kc, :], in0=Hc[:, kc, :], scalar=gcol[:, kc:kc + 1],
                                                          in1=rstd, op0=ALU.mult, op1=ALU.mult),
             r=[hkey, "rstd", "gn"], w=[(xkey, kc)])


def MM(P, out, lhsT, rhs, start, stop, r, w, **kw):
    return P.op("pe", lambda e: e.matmul(out, lhsT=lhsT, rhs=rhs, start=start, stop=stop, **kw), r=r, w=w)


def ACT(P, out, in_, func, r, w, **kw):
    return P.op("act", lambda e: e.activation(out=out, in_=in_, func=func, **kw), r=r, w=w)


def TT(P, eng, out, in0, in1, op, r, w):
    return P.op(eng, lambda e: e.tensor_tensor(out=out, in0=in0, in1=in1, op=op), r=r, w=w)


def STT(P, out, in0, scalar, in1, op0, op1, r, w):
    return P.op("dve", lambda e: e.scalar_tensor_tensor(out=out, in0=in0, scalar=scalar, in1=in1, op0=op0, op1=op1), r=r, w=w)


def TS(P, eng, out, in0, s1, s2, op0, op1, r, w):
    if s2 is None:
        return P.op(eng, lambda e: e.tensor_scalar(out=out, in0=in0, scalar1=s1, scalar2=None, op0=op0), r=r, w=w)
    return P.op(eng, lambda e: e.tensor_scalar(out=out, in0=in0, scalar1=s1, scalar2=s2, op0=op0, op1=op1), r=r, w=w)


def CP(P, eng, out, in_, r, w):
    if eng == "act":
        return P.op("act", lambda e: e.copy(out=out, in_=in_), r=r, w=w)
    return P.op(eng, lambda e: e.tensor_copy(out=out, in_=in_), r=r, w=w)


def MEMSET(P, eng, out, val, w):
    return P.op(eng, lambda e: e.memset(out, val), w=w)


def RECIP(P, out, in_, r, w):
    return P.op("dve", lambda e: e.reciprocal(out=out, in_=in_), r=r, w=w)


def ffn_pass(P, cx, l, h_in, h_out):
    TC = 512
    NCH = S // TC
    A = cx.arena
    A.reset()
    WU = A.bf16([128, 8, 2 * FFN])
    WD = A.bf16([128, 22, D])
    Hst = A.f32([128, 4096])
    cx.stg = [Hst[:, 0:2048], Hst[:, 2048:4096]]
    Hc = _shape(Hst, [128, 8, TC])
    hkey = "H"
    xnT = A.bf16([128, 8, TC])
    cx.sq = [A.bf16([128, TC]) for _ in range(2)]
    cx.rstd = A.f32([128, TC])
    U = [A.f32([128, TC + 2]) for _ in range(2)]
    T1 = [A.f32([128, TC]) for _ in range(3)]
    SA = [A.bf16([128, TC]) for _ in range(2)]
    Rb = [A.f32([128, TC]) for _ in range(2)]
    actT = A.bf16([128, 22, TC])
    HALO = A.f32([128, 44, 2])
    cw = A.f32([128, 44, 3])
    cb = A.f32([128, 44])
    gn = A.f32([128, 8])

    P.dma("sync", cw, cx.d["conv_wT"][l], w=["cw"])
    P.dma("sync", cb, cx.d["conv_bT"][l], w=["cb"])
    P.dma("sync", gn, cx.d["norm_ffnT"][l], w=["gn"])
    wu_src = cx.d["w_up"][l].rearrange("(kc p) c -> p kc c", p=128)
    ci = 0
    for c0 in range(0, 2 * FFN, 256):
        load_cast(P, cx, WU[:, :, c0:c0 + 256], wu_src[:, :, c0:c0 + 256], "WU", [128, 8, 256], ci)
        ci += 1
    wd_src = cx.d["w_down"][l].rearrange("(fc p) n -> p fc n", p=128)
    for f0 in range(0, 22, 2):
        load_cast(P, cx, WD[:, f0:f0 + 2, :], wd_src[:, f0:f0 + 2, :], "WD", [128, 2, D], ci)
        ci += 1
    P.op("pool", lambda e: e.memset(HALO, 0.0), w=["halo"])

    hin_v = h_in.rearrange("(kc p) t -> p kc t", p=128)
    hout_v = h_out.rearrange("(kc p) t -> p kc t", p=128)
    P.dma("sync", Hc, hin_v[:, :, 0:TC], w=[hkey, ("stg", 0), ("stg", 1)])
    nu = 0
    for ch in range(NCH):
        t0 = ch * TC
        rms_chunk(P, cx, Hc, hkey, gn, xnT, "xnT", TC)
        if ch + 1 < NCH:
            P.dma("sync", Hc, hin_v[:, :, t0 + TC:t0 + 2 * TC], w=[hkey])
        P.dma("sync", Rb[0], hin_v[:, 0, t0:t0 + TC], w=[("R", 0)])
        for fc in range(22):
            tt = []
            for half in range(2):
                cc = fc + 22 * half
                pi = nu % 4
                ui = nu % 2
                ti = nu % 3
                nu += 1
                pst = cx.ps[1 + pi][:, 0:TC]
                pk = ("ps", 1 + pi)
                for kc in range(8):
                    MM(P, pst, WU[:, kc, cc * 128:(cc + 1) * 128], xnT[:, kc, :], kc == 0, kc == 7, r=[("xnT", kc), "WU"], w=[pk])
                Ut = U[ui]
                uk = ("U", ui)
                t1 = T1[ti]
                tk = ("T1", ti)
                CP(P, "pool", Ut[:, 0:2], HALO[:, cc, :], r=["halo"], w=[uk])
                CP(P, "act", Ut[:, 2:TC + 2], pst, r=[pk], w=[uk])
                ACT(P, t1, pst, AF.Identity, r=[pk, "cw", "cb"], w=[tk], bias=cb[:, cc:cc + 1], scale=cw[:, cc, 2:3])
                STT(P, t1, Ut[:, 1:TC + 1], cw[:, cc, 1:2], t1, ALU.mult, ALU.add, r=[uk, tk, "cw"], w=[tk])
                STT(P, t1, Ut[:, 0:TC], cw[:, cc, 0:1], t1, ALU.mult, ALU.add, r=[uk, tk, "cw"], w=[tk])
                CP(P, "pool", HALO[:, cc, :], Ut[:, TC:TC + 2], r=[uk], w=["halo"])
                tt.append((t1, tk))
            sa = SA[fc % 2]
            sk = ("SA", fc % 2)
            ACT(P, sa, tt[0][0], AF.Silu, r=[tt[0][1]], w=[sk])
            TT(P, "pool", actT[:, fc, :], sa, tt[1][0], ALU.mult, r=[sk, tt[1][1]], w=[("actT", fc)])
        for n in range(8):
            pd = cx.ps[5 + n % 2][:, 0:TC]
            pk = ("ps", 5 + n % 2)
            for fc in range(22):
                MM(P, pd, WD[:, fc, n * 128:(n + 1) * 128], actT[:, fc, :], fc == 0, fc == 21, r=[("actT", fc), "WD"], w=[pk])
            if n + 1 < 8:
                P.dma("sync", Rb[(n + 1) % 2], hin_v[:, n + 1, t0:t0 + TC], w=[("R", (n + 1) % 2)])
            rb = Rb[n % 2]
            TT(P, "dve", rb, rb, pd, ALU.add, r=[pk, ("R", n % 2)], w=[("R", n % 2)])
            P.dma("sync", hout_v[:, n, t0:t0 + TC], rb, r=[("R", n % 2)])
    P.barrier()


class PsPool:
    def __init__(self, cx, banks):
        self.cx = cx
        self.banks = banks
        self.i = 0

    def get(self):
        b = self.banks[self.i % len(self.banks)]
        self.i += 1
        return self.cx.ps[b], ("ps", b)


def mem_setup(P, cx, l, WMK, xmT, KmT, Vm, gm, Hstage):
    pp = PsPool(cx, [1, 2, 3])
    P.dma("sync", gm, cx.d["norm_memT"][l], w=["gn"])
    src = cx.d["w_mem_kv"][l].rearrange("(kc p) c -> p kc c", p=128)
    for i, c0 in enumerate(range(0, 512, 256)):
        load_cast(P, cx, WMK[:, :, c0:c0 + 256], src[:, :, c0:c0 + 256], "WMK", [128, 8, 256], i)
    Hm = _shape(Hstage[:, 0:8 * MEM], [128, 8, MEM])
    P.dma("sync", Hm, cx.d["memT"].rearrange("(kc p) t -> p kc t", p=128), w=[("stg", 0)])
    rms_chunk(P, cx, Hm, ("stg", 0), gm, xmT, "xmT", MEM)
    for h in range(4):
        ps, pk = pp.get()
        for kc in range(8):
            MM(P, ps[0:64, 0:MEM], WMK[:, kc, h * 64:(h + 1) * 64], xmT[:, kc, :], kc == 0, kc == 7,
               r=[("xmT", kc), "WMK"], w=[pk])
        CP(P, "act", KmT[0:64, h, :], ps[0:64, 0:MEM], r=[pk], w=["KmT"])
    for mt in range(2):
        ps, pk = pp.get()
        for kc in range(8):
            MM(P, ps[:, 0:256], xmT[:, kc, mt * 128:(mt + 1) * 128], WMK[:, kc, 256:512], kc == 0, kc == 7,
               r=[("xmT", kc), "WMK"], w=[pk])
        CP(P, "dve", Vm[:, mt, :], ps[:, 0:256], r=[pk], w=["Vm"])


def mem_attn_chunk(P, cx, pp, MQ, KmT, Vm, PT, Rb, MIXT, TC):
    for h in range(4):
        po = (h % 2) * 64
        for mt in range(2):
            ps, pk = pp.get()
            MM(P, ps[:, 0:TC], KmT[0:64, h, mt * 128:(mt + 1) * 128], MQ[0:64, h, :], True, True,
               r=["KmT", ("MQ", h)], w=[pk])
            ACT(P, PT[mt][:, 0:TC], ps[:, 0:TC], AF.Exp, r=[pk], w=[("PT", mt)], scale=0.125)
        pso = cx.ps[4]
        pss = cx.ps[5]
        for mt in range(2):
            MM(P, pso[po:po + 64, 0:TC], Vm[:, mt, h * 64:(h + 1) * 64], PT[mt][:, 0:TC], mt == 0, mt == 1,
               r=["Vm", ("PT", mt)], w=[("ps", 4)])
        for mt in range(2):
            MM(P, pss[:, 0:TC], cx.ones1[:, :], PT[mt][:, 0:TC], mt == 0, mt == 1,
               r=[("PT", mt)], w=[("ps", 5)])
        RECIP(P, Rb[:, 0:TC], pss[:, 0:TC], r=[("ps", 5)], w=["Rb"])
        TT(P, "dve", MIXT[po:po + 64, 6 + h // 2, :], pso[po:po + 64, 0:TC], Rb[po:po + 64, 0:TC], ALU.mult,
           r=[("ps", 4), "Rb"], w=[("MIXT", 6 + h // 2)])


def out_proj_chunk(P, cx, pp, WO, MIXT, Hc, hkey, TC):
    for n in range(8):
        ps, pk = pp.get()
        for c in range(8):
            MM(P, ps[:, 0:TC], WO[:, c, n * 128:(n + 1) * 128], MIXT[:, c, :], c == 0, c == 7,
               r=[("MIXT", c), "WO"], w=[pk])
        TT(P, "dve", Hc[:, n, :], Hc[:, n, :], ps[:, 0:TC], ALU.add, r=[pk, hkey], w=[hkey])


def gla_pass(P, cx, l, h_in, h_out):
    TC = 512
    NCH = S // TC
    A = cx.arena
    A.reset()
    WIN = A.bf16([128, 8, 2576])
    WO = A.bf16([128, 8, D])
    Hst = A.f32([128, 4096])
    cx.stg = [Hst[:, 0:2048], Hst[:, 2048:4096]]
    Hc = _shape(Hst, [128, 8, TC])
    hkey = "H"
    xnT = A.bf16([128, 8, TC])
    cx.sq = [A.bf16([128, TC]) for _ in range(2)]
    cx.rstd = A.f32([128, TC])
    gn = A.f32([128, 8])
    gm = A.f32([128, 8])
    LR = A.bf16([32, TC])
    WGf = A.f32([32, 384])
    WGa = A.bf16([32, 384])
    LA = A.f32([128, 4, 384])
    E1 = A.f32([128, 384])
    EQ = A.f32([96, 4, TC])
    EK = A.f32([96, 4, TC])
    EKO = A.f32([128, 4, 384])
    QIN = A.bf16([96, 4, TC])
    KDEC = A.bf16([96, 4, TC])
    V = A.bf16([128, 4, 768])
    GS = A.f32([128, 768])
    GG = A.bf16([128, 4, 768])
    KOUT = A.bf16([128, 4, 384])
    MQ = A.bf16([64, 4, TC])
    Sst = A.f32([96, 4, 192])
    Sbf = A.bf16([96, 4, 192])
    ATm = [A.bf16([128, 128]) for _ in range(2)]
    MAIN = [A.bf16([128, 768]) for _ in range(2)]
    MIXT = A.bf16([128, 8, TC])
    PT = [A.bf16([128, TC]) for _ in range(2)]
    Rb = A.f32([128, TC])
    KmT = A.bf16([64, 4, MEM])
    Vm = A.bf16([128, 2, MEM])
    ON = A.f32([128, 192])
    SS = A.f32([128, 4])
    JUNK = A.f32([128, 192])
    xmT = A.bf16([128, 8, MEM])
    WMK = A.bf16([128, 8, 512])

    pp = PsPool(cx, [1, 2, 3])
    P.dma("sync", gn, cx.d["norm_mixT"][l], w=["gn"])
    mem_setup(P, cx, l, WMK, xmT, KmT, Vm, gm, Hst)
    P.dma("sync", gn, cx.d["norm_mixT"][l], w=["gn"])
    P.dma("sync", ON, cx.d["gla_out_norm_rep"][l], w=["ON"])
    src = cx.d["gla_w_in"][l].rearrange("(kc p) c -> p kc c", p=128)
    ci = 0
    for c0 in range(0, 2576, 256):
        c1 = min(2576, c0 + 256)
        load_cast(P, cx, WIN[:, :, c0:c1], src[:, :, c0:c1], "WIN", [128, 8, c1 - c0], ci)
        ci += 1
    src = cx.d["w_out"][l].rearrange("(kc p) c -> p kc c", p=128)
    for c0 in range(0, D, 256):
        load_cast(P, cx, WO[:, :, c0:c0 + 256], src[:, :, c0:c0 + 256], "WO", [128, 8, 256], ci)
        ci += 1
    MEMSET(P, "dve", WGf, 0.0, w=["WGf"])
    P.dma("sync", WGf[0:16, :], cx.d["gla_w_gate_up"][l], w=["WGf"])
    P.dma("sync", WGf[16:17, :], cx.d["gla_b_gate"][l:l + 1, :], w=["WGf"])
    CP(P, "dve", WGa, WGf, r=["WGf"], w=["WGa"])
    MEMSET(P, "dve", LR, 1.0, w=["LR"])
    MEMSET(P, "dve", Sst, 0.0, w=["S"])
    MEMSET(P, "dve", Sbf, 0.0, w=["Sbf"])

    hin_v = h_in.rearrange("(kc p) t -> p kc t", p=128)
    hout_v = h_out.rearrange("(kc p) t -> p kc t", p=128)
    for ch in range(NCH):
        t0 = ch * TC
        P.dma("sync", Hc, hin_v[:, :, t0:t0 + TC], w=[hkey, ("stg", 0), ("stg", 1)])
        rms_chunk(P, cx, Hc, hkey, gn, xnT, "xnT", TC)
        xr = [("xnT", kc) for kc in range(8)]
        ps, pk = pp.get()
        for kc in range(8):
            MM(P, ps[0:16, 0:TC], WIN[:, kc, 2304:2320], xnT[:, kc, :], kc == 0, kc == 7, r=[("xnT", kc), "WIN"], w=[pk])
        CP(P, "act", LR[0:16, :], ps[0:16, 0:TC], r=[pk], w=["LR"])
        for s in range(4):
            ts = slice(s * 128, (s + 1) * 128)
            ps, pk = pp.get()
            MM(P, ps[:, 0:384], LR[0:32, ts], WGa[0:32, :], True, True, r=["LR", "WGa"], w=[pk])
            ACT(P, E1, ps[:, 0:384], AF.Exp, r=[pk], w=["E1"], scale=-1.0)
            ACT(P, LA[:, s, :], E1, AF.Ln, r=["E1"], w=[("LA", s)], bias=cx.one_col[:, 0:1])
            psb = cx.ps[6]
            for h in range(4):
                MM(P, psb[0:96, h * 128:(h + 1) * 128], LA[:, s, h * 96:(h + 1) * 96], cx.triu[:, :], True, True,
                   r=[("LA", s)], w=[("ps", 6)])
            ACT(P, EQ[:, :, ts], psb[0:96, :].rearrange("p (h t) -> p h t", h=4), AF.Exp, r=[("ps", 6)], w=[("EQ", s)])
            ACT(P, EK[:, :, ts], psb[0:96, :].rearrange("p (h t) -> p h t", h=4), AF.Exp, r=[("ps", 6)], w=[("EK", s)], scale=-1.0)
            psl = cx.ps[7]
            MM(P, psl[:, 0:384], cx.strictl[:, :], LA[:, s, :], True, True, r=[("LA", s)], w=[("ps", 7)])
            ACT(P, EKO[:, s, :], psl[:, 0:384], AF.Exp, r=[("ps", 7)], w=[("EKO", s)])
        eqr = [("EQ", s) for s in range(4)]
        ekr = [("EK", s) for s in range(4)]
        for h in range(4):
            ps, pk = pp.get()
            for kc in range(8):
                MM(P, ps[0:96, 0:TC], WIN[:, kc, h * 96:(h + 1) * 96], xnT[:, kc, :], kc == 0, kc == 7, r=[("xnT", kc), "WIN"], w=[pk])
            STT(P, QIN[:, h, :], ps[0:96, 0:TC], float(96 ** -0.5), EQ[:, h, :], ALU.mult, ALU.mult, r=[pk] + eqr, w=[("QIN", h)])
            ps, pk = pp.get()
            for kc in range(8):
                MM(P, ps[0:96, 0:TC], WIN[:, kc, 384 + h * 96:384 + (h + 1) * 96], xnT[:, kc, :], kc == 0, kc == 7, r=[("xnT", kc), "WIN"], w=[pk])
            TT(P, "dve", KDEC[:, h, :], ps[0:96, 0:TC], EK[:, h, :], ALU.mult, r=[pk] + ekr, w=[("KDEC", h)])
            ps, pk = pp.get()
            for kc in range(8):
                MM(P, ps[0:64, 0:TC], WIN[:, kc, 2320 + h * 64:2320 + (h + 1) * 64], xnT[:, kc, :], kc == 0, kc == 7, r=[("xnT", kc), "WIN"], w=[pk])
            CP(P, "act", MQ[:, h, :], ps[0:64, 0:TC], r=[pk], w=[("MQ", h)])
        for s in range(4):
            ts = slice(s * 128, (s + 1) * 128)
            for half in range(2):
                ps, pk = pp.get()
                for kc in range(8):
                    MM(P, ps[:, 0:384], xnT[:, kc, ts], WIN[:, kc, 768 + half * 384:768 + (half + 1) * 384], kc == 0, kc == 7,
                       r=[("xnT", kc), "WIN"], w=[pk])
                CP(P, "act", V[:, s, half * 384:(half + 1) * 384], ps[:, 0:384], r=[pk], w=[("V", s)])
            for half in range(2):
                ps, pk = pp.get()
                for kc in range(8):
                    MM(P, ps[:, 0:384], xnT[:, kc, ts], WIN[:, kc, 1536 + half * 384:1536 + (half + 1) * 384], kc == 0, kc == 7,
                       r=[("xnT", kc), "WIN"], w=[pk])
                ACT(P, GS[:, half * 384:(half + 1) * 384], ps[:, 0:384], AF.Silu, r=[pk], w=[("GS", half)])
            TT(P, "pool", GG[:, s, :].rearrange("p (h v) -> p h v", h=4), GS.rearrange("p (h v) -> p h v", h=4),
               ON[:, None, :].to_broadcast([128, 4, 192]), ALU.mult, r=[("GS", 0), ("GS", 1), "ON"], w=[("GG", s)])
            ps, pk = pp.get()
            for kc in range(8):
                MM(P, ps[:, 0:384], xnT[:, kc, ts], WIN[:, kc, 384:768], kc == 0, kc == 7, r=[("xnT", kc), "WIN"], w=[pk])
            TT(P, "dve", KOUT[:, s, :], ps[:, 0:384], EKO[:, s, :], ALU.mult, r=[pk, ("EKO", s)], w=[("KOUT", s)])
            MEMSET(P, "dve", SS, 0.0, w=["SS"])
            for h in range(4):
                ps, pk = pp.get()
                MM(P, ps[:, 0:128], KDEC[0:96, h, ts], QIN[0:96, h, ts], True, True, r=[("KDEC", h), ("QIN", h)], w=[pk])
                at = ATm[h % 2]
                TT(P, "dve", at, ps[:, 0:128], cx.maskji[:, :], ALU.mult, r=[pk], w=[("ATm", h % 2)])
                pso = cx.ps[4 + h]
                ok = ("ps", 4 + h)
                osl = slice(0, 192)
                MM(P, pso[:, osl], at, V[:, s, h * 192:(h + 1) * 192], True, False, r=[("ATm", h % 2), ("V", s)], w=[ok])
                MM(P, pso[:, osl], QIN[0:96, h, ts], Sbf[0:96, h, :], False, True, r=[("QIN", h), "Sbf"], w=[ok])
                ps, pk = pp.get()
                MM(P, ps[0:96, 0:192], KOUT[:, s, h * 96:(h + 1) * 96], V[:, s, h * 192:(h + 1) * 192], True, True,
                   r=[("KOUT", s), ("V", s)], w=[pk])
                STT(P, Sst[:, h, :], Sst[:, h, :], EQ[:, h, s * 128 + 127:s * 128 + 128], ps[0:96, 0:192], ALU.mult, ALU.add,
                    r=[pk, "S", ("EQ", s)], w=["S"])
                CP(P, "act", Sbf[:, h, :], Sst[:, h, :], r=["S"], w=["Sbf"])
                ACT(P, JUNK, pso[:, osl], AF.Square, r=[ok], w=["JUNK", "SS"], accum_out=SS[:, h:h + 1])
            TS(P, "dve", SS, SS, 1.0 / 192, EPS, ALU.mult, ALU.add, r=["SS"], w=["SS"])
            ACT(P, SS, SS, AF.Sqrt, r=["SS"], w=["SS"])
            RECIP(P, SS, SS, r=["SS"], w=["SS"])
            mn = MAIN[s % 2]
            mk = ("MAIN", s % 2)
            for h in range(4):
                pso = cx.ps[4 + h]
                ok = ("ps", 4 + h)
                osl = slice(0, 192)
                STT(P, mn[:, h * 192:(h + 1) * 192], pso[:, osl], SS[:, h:h + 1], GG[:, s, h * 192:(h + 1) * 192], ALU.mult, ALU.mult,
                    r=[ok, "SS", ("GG", s)], w=[mk])
            for c in range(6):
                ps, pk = pp.get()
                MM(P, ps[:, 0:128], mn[:, c * 128:(c + 1) * 128], cx.ident[:, :], True, True, r=[mk], w=[pk])
                CP(P, "act" if c % 2 == 0 else "dve", MIXT[:, c, ts], ps[:, 0:128], r=[pk], w=[("MIXT", c)])
        mem_attn_chunk(P, cx, pp, MQ, KmT, Vm, PT, Rb, MIXT, TC)
        out_proj_chunk(P, cx, pp, WO, MIXT, Hc, hkey, TC)
        P.dma("sync", hout_v[:, :, t0:t0 + TC], Hc, r=[hkey])
    P.barrier()


def kv_pass(P, cx, h_in):
    TC = 512
    NCH = S // TC
    A = cx.arena
    A.reset()
    WKV = A.bf16([128, 8, 1536])
    Hst = A.f32([128, 4096])
    cx.stg = [Hst[:, 0:2048], Hst[:, 2048:4096]]
    Hc = _shape(Hst, [128, 8, TC])
    xnT = A.bf16([128, 8, TC])
    cx.sq = [A.bf16([128, TC]) for _ in range(2)]
    cx.rstd = A.f32([128, TC])
    gn = A.f32([128, 8])
    KT = A.bf16([128, 4, S])
    CF = A.bf16([128, 4, S])
    VS = A.bf16([128, 32, 4, 65])
    VW = A.bf16([128, 32, 4, 65])
    W1f = A.f32([128, 32 * 128])
    W1 = A.bf16([128, 32, 128])
    W2f = A.f32([128, 2, 64])
    W2 = A.bf16([128, 2, 64])
    POSf = A.f32([128, 32])
    POS = A.bf16([128, 32])
    B1 = A.f32([128, 2])
    B2c = A.f32([64, 2])
    B2rf = A.f32([1, 64])
    B2r = A.bf16([1, 64])
    CST = A.f32([128, 2])
    HID = A.bf16([128, 256])
    KC = A.bf16([64, 4, 256])
    VC = A.bf16([128, 2, 4, 65])
    pp = PsPool(cx, [1, 2, 3, 4, 5, 6, 7])

    P.dma("sync", gn, cx.d["kv_normT"], w=["gn"])
    src = cx.d["w_kv_shared"].rearrange("(kc p) c -> p kc c", p=128)
    for i, c0 in enumerate(range(0, 1536, 256)):
        load_cast(P, cx, WKV[:, :, c0:c0 + 256], src[:, :, c0:c0 + 256], "WKV", [128, 8, 256], i)
    MEMSET(P, "pool", VS, 1.0, w=["VS"])
    MEMSET(P, "pool", VW, 1.0, w=["VW"])
    hin_v = h_in.rearrange("(kc p) t -> p kc t", p=128)
    for ch in range(NCH):
        t0 = ch * TC
        P.dma("sync", Hc, hin_v[:, :, t0:t0 + TC], w=["H", ("stg", 0), ("stg", 1)])
        rms_chunk(P, cx, Hc, "H", gn, xnT, "xnT", TC)
        for (j, dst, po, key) in ((0, CF, 0, "CF"), (1, CF, 64, "CF"), (2, KT, 0, "KT"), (4, KT, 64, "KT")):
            for g in range(4):
                ps, pk = pp.get()
                c0 = j * 256 + g * 64
                for kc in range(8):
                    MM(P, ps[po:po + 64, 0:TC], WKV[:, kc, c0:c0 + 64], xnT[:, kc, :], kc == 0, kc == 7,
                       r=[("xnT", kc), "WKV"], w=[pk])
                CP(P, "act" if g % 2 == 0 else "dve", dst[po:po + 64, g, t0:t0 + TC], ps[po:po + 64, 0:TC], r=[pk], w=[key])
        for s4 in range(4):
            tile = ch * 4 + s4
            ts = slice(s4 * 128, (s4 + 1) * 128)
            for (j, dst, key) in ((3, VS, "VS"), (5, VW, "VW")):
                ps, pk = pp.get()
                for kc in range(8):
                    MM(P, ps[:, 0:256], xnT[:, kc, ts], WKV[:, kc, j * 256:(j + 1) * 256], kc == 0, kc == 7,
                       r=[("xnT", kc), "WKV"], w=[pk])
                CP(P, "act" if j == 3 else "dve", dst[:, tile, :, 0:64], ps[:, 0:256].rearrange("p (g d) -> p g d", g=4),
                   r=[pk], w=[key])
    P.dma("sync", W1f[0:64, :].rearrange("p (l n) -> p l n", l=32), cx.d["cmp_w1"][0].rearrange("(l d) n -> d l n", d=64), w=["W1f"])
    P.dma("sync", W1f[64:128, :].rearrange("p (l n) -> p l n", l=32), cx.d["cmp_w1"][1].rearrange("(l d) n -> d l n", d=64), w=["W1f"])
    CP(P, "dve", W1.rearrange("p l n -> p (l n)"), W1f, r=["W1f"], w=["W1"])
    P.dma("sync", W2f, cx.d["cmp_w2"].rearrange("j n d -> n j d"), w=["W2f"])
    CP(P, "dve", W2, W2f, r=["W2f"], w=["W2"])
    P.dma("sync", POSf[0:64, :], cx.d["cmp_posT"][0], w=["POSf"])
    P.dma("sync", POSf[64:128, :], cx.d["cmp_posT"][1], w=["POSf"])
    CP(P, "dve", POS, POSf, r=["POSf"], w=["POS"])
    P.dma("sync", B1, cx.d["cmp_b1T"], w=["B1"])
    P.dma("sync", B2c, cx.d["cmp_b2T"], w=["B2c"])
    P.dma("sync", B2rf, cx.d["cmp_b2"][1:2, :], w=["B2rf"])
    CP(P, "dve", B2r, B2rf, r=["B2rf"], w=["B2r"])
    MEMSET(P, "dve", KC, 0.0, w=["KC"])
    MEMSET(P, "dve", VC, 0.0, w=["VC"])
    MEMSET(P, "dve", HID, 0.0, w=["HID"])
    for j in range(2):
        po = j * 64
        ps, pk = pp.get()
        for l in range(32):
            MM(P, ps[:, 0:1], W1[po:po + 64, l, :], POS[po:po + 64, l:l + 1], l == 0, l == 31, r=["W1", "POS"], w=[pk])
        TT(P, "dve", CST[:, j:j + 1], ps[:, 0:1], B1[:, j:j + 1], ALU.add, r=[pk, "B1"], w=["CST"])
        for g in range(4):
            ps, pk = pp.get()
            for l in range(32):
                MM(P, ps[:, 0:255], W1[po:po + 64, l, :], CF[po:po + 64, g, l:l + 16 * 254 + 1:16], l == 0, l == 31,
                   r=["W1", "CF"], w=[pk])
            ACT(P, HID[:, 0:255], ps[:, 0:255], AF.Silu, r=[pk, "CST"], w=["HID"], bias=CST[:, j:j + 1])
            if j == 0:
                ps, pk = pp.get()
                MM(P, ps[0:64, 0:255], W2[:, 0, :], HID[:, 0:255], True, True, r=["W2", "HID"], w=[pk])
                ACT(P, KC[:, g, 0:255], ps[0:64, 0:255], AF.Identity, r=[pk, "B2c"], w=["KC"], bias=B2c[:, 0:1])
            else:
                for kt in range(2):
                    n = 128 if kt == 0 else 127
                    ps, pk = pp.get()
                    MM(P, ps[0:n, 0:64], HID[:, kt * 128:kt * 128 + n], W2[:, 1, :], True, False, r=["W2", "HID"], w=[pk])
                    MM(P, ps[0:n, 0:64], cx.ones1[0:1, 0:n], B2r[0:1, :], False, True, r=["B2r"], w=[pk])
                    CP(P, "dve", VC[0:n, kt, g, 0:64], ps[0:n, 0:64], r=[pk], w=["VC"])
    MEMSET(P, "dve", VC[:, :, :, 64:65], 1.0, w=["VC"])
    P.dma("sync", cx.d_KT, KT.rearrange("p g t -> p (g t)"), r=["KT"])
    P.dma("sync", cx.d_VS, VS.rearrange("p a g d -> p (a g d)"), r=["VS"])
    P.dma("sync", cx.d_VW, VW.rearrange("p a g d -> p (a g d)"), r=["VW"])
    P.dma("sync", cx.d_KC, KC.rearrange("p g t -> p (g t)"), r=["KC"])
    P.dma("sync", cx.d_VC, VC.rearrange("p a g d -> p (a g d)"), r=["VC"])
    P.barrier()


NSA_U = 768
import os
NSA_DBG = int(os.environ.get('NSA_DBG', '9'))


def nsa_pass(P, cx, l, h_in, h_out):
    li = l - 2
    TC = 256
    NCH = S // TC
    A = cx.arena
    A.reset()
    Hst = A.f32([128, 4096])
    cx.stg = [Hst[:, 0:2048], Hst[:, 2048:4096]]
    Hc = _shape(Hst[:, 0:2048], [128, 8, TC])
    hkey = ("stg", 0)
    cx.sq = [A.bf16([128, TC]) for _ in range(2)]
    cx.rstd = A.f32([128, TC])
    gn = A.f32([128, 8])
    gm = A.f32([128, 8])
    KmT = A.bf16([64, 4, MEM])
    Vm = A.bf16([128, 2, MEM])
    mark = A.off
    xmT = A.bf16([128, 8, MEM])
    WMK = A.bf16([128, 8, 512])
    pp = PsPool(cx, [1, 2])
    pp1 = PsPool(cx, [3])
    psall = cx.psall
    mem_setup(P, cx, l, WMK, xmT, KmT, Vm, gm, Hst)
    P.barrier()
    A.off = mark
    WIN = A.bf16([128, 8, 1060])
    WO = A.bf16([128, 8, D])
    xnT = A.bf16([128, 8, TC])
    KT = A.bf16([128, 4, S])
    IND = A.bf16([128, S])
    VS = A.bf16([128, 32, 4, 65])
    VW = A.bf16([128, 32, 4, 65])
    KcT = A.bf16([64, 4, 256])
    VC = A.bf16([128, 2, 4, 65])
    OV = A.bf16([128, 2, 64])
    CPT = [A.f32([128, 12, 128]) for _ in range(2)]
    WMP = A.f32([128, 128])
    Fn = A.f32([64, 12, 12])
    RB = A.f32([33, 12])
    OHs = Hst[0:33, 2048:2048 + NSA_U]
    GVs = Hst[0:12, 2816:2816 + NSA_U]
    QS = A.bf16([128, 12, TC])
    QMP = [A.bf16([64, 12, 128]) for _ in range(2)]
    OT = [A.f32([65, 384]) for _ in range(2)]
    GT = A.f32([64, 4, 36])
    MQ = A.bf16([64, 4, TC])
    EC = A.f32([64, 3, 256])
    PC = A.bf16([64, 3, 256])
    PCT = A.bf16([128, 2, 192])
    SUMC = A.f32([64, 4])
    SC = A.f32([64, 4, 64])
    SC2 = A.f32([64, 64])
    M8 = A.f32([64, 16])
    SCT4 = A.f32([64, 4, 2, 64])
    CMPC = [A.bf16([64, 2, 12, 64]) for _ in range(2)]
    SMpad = A.bf16([64, 4, 128])
    SB = [A.f32([128, 384]) for _ in range(2)]
    PTs = [A.bf16([128, 384]) for _ in range(3)]
    MAINQ = A.bf16([64, 2, 768])
    TMP = [A.f32([64, 2, 3, 64]) for _ in range(2)]
    WX = A.f32([64, 2, 3, 4])
    MIXT = A.bf16([128, 8, TC])
    PT = [A.bf16([128, TC]) for _ in range(2)]
    Rb = A.f32([128, TC])
    P.dma("sync", gn, cx.d["norm_mixT"][l], w=["gn"])
    src = cx.d["nsa_w_in"][li].rearrange("(kc p) c -> p kc c", p=128)
    ci = 0
    for c0 in range(0, 1060, 256):
        c1 = min(1060, c0 + 256)
        load_cast(P, cx, WIN[:, :, c0:c1], src[:, :, c0:c1], "WIN", [128, 8, c1 - c0], ci)
        ci += 1
    src = cx.d["w_out"][l].rearrange("(kc p) c -> p kc c", p=128)
    for c0 in range(0, D, 256):
        load_cast(P, cx, WO[:, :, c0:c0 + 256], src[:, :, c0:c0 + 256], "WO", [128, 8, 256], ci)
        ci += 1
    for c0 in range(0, S, 2048):
        load_cast(P, cx, IND[:, c0:c0 + 2048], cx.d["ind"][:, c0:c0 + 2048], "IND", [128, 2048], ci)
        ci += 1
    load_cast(P, cx, OV.rearrange("p a j -> p (a j)"), cx.d["ovl"], "OV", [128, 128], ci)
    ci += 1
    P.dma("sync", KT.rearrange("p g t -> p (g t)"), cx.d_KT, w=["KT"])
    P.dma("sync", VS.rearrange("p a g d -> p (a g d)"), cx.d_VS, w=["VS"])
    P.dma("sync", VW.rearrange("p a g d -> p (a g d)"), cx.d_VW, w=["VW"])
    P.dma("sync", VC.rearrange("p a g d -> p (a g d)"), cx.d_VC, w=["VC"])
    P.dma("sync", KcT.rearrange("p g t -> p (g t)"), cx.d_KC, w=["KcT"])
    MEMSET(P, "dve", RB, -30000.0, w=["RB"])
    P.dma("sync", RB[0:32, :], cx.d["rel_bias"], w=["RB"])
    P.dma("sync", OHs, cx.d["oh_u"], w=[("stg", 1)])
    for u0 in range(0, NSA_U, 384):
        ps, pk = pp.get()
        MM(P, ps[0:12, 0:384], RB[0:33, :], OHs[0:33, u0:u0 + 384], True, True, r=["RB", ("stg", 1)], w=[pk])
        CP(P, "dve", GVs[:, u0:u0 + 384], ps[0:12, 0:384], r=[pk], w=[("stg", 1)])
    gv_w = P.dma("sync", cx.d_GV, GVs, r=[("stg", 1)], w=["dGV"])
    XH = _shape(Hst[:, 0:768], [128, 12, 64])
    for mi, m in enumerate((0, 64, 128, 192, 512, 576)):
        src = bass.AP(tensor=cx.d_GV.tensor, offset=m, ap=[[1, 128], [NSA_U, 12], [1, 64]])
        P.dma("sync", XH, src, r=["dGV"], w=[("stg", 0)])
        for half in range(2):
            ps, pk = pp.get()
            MM(P, ps[:, 0:384], cx.antiid[:, :], XH.rearrange("p h i -> p (h i)")[:, half * 384:(half + 1) * 384], True, True,
               r=[("stg", 0)], w=[pk])
            if mi < 4:
                CP(P, "dve", CPT[mi // 2][:, 6 * half:6 * half + 6, (mi % 2) * 64:(mi % 2) * 64 + 64],
                   ps[:, 0:384].rearrange("p (h i) -> p h i", h=6), r=[pk], w=[("CPT", mi // 2)])
            elif half == 0:
                CP(P, "dve", WMP[:, (mi - 4) * 64:(mi - 4) * 64 + 64], ps[:, 0:64], r=[pk], w=["WMP"])
    for xq in range(12):
        src = bass.AP(tensor=cx.d_GV.tensor, offset=240 - 16 * xq, ap=[[1, 64], [NSA_U, 12], [1, 1]])
        P.op("sync", lambda e, xq=xq, src=src: e.dma_start(out=Fn[:, :, xq:xq + 1], in_=src, allow_slow_non_contiguous=True),
             r=["dGV"], w=["Fn"], dma=True)
    P.barrier()
    o2 = 2048
    def carve(shape, bf):
        nonlocal o2
        n = int(np.prod(shape[1:]))
        nw = (n + 1) // 2 if bf else n
        ap = Hst[0:shape[0], o2:o2 + nw]
        if bf:
            ap = ap.bitcast(BF16)
        o2 += nw
        return _shape(ap, shape)
    RB = dict(EC=carve([64, 3, 256], False), PC=carve([64, 3, 256], True), PCT=carve([128, 2, 192], True), SUMC=carve([64, 4], False),
              SC2=carve([64, 64], False), M8=carve([64, 16], False), SC=carve([64, 4, 64], False), SMpad=carve([64, 4, 128], True),
              pcs=psall[0:64, 1 * 512:1 * 512 + 768].rearrange("p (r k) -> p r k", r=3), ck=[("ps", 1), ("ps", 2)],
              poc=cx.ps[0], pock=("ps", 0), pool=PsPool(cx, [4]), tag="B")
    assert o2 <= 4096
    RA = dict(EC=EC, PC=PC, PCT=PCT, SUMC=SUMC, SC2=SC2, M8=M8, SC=SC, SMpad=SMpad,
              pcs=psall[0:64, 6 * 512:6 * 512 + 768].rearrange("p (r k) -> p r k", r=3), ck=[("ps", 6), ("ps", 7)],
              poc=cx.ps[5], pock=("ps", 5), pool=pp1, tag="A")
    for R in (RA, RB):
        MEMSET(P, "dve", R["PC"], 0.0, w=[("PC", R["tag"])])
        MEMSET(P, "dve", R["PCT"], 0.0, w=[("PCT", R["tag"])])
        MEMSET(P, "dve", R["SMpad"], 0.0, w=[("SM", R["tag"])])
        MEMSET(P, "dve", R["EC"], 0.0, w=[("EC", R["tag"])])

    hin_v = h_in.rearrange("(kc p) t -> p kc t", p=128)
    hout_v = h_out.rearrange("(kc p) t -> p kc t", p=128)
    for ch in range(NCH):
        t0 = ch * TC
        P.dma("sync", Hc, hin_v[:, :, t0:t0 + TC], w=[hkey])
        rms_chunk(P, cx, Hc, hkey, gn, xnT, "xnT", TC)
        for h in range(12):
            ps, pk = pp.get()
            for kc in range(8):
                MM(P, ps[0:64, 0:TC], WIN[:, kc, h * 64:(h + 1) * 64], xnT[:, kc, :], kc == 0, kc == 7, r=[("xnT", kc), "WIN"], w=[pk])
            ACT(P, QS[0:64, h, :], ps[0:64, 0:TC], AF.Identity, r=[pk], w=[("QS", h)], scale=0.125)
            CP(P, "dve", QS[64:128, h, :], QS[0:64, h, :], r=[("QS", h)], w=[("QSw", h)])
        for h in range(4):
            ps, pk = pp.get()
            for kc in range(8):
                MM(P, ps[0:64, 0:TC], WIN[:, kc, 804 + h * 64:804 + (h + 1) * 64], xnT[:, kc, :], kc == 0, kc == 7, r=[("xnT", kc), "WIN"], w=[pk])
            CP(P, "act", MQ[:, h, :], ps[0:64, 0:TC], r=[pk], w=[("MQ", h)])
        for cl in range(4):
            ps, pk = pp.get()
            for kc in range(8):
                MM(P, ps[0:64, 0:36], xnT[:, kc, cl * 64:(cl + 1) * 64], WIN[:, kc, 768:804], kc == 0, kc == 7, r=[("xnT", kc), "WIN"], w=[pk])
            ACT(P, GT[:, cl, :], ps[0:64, 0:36], AF.Sigmoid, r=[pk], w=[("GT", cl)])
        def phase1(c, cl, R):
            tsl = slice(cl * 64, (cl + 1) * 64)
            hf = c % 2
            pr = (c // 2) % 2
            ncol = min(255, 4 * c + 3)
            nkt = 1 if ncol <= 128 else 2
            sct = SCT4[:, cl, :, :]
            sctk = "SCT4"
            kn0, kn1 = max(0, 4 * c - 9), min(ncol, 4 * c + 3)
            x0 = kn0 - (4 * c - 9)
            qm = QMP[pr]
            EC, PC, PCT, SUMC, SC2, M8, SC, SMpad = R["EC"], R["PC"], R["PCT"], R["SUMC"], R["SC2"], R["M8"], R["SC"], R["SMpad"]
            tg = R["tag"]
            for g in range(4):
                hs = slice(3 * g, 3 * g + 3)
                qmk = ("QM", pr, g, hf)
                pcs = R["pcs"]
                ck = R["ck"]
                for r in range(3):
                    h = 3 * g + r
                    MM(P, pcs[:, r, 0:ncol], QS[0:64, h, tsl], KcT[0:64, g, 0:ncol], True, True, r=[("QS", h), "KcT"], w=ck)
                yield
                TT(P, "dve", pcs[:, :, kn0:kn1], pcs[:, :, kn0:kn1], Fn[:, hs, x0:x0 + (kn1 - kn0)], ALU.add, r=ck + ["Fn"], w=ck)
                yield
                ACT(P, EC[:, :, 0:ncol], pcs[:, :, 0:ncol], AF.Exp, r=ck, w=[("EC", tg)])
                yield
                P.op("dve", lambda e, ncol=ncol: e.reduce_sum(out=SUMC[:, 0:3], in_=EC[:, :, 0:ncol], axis=AX.X), r=[("EC", tg)], w=[("SUMC", tg)])
                yield
                if c == 0:
                    TS(P, "dve", SUMC[:, 0:3], SUMC[:, 0:3], 1e-30, None, ALU.max, None, r=[("SUMC", tg)], w=[("SUMC", tg)])
                    yield
                RECIP(P, SUMC[:, 0:3], SUMC[:, 0:3], r=[("SUMC", tg)], w=[("SUMC", tg)])
                yield
                TT(P, "dve", PC[:, :, 0:ncol], EC[:, :, 0:ncol], SUMC[:, 0:3, None].to_broadcast([64, 3, ncol]), ALU.mult,
                   r=[("EC", tg), ("SUMC", tg)], w=[("PC", tg)])
                yield
                ps, pk = R["pool"].get()
                for kt in range(nkt):
                    for r in range(3):
                        MM(P, ps[:, (kt * 3 + r) * 64:(kt * 3 + r + 1) * 64], PC[0:64, r, kt * 128:(kt + 1) * 128], cx.ident[0:64, 0:64],
                           True, True, r=[("PC", tg)], w=[pk])
                CP(P, "act", PCT[:, 0:nkt, :].rearrange("p a b -> p (a b)"), ps[:, 0:nkt * 192], r=[pk], w=[("PCT", tg)])
                yield
                poc = R["poc"]
                pock = R["pock"]
                for r in range(3):
                    for kt in range(nkt):
                        MM(P, poc[0:64, r * 65:(r + 1) * 65], PCT[:, kt, r * 64:(r + 1) * 64], VC[:, kt, g, :], kt == 0, kt == nkt - 1,
                           r=[("PCT", tg), "VC"], w=[pock])
                n = 0
                for r in range(3):
                    for kt in range(nkt):
                        MM(P, poc[0:64, 256:320], PCT[:, kt, r * 64:(r + 1) * 64], OV[:, kt, :], n == 0, n == 3 * nkt - 1,
                           r=[("PCT", tg), "OV"], w=[pock])
                        n += 1
                yield
                gv = GT[:, cl, 9 * g:9 * g + 9].rearrange("p (r x) -> p r x", r=3)[:, :, 0:1]
                ov = poc[0:64, 0:195].rearrange("p (r d) -> p r d", r=3)
                TT(P, "dve", CMPC[pr][:, hf, hs, :], ov[:, :, 0:64], gv.to_broadcast([64, 3, 64]), ALU.mult,
                   r=[pock, ("GT", cl)], w=[("CMPC", pr, g, hf)])
                yield
                TT(P, "dve", SC[:, g, :], poc[0:64, 256:320], sct[:, 1, :], ALU.add, r=[pock, sctk], w=[("SC", tg, g)])
                yield
                P.op("dve", lambda e, g=g: e.max(out=M8[:, 0:8], in_=SC[:, g, :]), r=[("SC", tg, g)], w=[("M8", tg)])
                yield
                P.op("dve", lambda e, g=g: e.match_replace(out=SC2, in_to_replace=M8[:, 0:8], in_values=SC[:, g, :], imm_value=-1e9),
                     r=[("M8", tg), ("SC", tg, g)], w=[("SC2", tg)])
                yield
                P.op("dve", lambda e: e.max(out=M8[:, 8:16], in_=SC2), r=[("SC2", tg)], w=[("M8", tg)])
                yield
                TS(P, "dve", SMpad[:, g, 64:128], SC[:, g, :], M8[:, 15:16], None, ALU.is_ge, None, r=[("SC", tg, g), ("M8", tg)], w=[("SM", tg)])
                yield
                ps, pk = R["pool"].get()
                MM(P, ps[0:64, 0:64], SMpad[0:64, g, 64:128], cx.ident[0:64, 0:64], True, True, r=[("SM", tg)], w=[pk])
                TS(P, "dve", qm[0:64, hs, hf * 64:(hf + 1) * 64], ps[0:64, None, 0:64].to_broadcast([64, 3, 64]), 30000.0, -30000.0,
                   ALU.mult, ALU.add, r=[pk], w=[qmk])
                yield

        def phase2(c0, cl0):
            tsl2 = slice(cl0 * 64, cl0 * 64 + 128)
            kt_c = c0 // 2
            pr = (c0 // 2) % 2
            qm = QMP[pr]
            tok = cx.ps[3]
            tkey = ("ps", 3)
            for g in range(4):
                hs = slice(3 * g, 3 * g + 3)
                for br in range(2):
                    if br == 0:
                        kts = list(range(0, kt_c + 1))
                        qkey = [("QS", 3 * g + r) for r in range(3)] + [("QM", pr, g, 0), ("QM", pr, g, 1)]
                        VA = VS
                        vkey = "VS"
                        pob = cx.ps[4]
                        okey = ("ps", 4)
                    else:
                        kts = list(range(max(0, (c0 - 8) // 2), kt_c + 1))
                        qkey = [("QSw", 3 * g + r) for r in range(3)]
                        VA = VW
                        vkey = "VW"
                        pob = cx.ps[0]
                        okey = ("ps", 0)
                    def score(ki, kt):
                        ps, pk = pp.get()
                        if br == 0:
                            MM(P, ps[:, 0:384], KT[0:64, g, kt * 128:(kt + 1) * 128], QS[0:64, hs, tsl2], True, False,
                               r=["KT"] + qkey, w=[pk])
                            MM(P, ps[:, 0:384], IND[0:64, kt * 128:(kt + 1) * 128], qm[0:64, hs, :], False, True,
                               r=["IND"] + qkey, w=[pk])
                        else:
                            MM(P, ps[:, 0:384], KT[64:128, g, kt * 128:(kt + 1) * 128], QS[64:128, hs, tsl2], True, True,
                               r=["KT"] + qkey, w=[pk])
                        return ps, pk

                    def finish(ki, kt, ps, pk):
                        pt = PTs[(ki + br) % 3]
                        ptk = ("PTs", (ki + br) % 3)
                        corr = None
                        if kt == kt_c:
                            corr, corrk = CPT[0][:, hs, :], ("CPT", 0)
                        elif kt == kt_c - 1:
                            corr, corrk = CPT[1][:, hs, :], ("CPT", 1)
                        elif br == 1 and c0 >= 8 and ki == 0:
                            corr, corrk = WMP[:, None, :].to_broadcast([128, 3, 128]), "WMP"
                        if corr is not None:
                            sb = SB[ki % 2]
                            TT(P, "dve", sb.rearrange("p (r q) -> p r q", r=3), ps[:, 0:384].rearrange("p (r q) -> p r q", r=3),
                               corr, ALU.add, r=[pk, corrk], w=[("SB", ki % 2)])
                            ACT(P, pt, sb, AF.Exp, r=[("SB", ki % 2)], w=[ptk])
                        else:
                            ACT(P, pt, ps[:, 0:384], AF.Exp, r=[pk], w=[ptk])
                        MM(P, pob[0:65, 0:384], VA[:, kt, g, :], pt, ki == 0, ki == len(kts) - 1, r=[ptk, vkey], w=[okey])

                    prev = None
                    for ki, kt in enumerate(kts):
                        cur = (ki, kt) + score(ki, kt)
                        if prev is not None:
                            finish(*prev)
                            yield
                        prev = cur
                    finish(*prev)
                    yield
                    CP(P, "act" if br == 0 else "dve", OT[br], pob[0:65, 0:384], r=[okey], w=[("OT", br)])
                    for hf in range(2):
                        for r in range(3):
                            MM(P, tok[0:64, (hf * 3 + r) * 65:(hf * 3 + r + 1) * 65], OT[br][0:65, r * 128 + hf * 64:r * 128 + hf * 64 + 64],
                               cx.identf[0:65, 0:65], True, True, r=[("OT", br)], w=[tkey])
                    x = br + 1
                    ov = tok[0:64, 0:390].rearrange("p (h r d) -> p h r d", h=2, r=3)
                    wx = WX[:, :, :, x:x + 1]
                    TS(P, "dve", wx, ov[:, :, :, 64:65], 1e-30, None, ALU.max, None, r=[tkey], w=[("WX", x)])
                    RECIP(P, wx, wx, r=[("WX", x)], w=[("WX", x)])
                    gv = GT[:, cl0:cl0 + 2, 9 * g:9 * g + 9].rearrange("p h (r x) -> p h r x", r=3)[:, :, :, x:x + 1]
                    TT(P, "dve", wx, wx, gv, ALU.mult, r=[("WX", x), ("GT", cl0), ("GT", cl0 + 1)], w=[("WX", x)])
                    TT(P, "dve", TMP[br], ov[:, :, :, 0:64], wx.to_broadcast([64, 2, 3, 64]), ALU.mult,
                       r=[tkey, ("WX", x)], w=[("TMP", br)])
                    yield
                mq = MAINQ[:, :, 192 * g:192 * (g + 1)].rearrange("p h (r d) -> p h r d", r=3)
                TT(P, "dve", TMP[0], TMP[0], CMPC[pr][:, :, hs, :], ALU.add,
                   r=[("TMP", 0), ("CMPC", pr, g, 0), ("CMPC", pr, g, 1)], w=[("TMP", 0)])
                TT(P, "dve", mq, TMP[0], TMP[1], ALU.add, r=[("TMP", 0), ("TMP", 1)], w=["MAINQ"])
                yield
            for hf in range(2):
                ps, pk = pp.get()
                for c6 in range(6):
                    MM(P, ps[:, c6 * 64:(c6 + 1) * 64], MAINQ[0:64, hf, c6 * 128:(c6 + 1) * 128], cx.ident[0:64, 0:64], True, True, r=["MAINQ"], w=[pk])
                CP(P, "act", MIXT[:, 0:6, (cl0 + hf) * 64:(cl0 + hf + 1) * 64], ps[:, 0:384].rearrange("p (c q) -> p c q", c=6), r=[pk],
                   w=[("MIXT", i) for i in range(6)])
            yield

        def chain(*gens):
            for gq in gens:
                for _ in gq:
                    yield

        P.dma("sync", SCT4, cx.d["sct4b"][ch], w=["SCT4"])
        ga, gb = phase1(ch * 4, 0, RA), phase1(ch * 4 + 1, 1, RB)
        done_a = done_b = False
        while not (done_a and done_b):
            if not done_a:
                done_a = next(ga, "end") == "end"
            if not done_b:
                done_b = next(gb, "end") == "end"
        for pi in range(2):
            c0 = ch * 4 + 2 * pi
            g2 = phase2(c0, 2 * pi)
            g1 = chain(phase1(c0 + 2, 2, RA), phase1(c0 + 3, 3, RA)) if pi == 0 else iter(())
            n2 = 4 * ((c0 // 2 + 1) + (c0 // 2 + 1 - max(0, (c0 - 8) // 2)) + 5) + 1
            per = max(1, -(-140 // n2))
            for _ in g2:
                for _k in range(per):
                    next(g1, None)
            for _ in g1:
                pass
        mem_attn_chunk(P, cx, pp, MQ, KmT, Vm, PT, Rb, MIXT, TC)
        out_proj_chunk(P, cx, pp, WO, MIXT, Hc, hkey, TC)
        P.dma("sync", hout_v[:, :, t0:t0 + TC], Hc, r=[hkey])
    P.barrier()


def final_pass(P, cx, h_in, out):
    TC = 256
    A = cx.arena
    A.reset()
    Hb = [A.f32([128, 8, TC]) for _ in range(2)]
    Ob = [A.f32([128, 8, TC]) for _ in range(2)]
    cx.sq = [A.bf16([128, TC]) for _ in range(2)]
    cx.rstd = A.f32([128, TC])
    gn = A.f32([128, 8])
    P.dma("sync", gn, cx.d["final_normT"], w=["gn"])
    hin_v = h_in.rearrange("(kc p) t -> p kc t", p=128)
    P.dma("sync", Hb[0], hin_v[:, :, 0:TC], w=[("H", 0)])
    for ch in range(S // TC):
        t0 = ch * TC
        b = ch % 2
        if ch + 1 < S // TC:
            P.dma("sync", Hb[1 - b], hin_v[:, :, t0 + TC:t0 + 2 * TC], w=[("H", 1 - b)])
        rms_chunk(P, cx, Hb[b], ("H", b), gn, Ob[b], ("O", b), TC)
        o = P.dma("sync", out.rearrange("(kc p) t -> p kc t", p=128)[:, :, t0:t0 + TC], Ob[b],
                  r=[(("O", b), kc) for kc in range(8)])
        P.out_dmas.append(o)
    P.barrier()


IN_SPECS = [
    ("xT", [D, S]), ("memT", [D, MEM]),
    ("norm_mixT", [DEPTH, 128, 8]), ("norm_memT", [DEPTH, 128, 8]), ("norm_ffnT", [DEPTH, 128, 8]),
    ("final_normT", [128, 8]),
    ("w_up", [DEPTH, D, 2 * FFN]), ("w_down", [DEPTH, FFN, D]),
    ("conv_wT", [DEPTH, 128, 44, 3]), ("conv_bT", [DEPTH, 128, 44]),
    ("ones", [128, 128]),
    ("w_mem_kv", [DEPTH, D, 512]), ("w_out", [DEPTH, D, D]),
    ("gla_w_in", [2, D, 2576]), ("gla_w_gate_up", [2, 16, 384]), ("gla_b_gate", [2, 384]), ("gla_out_norm_rep", [2, 128, 192]),
    ("cmat", [6, 128, 128]),
    ("nsa_w_in", [2, D, 1060]), ("kv_normT", [128, 8]), ("w_kv_shared", [D, 1536]),
    ("cmp_posT", [2, 64, 32]), ("cmp_w1", [2, 2048, 128]), ("cmp_b1T", [128, 2]), ("cmp_w2", [2, 128, 64]),
    ("cmp_b2", [2, 64]), ("cmp_b2T", [64, 2]), ("rel_bias", [32, 12]),
    ("ind", [128, S]), ("ovl", [128, 128]), ("sct4b", [16, 64, 4, 2, 64]), ("oh_u", [33, NSA_U]),
]


def build(stages=("ffn0", "final")):
    nc = bass.Bass("TRN2", target_bir_lowering=False)
    cx = Ctx()
    cx.d = {}
    for name, shape in IN_SPECS:
        cx.d[name] = nc.dram_tensor(name, shape, F32, kind="ExternalInput").ap()
    outT = nc.dram_tensor("outT", [D, S], F32, kind="ExternalOutput").ap()
    hA = nc.dram_tensor("hA", [D, S], F32, kind="Internal").ap()
    cx.d_KT = nc.dram_tensor("d_KT", [128, 4 * S], BF16, kind="Internal").ap()
    cx.d_VS = nc.dram_tensor("d_VS", [128, 32 * 4 * 65], BF16, kind="Internal").ap()
    cx.d_VW = nc.dram_tensor("d_VW", [128, 32 * 4 * 65], BF16, kind="Internal").ap()
    cx.d_KC = nc.dram_tensor("d_KC", [64, 4 * 256], BF16, kind="Internal").ap()
    cx.d_VC = nc.dram_tensor("d_VC", [128, 2 * 4 * 65], BF16, kind="Internal").ap()
    cx.d_GV = nc.dram_tensor("d_GV", [12, NSA_U], F32, kind="Internal").ap()
    P = Prog(nc)
    with ExitStack() as st:
        NW = 51200
        arena_t = st.enter_context(nc.sbuf_tensor("arena", [128, NW], F32))
        ones_bf = st.enter_context(nc.sbuf_tensor("ones_bf", [128, 128], BF16))
        ones_f = st.enter_context(nc.sbuf_tensor("ones_f", [128, 128], F32))
        cx.arena = Arena(arena_t, NW)
        cx.ones_bf = ones_bf
        psall = st.enter_context(nc.psum_tensor("psall", [128, 4096], F32))
        cx.psall = psall
        cx.ps = [psall[:, i * 512:(i + 1) * 512] for i in range(8)]
        sems = {e: st.enter_context(nc.semaphore("s_" + e)) for e in Prog.ENGS}
        dsems = [st.enter_context(nc.semaphore("d%d" % i)) for i in range(NDSEM)]
        block = st.enter_context(nc.Block())

        P.dma("sync", ones_f[:, :], cx.d["ones"], w=["ones_f"])
        P.op("dve", lambda e: e.tensor_copy(out=ones_bf[:, :], in_=ones_f[:, :]), r=["ones_f"], w=["ones"])
        cmf = st.enter_context(nc.sbuf_tensor("cmf", [128, 6, 128], F32))
        cmb = st.enter_context(nc.sbuf_tensor("cmb", [128, 6, 128], BF16))
        one_col = st.enter_context(nc.sbuf_tensor("one_col", [128, 2], F32))
        P.dma("sync", cmf[:, :, :], cx.d["cmat"].rearrange("c p n -> p c n"), w=["cmf"])
        P.op("dve", lambda e: e.tensor_copy(out=cmb[:, :, :], in_=cmf[:, :, :]), r=["cmf"], w=["cmb"])
        P.op("dve", lambda e: e.memset(one_col[:, :], 1.0), w=["one_col"])
        cx.ident = cmb[:, 0, :]
        cx.identf = cmf[:, 0, :]
        cx.triu = cmf[:, 1, :]
        cx.strictl = cmf[:, 2, :]
        cx.maskji = cmf[:, 3, :]
        cx.ones1 = cmb[:, 4, :]
        cx.one_col = one_col
        cx.antiid = cmf[:, 5, :]
        P.barrier()

        h = cx.d["xT"]
        for stg in stages:
            if stg.startswith("ffn"):
                ffn_pass(P, cx, int(stg[3:]), h, hA)
                h = hA
            elif stg.startswith("gla"):
                gla_pass(P, cx, int(stg[3:]), h, hA)
                h = hA
            elif stg == "kv":
                kv_pass(P, cx, h)
            elif stg.startswith("nsa"):
                nsa_pass(P, cx, int(stg[3:]), h, hA)
                h = hA
            elif stg == "final":
                final_pass(P, cx, h, outT)
        P.barrier()
        P.emit(block, sems, dsems)
    cx.P = P
    return nc, cx


def prep_common(inp):
    f = np.float32

    def featT(v):
        v = np.asarray(v, f)
        return np.ascontiguousarray(v.reshape(v.shape[:-1] + (8, 128)).swapaxes(-1, -2))

    m = {}
    m["norm_mixT"] = featT(inp["norm_mix"])
    m["norm_memT"] = featT(inp["norm_mem"])
    m["norm_ffnT"] = featT(inp["norm_ffn"])
    m["final_normT"] = featT(inp["final_norm"])
    m["w_up"] = np.ascontiguousarray(np.asarray(inp["w_up"], f))
    m["w_down"] = np.ascontiguousarray(np.asarray(inp["w_down"], f))
    cwv = np.asarray(inp["conv_w"], f)
    m["conv_wT"] = np.ascontiguousarray(cwv.reshape(DEPTH, 3, 44, 128).transpose(0, 3, 2, 1))
    cbv = np.asarray(inp["conv_b"], f)
    m["conv_bT"] = np.ascontiguousarray(cbv.reshape(DEPTH, 44, 128).transpose(0, 2, 1))
    m["ones"] = np.full((128, 128), 1.0 / D, f)
    for k in ("w_mem_kv", "w_out", "gla_w_in", "gla_w_gate_up", "gla_b_gate"):
        m[k] = np.ascontiguousarray(np.asarray(inp[k], f))
    m["gla_out_norm_rep"] = np.ascontiguousarray(np.broadcast_to(np.asarray(inp["gla_out_norm"], f)[:, None, :], (2, 128, 192)))
    jj, ii = np.meshgrid(np.arange(128), np.arange(128), indexing="ij")
    cm = np.zeros((6, 128, 128), f)
    cm[0] = np.eye(128)
    cm[1] = (jj <= ii) * (-1.0 / 16.0)
    cm[2] = (jj > ii) * (-1.0 / 16.0)
    cm[3] = (jj <= ii) * 1.0
    cm[4] = 1.0
    cm[5] = np.eye(128)[::-1]
    m["cmat"] = cm
    for k in ("nsa_w_in", "w_kv_shared", "cmp_w1", "cmp_w2", "cmp_b2", "rel_bias"):
        m[k] = np.ascontiguousarray(np.asarray(inp[k], f))
    m["kv_normT"] = featT(inp["kv_norm"])
    m["cmp_posT"] = np.ascontiguousarray(np.asarray(inp["cmp_pos"], f).transpose(0, 2, 1))
    m["cmp_b1T"] = np.ascontiguousarray(np.asarray(inp["cmp_b1"], f).T)
    m["cmp_b2T"] = np.ascontiguousarray(np.asarray(inp["cmp_b2"], f).T)
    pidx = np.arange(128)[:, None] % 64
    m["ind"] = (np.arange(S)[None, :] // 64 == pidx).astype(f)
    kk = (np.arange(2)[None, :, None] * 128 + np.arange(128)[:, None, None])
    jb = np.arange(64)[None, None, :]
    ov = ((16 * kk < 64 * jb + 64) & (16 * kk + 31 >= 64 * jb) & (kk < 255)).astype(f)
    m["ovl"] = np.ascontiguousarray(ov.reshape(128, 128))
    cb = np.arange(64)[:, None]
    jj2 = np.arange(64)[None, :]
    forced = (jj2 == 0) | (jj2 == cb) | (jj2 == cb - 1)
    valid = (jj2 <= cb) & ~forced
    add = np.where(forced, 1e4, np.where(jj2 <= cb, 0.0, -1.0))
    sct = np.stack([valid.astype(f), add.astype(f)], axis=1).reshape(16, 1, 4, 2, 64)
    m["sct4b"] = np.ascontiguousarray(np.broadcast_to(sct, (16, 64, 4, 2, 64)))
    dd = np.arange(NSA_U) - 127
    dcl = np.maximum(dd, 0)
    large = 16 + (np.log(np.maximum(dcl, 1).astype(f) / f(16)) / f(np.log(128 / 16)) * f(16)).astype(np.int32)
    bucket = np.where(dcl < 16, dcl, np.minimum(large, 31))
    near = (dd >= 0) & (dd < 113)
    oh = np.zeros((33, NSA_U), f)
    for b in range(31):
        oh[b] = (near & (bucket == b))
    oh[31] = -1.0 * near
    oh[32] = ((dd < 0) | (dd >= 512))
    assert not (near & (bucket == 31)).any()
    m["oh_u"] = oh
    return m


def prep_inputs(inp, b, common):
    f = np.float32
    m = dict(common)
    m["xT"] = np.ascontiguousarray(np.asarray(inp["x"][b], f).T)
    m["memT"] = np.ascontiguousarray(np.asarray(inp["mem"][b], f).T)
    return m


def run(inp, stages, ncores=8):
    nc, cx = build(stages)
    common = prep_common(inp)
    per_b = [prep_inputs(inp, b, common) for b in range(4)]
    in_maps = [per_b[c // 2] for c in range(ncores)]
    res = run_bass_kernel_spmd(nc, in_maps, core_ids=list(range(ncores)))
    out = np.stack([np.ascontiguousarray(res.results[2 * b]["outT"].T) for b in range((ncores + 1) // 2)], axis=0)
    return out.astype(np.float32)


FULL = ("gla0", "ffn0", "gla1", "ffn1", "kv", "nsa2", "ffn2", "nsa3", "ffn3", "final")


def kernel(**inputs):
    return run(inputs, FULL)
```

```python
import numpy as np
import concourse.bass as bass
import concourse.mybir as mybir
from concourse.bass_utils import run_bass_kernel_spmd
from contextlib import ExitStack

F32 = mybir.dt.float32
BF16 = mybir.dt.bfloat16
AF = mybir.ActivationFunctionType
ALU = mybir.AluOpType
AX = mybir.AxisListType

D = 1024
S = 4096
DEPTH = 4
FFN = 2816
MEM = 256
EPS = 1e-6
NDSEM = 24
SAME_ENGINE_SYNC = True


class Op:
    __slots__ = ("eng", "fn", "dma", "idx", "waits", "signal", "sigval", "dsem", "dval")


class Prog:
    ENGS = ("pe", "act", "dve", "pool", "sync")

    def __init__(self, nc):
        self.nc = nc
        self.ops = {e: [] for e in self.ENGS}
        self.last_w = {}
        self.readers = {}
        self.waited_c = {e: {x: -1 for x in self.ENGS} for e in self.ENGS}
        self.waited_d = {e: {} for e in self.ENGS}
        self.ndma = 0
        self.dma_since_barrier = {}
        self.out_dmas = []

    def _add_dep(self, o, d):
        if d is None or d is o:
            return
        e = o.eng
        if d.dma:
            if self.waited_d[e].get(d.dsem, 0) >= d.dval:
                return
            self.waited_d[e][d.dsem] = d.dval
            o.waits.append(("d", d.dsem, d.dval))
            return
        if d.eng == e:
            if e == "pe" or not SAME_ENGINE_SYNC:
                return
        if self.waited_c[e][d.eng] >= d.idx:
            return
        self.waited_c[e][d.eng] = d.idx
        d.signal = True
        o.waits.append(("c", d.eng, d))

    def op(self, eng, fn, r=(), w=(), dma=False):
        o = Op()
        o.eng = eng
        o.fn = fn
        o.dma = dma
        o.idx = len(self.ops[eng])
        o.waits = []
        o.signal = False
        o.sigval = 0
        o.dsem = o.dval = None
        if dma:
            n = self.ndma
            self.ndma += 1
            o.dsem = n % NDSEM
            o.dval = 16 * (n // NDSEM + 1)
            if n >= NDSEM:
                prev = 16 * (n // NDSEM)
                if self.waited_d[eng].get(o.dsem, 0) < prev:
                    self.waited_d[eng][o.dsem] = prev
                    o.waits.append(("d", o.dsem, prev))
            self.dma_since_barrier[o.dsem] = o
        for b in r:
            self._add_dep(o, self.last_w.get(b))
            if isinstance(b, tuple) and b[0] == "ps":
                for t in self.readers.get(b, ()):
                    if t.eng != eng:
                        self._add_dep(o, t)
        for b in w:
            self._add_dep(o, self.last_w.get(b))
            for t in self.readers.get(b, ()):
                self._add_dep(o, t)
        for b in r:
            self.readers.setdefault(b, []).append(o)
        for b in w:
            self.last_w[b] = o
            self.readers[b] = []
        self.ops[eng].append(o)
        return o

    def dma(self, eng, out, in_, r=(), w=()):
        return self.op(eng, lambda e: e.dma_start(out=out, in_=in_), r=r, w=w, dma=True)

    def barrier(self):
        lasts = {}
        for e in self.ENGS:
            real = [o for o in self.ops[e][-64:] if o.fn is not None]
            if not real:
                real = [o for o in self.ops[e] if o.fn is not None]
            lasts[e] = real[-1] if real else None
        dmas = list(self.dma_since_barrier.values())
        for e in self.ENGS:
            o = self.op(e, None)
            for x in self.ENGS:
                d = lasts[x]
                if d is not None and x != e and not d.dma:
                    self._add_dep(o, d)
                elif d is not None and d.dma:
                    self._add_dep(o, d)
            for d in dmas:
                self._add_dep(o, d)
        self.dma_since_barrier = {}
        self.last_w = {}
        self.readers = {}

    def emit(self, block, sems, dsems):
        nc = self.nc
        for e in self.ENGS:
            c = 0
            for o in self.ops[e]:
                if o.signal:
                    c += 1
                    o.sigval = c
        self.sig_counts = {e: sum(1 for o in self.ops[e] if o.signal) for e in self.ENGS}

        def run(ename, eng):
            for o in self.ops[ename]:
                for wt in o.waits:
                    if wt[0] == "d":
                        eng.wait_ge(dsems[wt[1]], wt[2])
                    else:
                        eng.wait_ge(sems[wt[1]], wt[2].sigval)
                if o.fn is None:
                    continue
                ins = o.fn(eng)
                if o.dma:
                    ins.then_inc(dsems[o.dsem], 16)
                elif o.signal:
                    ins.then_inc(sems[ename], 1)

        @block.tensor
        def _(eng):
            run("pe", eng)

        @block.scalar
        def _(eng):
            run("act", eng)

        @block.vector
        def _(eng):
            run("dve", eng)

        @block.gpsimd
        def _(eng):
            run("pool", eng)

        @block.sync
        def _(eng):
            run("sync", eng)


class Arena:
    def __init__(self, tens, nwords):
        self.t = tens
        self.n = nwords
        self.off = 0

    def reset(self):
        self.off = 0

    def f32(self, shape):
        n = int(np.prod(shape[1:]))
        assert self.off + n <= self.n, ("arena overflow", self.off, n, self.n)
        ap = self.t[0:shape[0], self.off:self.off + n]
        self.off += n
        return _shape(ap, shape)

    def bf16(self, shape):
        n = int(np.prod(shape[1:]))
        nw = (n + 1) // 2
        assert self.off + nw <= self.n, ("arena overflow", self.off, nw, self.n)
        ap = self.t[0:shape[0], self.off:self.off + nw].bitcast(BF16)
        if 2 * nw != n:
            ap = ap[:, 0:n]
        self.off += nw
        return _shape(ap, shape)


def _shape(ap, shape):
    if len(shape) == 2:
        return ap
    if len(shape) == 3:
        return ap.rearrange("p (a b) -> p a b", a=shape[1], b=shape[2])
    if len(shape) == 4:
        return ap.rearrange("p (a b c) -> p a b c", a=shape[1], b=shape[2], c=shape[3])
    raise ValueError(shape)


class Ctx:
    pass


def load_cast(P, cx, dst, src, key, shape, cast_i):
    sb = cast_i % 2
    n = int(np.prod(shape[1:]))
    stg = _shape(cx.stg[sb][0:shape[0], 0:n], shape)
    P.dma("sync", stg, src, w=[("stg", sb)])
    eng = ("dve", "pool", "act")[cast_i % 3]
    if eng == "act":
        P.op("act", lambda e: e.copy(out=dst, in_=stg), r=[("stg", sb)], w=[key])
    else:
        P.op(eng, lambda e: e.tensor_copy(out=dst, in_=stg), r=[("stg", sb)], w=[key])


def rms_chunk(P, cx, Hc, hkey, gcol, xnT, xkey, TC, pool_share=False):
    ps = cx.ps[0][:, 0:TC]
    for kc in range(8):
        sq = cx.sq[kc % 2][:, 0:TC]
        P.op("act", lambda e, sq=sq, kc=kc: e.activation(out=sq, in_=Hc[:, kc, :], func=AF.Square),
             r=[hkey], w=[("sq", kc % 2)])
        P.op("pe", lambda e, sq=sq, kc=kc: e.matmul(ps, lhsT=cx.ones_bf[:, :], rhs=sq, start=(kc == 0), stop=(kc == 7)),
             r=[("sq", kc % 2)], w=[("ps", 0)])
    rstd = cx.rstd[:, 0:TC]
    P.op("dve", lambda e: e.tensor_scalar(out=rstd, in0=ps, scalar1=EPS, scalar2=None, op0=ALU.add),
         r=[("ps", 0)], w=["rstd"])
    P.op("act", lambda e: e.activation(out=rstd, in_=rstd, func=AF.Sqrt), r=["rstd"], w=["rstd"])
    P.op("dve", lambda e: e.reciprocal(out=rstd, in_=rstd), r=["rstd"], w=["rstd"])
    for kc in range(8):
        eng = "pool" if (pool_share and kc % 2 == 1) else "dve"
        P.op(eng, lambda e, kc=kc: e.scalar_tensor_tensor(out=xnT[:, kc, :], in0=Hc[:, kc, :], scalar=gcol[:, kc:kc + 1],
                                                          in1=rstd, op0=ALU.mult, op1=ALU.mult),
             r=[hkey, "rstd", "gn"], w=[(xkey, kc)])


def MM(P, out, lhsT, rhs, start, stop, r, w, **kw):
    return P.op("pe", lambda e: e.matmul(out, lhsT=lhsT, rhs=rhs, start=start, stop=stop, **kw), r=r, w=w)


def ACT(P, out, in_, func, r, w, **kw):
    return P.op("act", lambda e: e.activation(out=out, in_=in_, func=func, **kw), r=r, w=w)


def TT(P, eng, out, in0, in1, op, r, w):
    return P.op(eng, lambda e: e.tensor_tensor(out=out, in0=in0, in1=in1, op=op), r=r, w=w)


def STT(P, out, in0, scalar, in1, op0, op1, r, w):
    return P.op("dve", lambda e: e.scalar_tensor_tensor(out=out, in0=in0, scalar=scalar, in1=in1, op0=op0, op1=op1), r=r, w=w)


def TS(P, eng, out, in0, s1, s2, op0, op1, r, w):
    if s2 is None:
        return P.op(eng, lambda e: e.tensor_scalar(out=out, in0=in0, scalar1=s1, scalar2=None, op0=op0), r=r, w=w)
    return P.op(eng, lambda e: e.tensor_scalar(out=out, in0=in0, scalar1=s1, scalar2=s2, op0=op0, op1=op1), r=r, w=w)


def CP(P, eng, out, in_, r, w):
    if eng == "act":
        return P.op("act", lambda e: e.copy(out=out, in_=in_), r=r, w=w)
    return P.op(eng, lambda e: e.tensor_copy(out=out, in_=in_), r=r, w=w)


def MEMSET(P, eng, out, val, w):
    return P.op(eng, lambda e: e.memset(out, val), w=w)


def RECIP(P, out, in_, r, w):
    return P.op("dve", lambda e: e.reciprocal(out=out, in_=in_), r=r, w=w)


def ffn_pass(P, cx, l, h_in, h_out):
    TC = 512
    NCH = S // TC
    A = cx.arena
    A.reset()
    WU = A.bf16([128, 8, 2 * FFN])
    WD = A.bf16([128, 22, D])
    Hst = A.f32([128, 4096])
    cx.stg = [Hst[:, 0:2048], Hst[:, 2048:4096]]
    Hc = _shape(Hst, [128, 8, TC])
    hkey = "H"
    xnT = A.bf16([128, 8, TC])
    cx.sq = [A.bf16([128, TC]) for _ in range(2)]
    cx.rstd = A.f32([128, TC])
    U = [A.f32([128, TC + 2]) for _ in range(2)]
    T1 = [A.f32([128, TC]) for _ in range(3)]
    SA = [A.bf16([128, TC]) for _ in range(2)]
    Rb = [A.f32([128, TC]) for _ in range(2)]
    actT = A.bf16([128, 22, TC])
    HALO = A.f32([128, 44, 2])
    cw = A.f32([128, 44, 3])
    cb = A.f32([128, 44])
    gn = A.f32([128, 8])

    P.dma("sync", cw, cx.d["conv_wT"][l], w=["cw"])
    P.dma("sync", cb, cx.d["conv_bT"][l], w=["cb"])
    P.dma("sync", gn, cx.d["norm_ffnT"][l], w=["gn"])
    wu_src = cx.d["w_up"][l].rearrange("(kc p) c -> p kc c", p=128)
    ci = 0
    for c0 in range(0, 2 * FFN, 256):
        load_cast(P, cx, WU[:, :, c0:c0 + 256], wu_src[:, :, c0:c0 + 256], "WU", [128, 8, 256], ci)
        ci += 1
    wd_src = cx.d["w_down"][l].rearrange("(fc p) n -> p fc n", p=128)
    for f0 in range(0, 22, 2):
        load_cast(P, cx, WD[:, f0:f0 + 2, :], wd_src[:, f0:f0 + 2, :], "WD", [128, 2, D], ci)
        ci += 1
    P.op("pool", lambda e: e.memset(HALO, 0.0), w=["halo"])

    hin_v = h_in.rearrange("(kc p) t -> p kc t", p=128)
    hout_v = h_out.rearrange("(kc p) t -> p kc t", p=128)
    P.dma("sync", Hc, hin_v[:, :, 0:TC], w=[hkey, ("stg", 0), ("stg", 1)])
    nu = 0
    for ch in range(NCH):
        t0 = ch * TC
        rms_chunk(P, cx, Hc, hkey, gn, xnT, "xnT", TC)
        if ch + 1 < NCH:
            P.dma("sync", Hc, hin_v[:, :, t0 + TC:t0 + 2 * TC], w=[hkey])
        P.dma("sync", Rb[0], hin_v[:, 0, t0:t0 + TC], w=[("R", 0)])
        for fc in range(22):
            tt = []
            for half in range(2):
                cc = fc + 22 * half
                pi = nu % 4
                ui = nu % 2
                ti = nu % 3
                nu += 1
                pst = cx.ps[1 + pi][:, 0:TC]
                pk = ("ps", 1 + pi)
                for kc in range(8):
                    MM(P, pst, WU[:, kc, cc * 128:(cc + 1) * 128], xnT[:, kc, :], kc == 0, kc == 7, r=[("xnT", kc), "WU"], w=[pk])
                Ut = U[ui]
                uk = ("U", ui)
                t1 = T1[ti]
                tk = ("T1", ti)
                CP(P, "pool", Ut[:, 0:2], HALO[:, cc, :], r=["halo"], w=[uk])
                CP(P, "act", Ut[:, 2:TC + 2], pst, r=[pk], w=[uk])
                ACT(P, t1, pst, AF.Identity, r=[pk, "cw", "cb"], w=[tk], bias=cb[:, cc:cc + 1], scale=cw[:, cc, 2:3])
                STT(P, t1, Ut[:, 1:TC + 1], cw[:, cc, 1:2], t1, ALU.mult, ALU.add, r=[uk, tk, "cw"], w=[tk])
                STT(P, t1, Ut[:, 0:TC], cw[:, cc, 0:1], t1, ALU.mult, ALU.add, r=[uk, tk, "cw"], w=[tk])
                CP(P, "pool", HALO[:, cc, :], Ut[:, TC:TC + 2], r=[uk], w=["halo"])
                tt.append((t1, tk))
            sa = SA[fc % 2]
            sk = ("SA", fc % 2)
            ACT(P, sa, tt[0][0], AF.Silu, r=[tt[0][1]], w=[sk])
            TT(P, "pool", actT[:, fc, :], sa, tt[1][0], ALU.mult, r=[sk, tt[1][1]], w=[("actT", fc)])
        for n in range(8):
            pd = cx.ps[5 + n % 2][:, 0:TC]
            pk = ("ps", 5 + n % 2)
            for fc in range(22):
                MM(P, pd, WD[:, fc, n * 128:(n + 1) * 128], actT[:, fc, :], fc == 0, fc == 21, r=[("actT", fc), "WD"], w=[pk])
            if n + 1 < 8:
                P.dma("sync", Rb[(n + 1) % 2], hin_v[:, n + 1, t0:t0 + TC], w=[("R", (n + 1) % 2)])
            rb = Rb[n % 2]
            TT(P, "dve", rb, rb, pd, ALU.add, r=[pk, ("R", n % 2)], w=[("R", n % 2)])
            P.dma("sync", hout_v[:, n, t0:t0 + TC], rb, r=[("R", n % 2)])
    P.barrier()


class PsPool:
    def __init__(self, cx, banks):
        self.cx = cx
        self.banks = banks
        self.i = 0

    def get(self):
        b = self.banks[self.i % len(self.banks)]
        self.i += 1
        return self.cx.ps[b], ("ps", b)


def mem_setup(P, cx, l, WMK, xmT, KmT, Vm, gm, Hstage):
    pp = PsPool(cx, [1, 2, 3])
    P.dma("sync", gm, cx.d["norm_memT"][l], w=["gn"])
    src = cx.d["w_mem_kv"][l].rearrange("(kc p) c -> p kc c", p=128)
    for i, c0 in enumerate(range(0, 512, 256)):
        load_cast(P, cx, WMK[:, :, c0:c0 + 256], src[:, :, c0:c0 + 256], "WMK", [128, 8, 256], i)
    Hm = _shape(Hstage[:, 0:8 * MEM], [128, 8, MEM])
    P.dma("sync", Hm, cx.d["memT"].rearrange("(kc p) t -> p kc t", p=128), w=[("stg", 0)])
    rms_chunk(P, cx, Hm, ("stg", 0), gm, xmT, "xmT", MEM)
    for h in range(4):
        ps, pk = pp.get()
        for kc in range(8):
            MM(P, ps[0:64, 0:MEM], WMK[:, kc, h * 64:(h + 1) * 64], xmT[:, kc, :], kc == 0, kc == 7,
               r=[("xmT", kc), "WMK"], w=[pk])
        CP(P, "act", KmT[0:64, h, :], ps[0:64, 0:MEM], r=[pk], w=["KmT"])
    for mt in range(2):
        ps, pk = pp.get()
        for kc in range(8):
            MM(P, ps[:, 0:256], xmT[:, kc, mt * 128:(mt + 1) * 128], WMK[:, kc, 256:512], kc == 0, kc == 7,
               r=[("xmT", kc), "WMK"], w=[pk])
        CP(P, "dve", Vm[:, mt, :], ps[:, 0:256], r=[pk], w=["Vm"])


def mem_attn_chunk(P, cx, pp, MQ, KmT, Vm, PT, Rb, MIXT, TC):
    for h in range(4):
        po = (h % 2) * 64
        for mt in range(2):
            ps, pk = pp.get()
            MM(P, ps[:, 0:TC], KmT[0:64, h, mt * 128:(mt + 1) * 128], MQ[0:64, h, :], True, True,
               r=["KmT", ("MQ", h)], w=[pk])
            ACT(P, PT[mt][:, 0:TC], ps[:, 0:TC], AF.Exp, r=[pk], w=[("PT", mt)], scale=0.125)
        pso = cx.ps[4]
        pss = cx.ps[5]
        for mt in range(2):
            MM(P, pso[po:po + 64, 0:TC], Vm[:, mt, h * 64:(h + 1) * 64], PT[mt][:, 0:TC], mt == 0, mt == 1,
               r=["Vm", ("PT", mt)], w=[("ps", 4)])
        for mt in range(2):
            MM(P, pss[:, 0:TC], cx.ones1[:, :], PT[mt][:, 0:TC], mt == 0, mt == 1,
               r=[("PT", mt)], w=[("ps", 5)])
        RECIP(P, Rb[:, 0:TC], pss[:, 0:TC], r=[("ps", 5)], w=["Rb"])
        TT(P, "dve", MIXT[po:po + 64, 6 + h // 2, :], pso[po:po + 64, 0:TC], Rb[po:po + 64, 0:TC], ALU.mult,
           r=[("ps", 4), "Rb"], w=[("MIXT", 6 + h // 2)])


def out_proj_chunk(P, cx, pp, WO, MIXT, Hc, hkey, TC):
    for n in range(8):
        ps, pk = pp.get()
        for c in range(8):
            MM(P, ps[:, 0:TC], WO[:, c, n * 128:(n + 1) * 128], MIXT[:, c, :], c == 0, c == 7,
               r=[("MIXT", c), "WO"], w=[pk])
        TT(P, "dve", Hc[:, n, :], Hc[:, n, :], ps[:, 0:TC], ALU.add, r=[pk, hkey], w=[hkey])


def gla_pass(P, cx, l, h_in, h_out):
    TC = 512
    NCH = S // TC
    A = cx.arena
    A.reset()
    WIN = A.bf16([128, 8, 2576])
    WO = A.bf16([128, 8, D])
    Hst = A.f32([128, 4096])
    cx.stg = [Hst[:, 0:2048], Hst[:, 2048:4096]]
    Hc = _shape(Hst, [128, 8, TC])
    hkey = "H"
    xnT = A.bf16([128, 8, TC])
    cx.sq = [A.bf16([128, TC]) for _ in range(2)]
    cx.rstd = A.f32([128, TC])
    gn = A.f32([128, 8])
    gm = A.f32([128, 8])
    LR = A.bf16([32, TC])
    WGf = A.f32([32, 384])
    WGa = A.bf16([32, 384])
    LA = A.f32([128, 4, 384])
    E1 = A.f32([128, 384])
    EQ = A.f32([96, 4, TC])
    EK = A.f32([96, 4, TC])
    EKO = A.f32([128, 4, 384])
    QIN = A.bf16([96, 4, TC])
    KDEC = A.bf16([96, 4, TC])
    V = A.bf16([128, 4, 768])
    GS = A.f32([128, 768])
    GG = A.bf16([128, 4, 768])
    KOUT = A.bf16([128, 4, 384])
    MQ = A.bf16([64, 4, TC])
    Sst = A.f32([96, 4, 192])
    Sbf = A.bf16([96, 4, 192])
    ATm = [A.bf16([128, 128]) for _ in range(2)]
    MAIN = [A.bf16([128, 768]) for _ in range(2)]
    MIXT = A.bf16([128, 8, TC])
    PT = [A.bf16([128, TC]) for _ in range(2)]
    Rb = A.f32([128, TC])
    KmT = A.bf16([64, 4, MEM])
    Vm = A.bf16([128, 2, MEM])
    ON = A.f32([128, 192])
    SS = A.f32([128, 4])
    JUNK = A.f32([128, 192])
    xmT = A.bf16([128, 8, MEM])
    WMK = A.bf16([128, 8, 512])

    pp = PsPool(cx, [1, 2, 3])
    P.dma("sync", gn, cx.d["norm_mixT"][l], w=["gn"])
    mem_setup(P, cx, l, WMK, xmT, KmT, Vm, gm, Hst)
    P.dma("sync", gn, cx.d["norm_mixT"][l], w=["gn"])
    P.dma("sync", ON, cx.d["gla_out_norm_rep"][l], w=["ON"])
    src = cx.d["gla_w_in"][l].rearrange("(kc p) c -> p kc c", p=128)
    ci = 0
    for c0 in range(0, 2576, 256):
        c1 = min(2576, c0 + 256)
        load_cast(P, cx, WIN[:, :, c0:c1], src[:, :, c0:c1], "WIN", [128, 8, c1 - c0], ci)
        ci += 1
    src = cx.d["w_out"][l].rearrange("(kc p) c -> p kc c", p=128)
    for c0 in range(0, D, 256):
        load_cast(P, cx, WO[:, :, c0:c0 + 256], src[:, :, c0:c0 + 256], "WO", [128, 8, 256], ci)
        ci += 1
    MEMSET(P, "dve", WGf, 0.0, w=["WGf"])
    P.dma("sync", WGf[0:16, :], cx.d["gla_w_gate_up"][l], w=["WGf"])
    P.dma("sync", WGf[16:17, :], cx.d["gla_b_gate"][l:l + 1, :], w=["WGf"])
    CP(P, "dve", WGa, WGf, r=["WGf"], w=["WGa"])
    MEMSET(P, "dve", LR, 1.0, w=["LR"])
    MEMSET(P, "dve", Sst, 0.0, w=["S"])
    MEMSET(P, "dve", Sbf, 0.0, w=["Sbf"])

    hin_v = h_in.rearrange("(kc p) t -> p kc t", p=128)
    hout_v = h_out.rearrange("(kc p) t -> p kc t", p=128)
    for ch in range(NCH):
        t0 = ch * TC
        P.dma("sync", Hc, hin_v[:, :, t0:t0 + TC], w=[hkey, ("stg", 0), ("stg", 1)])
        rms_chunk(P, cx, Hc, hkey, gn, xnT, "xnT", TC)
        xr = [("xnT", kc) for kc in range(8)]
        ps, pk = pp.get()
        for kc in range(8):
            MM(P, ps[0:16, 0:TC], WIN[:, kc, 2304:2320], xnT[:, kc, :], kc == 0, kc == 7, r=[("xnT", kc), "WIN"], w=[pk])
        CP(P, "act", LR[0:16, :], ps[0:16, 0:TC], r=[pk], w=["LR"])
        for s in range(4):
            ts = slice(s * 128, (s + 1) * 128)
            ps, pk = pp.get()
            MM(P, ps[:, 0:384], LR[0:32, ts], WGa[0:32, :], True, True, r=["LR", "WGa"], w=[pk])
            ACT(P, E1, ps[:, 0:384], AF.Exp, r=[pk], w=["E1"], scale=-1.0)
            ACT(P, LA[:, s, :], E1, AF.Ln, r=["E1"], w=[("LA", s)], bias=cx.one_col[:, 0:1])
            psb = cx.ps[6]
            for h in range(4):
                MM(P, psb[0:96, h * 128:(h + 1) * 128], LA[:, s, h * 96:(h + 1) * 96], cx.triu[:, :], True, True,
                   r=[("LA", s)], w=[("ps", 6)])
            ACT(P, EQ[:, :, ts], psb[0:96, :].rearrange("p (h t) -> p h t", h=4), AF.Exp, r=[("ps", 6)], w=[("EQ", s)])
            ACT(P, EK[:, :, ts], psb[0:96, :].rearrange("p (h t) -> p h t", h=4), AF.Exp, r=[("ps", 6)], w=[("EK", s)], scale=-1.0)
            psl = cx.ps[7]
            MM(P, psl[:, 0:384], cx.strictl[:, :], LA[:, s, :], True, True, r=[("LA", s)], w=[("ps", 7)])
            ACT(P, EKO[:, s, :], psl[:, 0:384], AF.Exp, r=[("ps", 7)], w=[("EKO", s)])
        eqr = [("EQ", s) for s in range(4)]
        ekr = [("EK", s) for s in range(4)]
        for h in range(4):
            ps, pk = pp.get()
            for kc in range(8):
                MM(P, ps[0:96, 0:TC], WIN[:, kc, h * 96:(h + 1) * 96], xnT[:, kc, :], kc == 0, kc == 7, r=[("xnT", kc), "WIN"], w=[pk])
            STT(P, QIN[:, h, :], ps[0:96, 0:TC], float(96 ** -0.5), EQ[:, h, :], ALU.mult, ALU.mult, r=[pk] + eqr, w=[("QIN", h)])
            ps, pk = pp.get()
            for kc in range(8):
                MM(P, ps[0:96, 0:TC], WIN[:, kc, 384 + h * 96:384 + (h + 1) * 96], xnT[:, kc, :], kc == 0, kc == 7, r=[("xnT", kc), "WIN"], w=[pk])
            TT(P, "dve", KDEC[:, h, :], ps[0:96, 0:TC], EK[:, h, :], ALU.mult, r=[pk] + ekr, w=[("KDEC", h)])
            ps, pk = pp.get()
            for kc in range(8):
                MM(P, ps[0:64, 0:TC], WIN[:, kc, 2320 + h * 64:2320 + (h + 1) * 64], xnT[:, kc, :], kc == 0, kc == 7, r=[("xnT", kc), "WIN"], w=[pk])
            CP(P, "act", MQ[:, h, :], ps[0:64, 0:TC], r=[pk], w=[("MQ", h)])
        for s in range(4):
            ts = slice(s * 128, (s + 1) * 128)
            for half in range(2):
                ps, pk = pp.get()
                for kc in range(8):
                    MM(P, ps[:, 0:384], xnT[:, kc, ts], WIN[:, kc, 768 + half * 384:768 + (half + 1) * 384], kc == 0, kc == 7,
                       r=[("xnT", kc), "WIN"], w=[pk])
                CP(P, "act", V[:, s, half * 384:(half + 1) * 384], ps[:, 0:384], r=[pk], w=[("V", s)])
            for half in range(2):
                ps, pk = pp.get()
                for kc in range(8):
                    MM(P, ps[:, 0:384], xnT[:, kc, ts], WIN[:, kc, 1536 + half * 384:1536 + (half + 1) * 384], kc == 0, kc == 7,
                       r=[("xnT", kc), "WIN"], w=[pk])
                ACT(P, GS[:, half * 384:(half + 1) * 384], ps[:, 0:384], AF.Silu, r=[pk], w=[("GS", half)])
            TT(P, "pool", GG[:, s, :].rearrange("p (h v) -> p h v", h=4), GS.rearrange("p (h v) -> p h v", h=4),
               ON[:, None, :].to_broadcast([128, 4, 192]), ALU.mult, r=[("GS", 0), ("GS", 1), "ON"], w=[("GG", s)])
            ps, pk = pp.get()
            for kc in range(8):
                MM(P, ps[:, 0:384], xnT[:, kc, ts], WIN[:, kc, 384:768], kc == 0, kc == 7, r=[("xnT", kc), "WIN"], w=[pk])
            TT(P, "dve", KOUT[:, s, :], ps[:, 0:384], EKO[:, s, :], ALU.mult, r=[pk, ("EKO", s)], w=[("KOUT", s)])
            MEMSET(P, "dve", SS, 0.0, w=["SS"])
            for h in range(4):
                ps, pk = pp.get()
                MM(P, ps[:, 0:128], KDEC[0:96, h, ts], QIN[0:96, h, ts], True, True, r=[("KDEC", h), ("QIN", h)], w=[pk])
                at = ATm[h % 2]
                TT(P, "dve", at, ps[:, 0:128], cx.maskji[:, :], ALU.mult, r=[pk], w=[("ATm", h % 2)])
                pso = cx.ps[4 + h]
                ok = ("ps", 4 + h)
                osl = slice(0, 192)
                MM(P, pso[:, osl], at, V[:, s, h * 192:(h + 1) * 192], True, False, r=[("ATm", h % 2), ("V", s)], w=[ok])
                MM(P, pso[:, osl], QIN[0:96, h, ts], Sbf[0:96, h, :], False, True, r=[("QIN", h), "Sbf"], w=[ok])
                ps, pk = pp.get()
                MM(P, ps[0:96, 0:192], KOUT[:, s, h * 96:(h + 1) * 96], V[:, s, h * 192:(h + 1) * 192], True, True,
                   r=[("KOUT", s), ("V", s)], w=[pk])
                STT(P, Sst[:, h, :], Sst[:, h, :], EQ[:, h, s * 128 + 127:s * 128 + 128], ps[0:96, 0:192], ALU.mult, ALU.add,
                    r=[pk, "S", ("EQ", s)], w=["S"])
                CP(P, "act", Sbf[:, h, :], Sst[:, h, :], r=["S"], w=["Sbf"])
                ACT(P, JUNK, pso[:, osl], AF.Square, r=[ok], w=["JUNK", "SS"], accum_out=SS[:, h:h + 1])
            TS(P, "dve", SS, SS, 1.0 / 192, EPS, ALU.mult, ALU.add, r=["SS"], w=["SS"])
            ACT(P, SS, SS, AF.Sqrt, r=["SS"], w=["SS"])
            RECIP(P, SS, SS, r=["SS"], w=["SS"])
            mn = MAIN[s % 2]
            mk = ("MAIN", s % 2)
            for h in range(4):
                pso = cx.ps[4 + h]
                ok = ("ps", 4 + h)
                osl = slice(0, 192)
                STT(P, mn[:, h * 192:(h + 1) * 192], pso[:, osl], SS[:, h:h + 1], GG[:, s, h * 192:(h + 1) * 192], ALU.mult, ALU.mult,
                    r=[ok, "SS", ("GG", s)], w=[mk])
            for c in range(6):
                ps, pk = pp.get()
                MM(P, ps[:, 0:128], mn[:, c * 128:(c + 1) * 128], cx.ident[:, :], True, True, r=[mk], w=[pk])
                CP(P, "act" if c % 2 == 0 else "dve", MIXT[:, c, ts], ps[:, 0:128], r=[pk], w=[("MIXT", c)])
        mem_attn_chunk(P, cx, pp, MQ, KmT, Vm, PT, Rb, MIXT, TC)
        out_proj_chunk(P, cx, pp, WO, MIXT, Hc, hkey, TC)
        P.dma("sync", hout_v[:, :, t0:t0 + TC], Hc, r=[hkey])
    P.barrier()


def kv_pass(P, cx, h_in):
    TC = 512
    NCH = S // TC
    A = cx.arena
    A.reset()
    WKV = A.bf16([128, 8, 1536])
    Hst = A.f32([128, 4096])
    cx.stg = [Hst[:, 0:2048], Hst[:, 2048:4096]]
    Hc = _shape(Hst, [128, 8, TC])
    xnT = A.bf16([128, 8, TC])
    cx.sq = [A.bf16([128, TC]) for _ in range(2)]
    cx.rstd = A.f32([128, TC])
    gn = A.f32([128, 8])
    KT = A.bf16([128, 4, S])
    CF = A.bf16([128, 4, S])
    VS = A.bf16([128, 32, 4, 65])
    VW = A.bf16([128, 32, 4, 65])
    W1f = A.f32([128, 32 * 128])
    W1 = A.bf16([128, 32, 128])
    W2f = A.f32([128, 2, 64])
    W2 = A.bf16([128, 2, 64])
    POSf = A.f32([128, 32])
    POS = A.bf16([128, 32])
    B1 = A.f32([128, 2])
    B2c = A.f32([64, 2])
    B2rf = A.f32([1, 64])
    B2r = A.bf16([1, 64])
    CST = A.f32([128, 2])
    HID = A.bf16([128, 256])
    KC = A.bf16([64, 4, 256])
    VC = A.bf16([128, 2, 4, 65])
    pp = PsPool(cx, [1, 2, 3, 4, 5, 6, 7])

    P.dma("sync", gn, cx.d["kv_normT"], w=["gn"])
    src = cx.d["w_kv_shared"].rearrange("(kc p) c -> p kc c", p=128)
    for i, c0 in enumerate(range(0, 1536, 256)):
        load_cast(P, cx, WKV[:, :, c0:c0 + 256], src[:, :, c0:c0 + 256], "WKV", [128, 8, 256], i)
    MEMSET(P, "pool", VS, 1.0, w=["VS"])
    MEMSET(P, "pool", VW, 1.0, w=["VW"])
    hin_v = h_in.rearrange("(kc p) t -> p kc t", p=128)
    for ch in range(NCH):
        t0 = ch * TC
        P.dma("sync", Hc, hin_v[:, :, t0:t0 + TC], w=["H", ("stg", 0), ("stg", 1)])
        rms_chunk(P, cx, Hc, "H", gn, xnT, "xnT", TC)
        for (j, dst, po, key) in ((0, CF, 0, "CF"), (1, CF, 64, "CF"), (2, KT, 0, "KT"), (4, KT, 64, "KT")):
            for g in range(4):
                ps, pk = pp.get()
                c0 = j * 256 + g * 64
                for kc in range(8):
                    MM(P, ps[po:po + 64, 0:TC], WKV[:, kc, c0:c0 + 64], xnT[:, kc, :], kc == 0, kc == 7,
                       r=[("xnT", kc), "WKV"], w=[pk])
                CP(P, "act" if g % 2 == 0 else "dve", dst[po:po + 64, g, t0:t0 + TC], ps[po:po + 64, 0:TC], r=[pk], w=[key])
        for s4 in range(4):
            tile = ch * 4 + s4
            ts = slice(s4 * 128, (s4 + 1) * 128)
            for (j, dst, key) in ((3, VS, "VS"), (5, VW, "VW")):
                ps, pk = pp.get()
                for kc in range(8):
                    MM(P, ps[:, 0:256], xnT[:, kc, ts], WKV[:, kc, j * 256:(j + 1) * 256], kc == 0, kc == 7,
                       r=[("xnT", kc), "WKV"], w=[pk])
                CP(P, "act" if j == 3 else "dve", dst[:, tile, :, 0:64], ps[:, 0:256].rearrange("p (g d) -> p g d", g=4),
                   r=[pk], w=[key])
    P.dma("sync", W1f[0:64, :].rearrange("p (l n) -> p l n", l=32), cx.d["cmp_w1"][0].rearrange("(l d) n -> d l n", d=64), w=["W1f"])
    P.dma("sync", W1f[64:128, :].rearrange("p (l n) -> p l n", l=32), cx.d["cmp_w1"][1].rearrange("(l d) n -> d l n", d=64), w=["W1f"])
    CP(P, "dve", W1.rearrange("p l n -> p (l n)"), W1f, r=["W1f"], w=["W1"])
    P.dma("sync", W2f, cx.d["cmp_w2"].rearrange("j n d -> n j d"), w=["W2f"])
    CP(P, "dve", W2, W2f, r=["W2f"], w=["W2"])
    P.dma("sync", POSf[0:64, :], cx.d["cmp_posT"][0], w=["POSf"])
    P.dma("sync", POSf[64:128, :], cx.d["cmp_posT"][1], w=["POSf"])
    CP(P, "dve", POS, POSf, r=["POSf"], w=["POS"])
    P.dma("sync", B1, cx.d["cmp_b1T"], w=["B1"])
    P.dma("sync", B2c, cx.d["cmp_b2T"], w=["B2c"])
    P.dma("sync", B2rf, cx.d["cmp_b2"][1:2, :], w=["B2rf"])
    CP(P, "dve", B2r, B2rf, r=["B2rf"], w=["B2r"])
    MEMSET(P, "dve", KC, 0.0, w=["KC"])
    MEMSET(P, "dve", VC, 0.0, w=["VC"])
    MEMSET(P, "dve", HID, 0.0, w=["HID"])
    for j in range(2):
        po = j * 64
        ps, pk = pp.get()
        for l in range(32):
            MM(P, ps[:, 0:1], W1[po:po + 64, l, :], POS[po:po + 64, l:l + 1], l == 0, l == 31, r=["W1", "POS"], w=[pk])
        TT(P, "dve", CST[:, j:j + 1], ps[:, 0:1], B1[:, j:j + 1], ALU.add, r=[pk, "B1"], w=["CST"])
        for g in range(4):
            ps, pk = pp.get()
            for l in range(32):
                MM(P, ps[:, 0:255], W1[po:po + 64, l, :], CF[po:po + 64, g, l:l + 16 * 254 + 1:16], l == 0, l == 31,
                   r=["W1", "CF"], w=[pk])
            ACT(P, HID[:, 0:255], ps[:, 0:255], AF.Silu, r=[pk, "CST"], w=["HID"], bias=CST[:, j:j + 1])
            if j == 0:
                ps, pk = pp.get()
                MM(P, ps[0:64, 0:255], W2[:, 0, :], HID[:, 0:255], True, True, r=["W2", "HID"], w=[pk])
                ACT(P, KC[:, g, 0:255], ps[0:64, 0:255], AF.Identity, r=[pk, "B2c"], w=["KC"], bias=B2c[:, 0:1])
            else:
                for kt in range(2):
                    n = 128 if kt == 0 else 127
                    ps, pk = pp.get()
                    MM(P, ps[0:n, 0:64], HID[:, kt * 128:kt * 128 + n], W2[:, 1, :], True, False, r=["W2", "HID"], w=[pk])
                    MM(P, ps[0:n, 0:64], cx.ones1[0:1, 0:n], B2r[0:1, :], False, True, r=["B2r"], w=[pk])
                    CP(P, "dve", VC[0:n, kt, g, 0:64], ps[0:n, 0:64], r=[pk], w=["VC"])
    MEMSET(P, "dve", VC[:, :, :, 64:65], 1.0, w=["VC"])
    P.dma("sync", cx.d_KT, KT.rearrange("p g t -> p (g t)"), r=["KT"])
    P.dma("sync", cx.d_VS, VS.rearrange("p a g d -> p (a g d)"), r=["VS"])
    P.dma("sync", cx.d_VW, VW.rearrange("p a g d -> p (a g d)"), r=["VW"])
    P.dma("sync", cx.d_KC, KC.rearrange("p g t -> p (g t)"), r=["KC"])
    P.dma("sync", cx.d_VC, VC.rearrange("p a g d -> p (a g d)"), r=["VC"])
    P.barrier()


NSA_U = 768
import os
NSA_DBG = int(os.environ.get('NSA_DBG', '9'))


def nsa_pass(P, cx, l, h_in, h_out):
    li = l - 2
    TC = 256
    NCH = S // TC
    A = cx.arena
    A.reset()
    Hst = A.f32([128, 4096])
    cx.stg = [Hst[:, 0:2048], Hst[:, 2048:4096]]
    Hc = _shape(Hst[:, 0:2048], [128, 8, TC])
    hkey = ("stg", 0)
    cx.sq = [A.bf16([128, TC]) for _ in range(2)]
    cx.rstd = A.f32([128, TC])
    gn = A.f32([128, 8])
    gm = A.f32([128, 8])
    KmT = A.bf16([64, 4, MEM])
    Vm = A.bf16([128, 2, MEM])
    mark = A.off
    xmT = A.bf16([128, 8, MEM])
    WMK = A.bf16([128, 8, 512])
    pp = PsPool(cx, [1, 2])
    pp1 = PsPool(cx, [3])
    psall = cx.psall
    mem_setup(P, cx, l, WMK, xmT, KmT, Vm, gm, Hst)
    P.barrier()
    A.off = mark
    WIN = A.bf16([128, 8, 1060])
    WO = A.bf16([128, 8, D])
    xnT = A.bf16([128, 8, TC])
    KSA = A.bf16([128, 4, S])
    kwbase = A.off
    KWR = A.f32([128, 8192])
    KW = KWR[64:128, :].bitcast(BF16).rearrange("p (g t) -> p g t", g=4)
    AL = Arena(A.t, kwbase + 8192)
    AL.off = kwbase
    VS = A.bf16([128, 32, 4, 65])
    VW = A.bf16([128, 32, 4, 65])
    KcT = AL.bf16([64, 4, 256])
    VC = A.bf16([128, 2, 4, 65])
    OV = A.bf16([128, 2, 64])
    CPT = [A.f32([128, 12, 128]) for _ in range(2)]
    WMP = A.f32([128, 128])
    Fn = AL.f32([64, 12, 12])
    RB = A.f32([33, 12])
    OHs = Hst[0:33, 2048:2048 + NSA_U]
    GVs = Hst[0:12, 2816:2816 + NSA_U]
    QS = A.bf16([128, 12, TC])
    QSel = [A.bf16([128, 12, 128]) for _ in range(2)]
    OT = [A.f32([65, 384]) for _ in range(2)]
    GT = AL.f32([64, 4, 36])
    MQ = AL.bf16([64, 4, TC])
    EC = AL.f32([64, 3, 256])
    PC = AL.bf16([64, 3, 256])
    PCT = A.bf16([128, 2, 192])
    SUMC = AL.f32([64, 4])
    SC = AL.f32([64, 4, 64])
    SC2 = AL.f32([64, 64])
    M8 = AL.f32([64, 16])
    SCT4 = AL.f32([64, 4, 2, 64])
    CMPC = [AL.bf16([64, 2, 12, 64]) for _ in range(2)]
    SMpad = AL.bf16([64, 4, 128])
    SB = [A.f32([128, 384]) for _ in range(2)]
    PTs = [A.bf16([128, 384]) for _ in range(3)]
    MAINQ = AL.bf16([64, 2, 768])
    TMP = [AL.f32([64, 2, 3, 64]) for _ in range(2)]
    WX = AL.f32([64, 2, 3, 4])
    MIXT = A.bf16([128, 8, TC])
    PT = [A.bf16([128, TC]) for _ in range(2)]
    Rb = A.f32([128, TC])
    P.dma("sync", gn, cx.d["norm_mixT"][l], w=["gn"])
    src = cx.d["nsa_w_in"][li].rearrange("(kc p) c -> p kc c", p=128)
    ci = 0
    for c0 in range(0, 1060, 256):
        c1 = min(1060, c0 + 256)
        load_cast(P, cx, WIN[:, :, c0:c1], src[:, :, c0:c1], "WIN", [128, 8, c1 - c0], ci)
        ci += 1
    src = cx.d["w_out"][l].rearrange("(kc p) c -> p kc c", p=128)
    for c0 in range(0, D, 256):
        load_cast(P, cx, WO[:, :, c0:c0 + 256], src[:, :, c0:c0 + 256], "WO", [128, 8, 256], ci)
        ci += 1
    for c0 in range(0, S, 2048):
        sb = ci % 2
        P.dma("sync", cx.stg[sb][64:128, 0:2048], cx.d["ind"][64:128, c0:c0 + 2048], w=[("stg", sb)])
        for g in range(4):
            CP(P, ("dve", "pool", "act")[g % 3], KSA[64:128, g, c0:c0 + 2048], cx.stg[sb][64:128, 0:2048], r=[("stg", sb)], w=["KSA"])
        ci += 1
    load_cast(P, cx, OV.rearrange("p a j -> p (a j)"), cx.d["ovl"], "OV", [128, 128], ci)
    ci += 1
    P.dma("sync", KSA[0:64, :, :].rearrange("p g t -> p (g t)"), cx.d_KT[0:64, :], w=["KSA"])
    P.dma("sync", KW.rearrange("p g t -> p (g t)"), cx.d_KT[64:128, :], w=["KW"])
    P.dma("sync", VS.rearrange("p a g d -> p (a g d)"), cx.d_VS, w=["VS"])
    P.dma("sync", VW.rearrange("p a g d -> p (a g d)"), cx.d_VW, w=["VW"])
    P.dma("sync", VC.rearrange("p a g d -> p (a g d)"), cx.d_VC, w=["VC"])
    P.dma("sync", KcT.rearrange("p g t -> p (g t)"), cx.d_KC, w=["KcT"])
    MEMSET(P, "dve", RB, -30000.0, w=["RB"])
    P.dma("sync", RB[0:32, :], cx.d["rel_bias"], w=["RB"])
    P.dma("sync", OHs, cx.d["oh_u"], w=[("stg", 1)])
    for u0 in range(0, NSA_U, 384):
        ps, pk = pp.get()
        MM(P, ps[0:12, 0:384], RB[0:33, :], OHs[0:33, u0:u0 + 384], True, True, r=["RB", ("stg", 1)], w=[pk])
        CP(P, "dve", GVs[:, u0:u0 + 384], ps[0:12, 0:384], r=[pk], w=[("stg", 1)])
    gv_w = P.dma("sync", cx.d_GV, GVs, r=[("stg", 1)], w=["dGV"])
    XH = _shape(Hst[:, 0:768], [128, 12, 64])
    for mi, m in enumerate((0, 64, 128, 192, 512, 576)):
        src = bass.AP(tensor=cx.d_GV.tensor, offset=m, ap=[[1, 128], [NSA_U, 12], [1, 64]])
        P.dma("sync", XH, src, r=["dGV"], w=[("stg", 0)])
        for half in range(2):
            ps, pk = pp.get()
            MM(P, ps[:, 0:384], cx.antiid[:, :], XH.rearrange("p h i -> p (h i)")[:, half * 384:(half + 1) * 384], True, True,
               r=[("stg", 0)], w=[pk])
            if mi < 4:
                CP(P, "dve", CPT[mi // 2][:, 6 * half:6 * half + 6, (mi % 2) * 64:(mi % 2) * 64 + 64],
                   ps[:, 0:384].rearrange("p (h i) -> p h i", h=6), r=[pk], w=[("CPT", mi // 2)])
            elif half == 0:
                CP(P, "dve", WMP[:, (mi - 4) * 64:(mi - 4) * 64 + 64], ps[:, 0:64], r=[pk], w=["WMP"])
    for xq in range(12):
        src = bass.AP(tensor=cx.d_GV.tensor, offset=240 - 16 * xq, ap=[[1, 64], [NSA_U, 12], [1, 1]])
        P.op("sync", lambda e, xq=xq, src=src: e.dma_start(out=Fn[:, :, xq:xq + 1], in_=src, allow_slow_non_contiguous=True),
             r=["dGV"], w=["Fn"], dma=True)
    P.barrier()
    o2 = 2048
    def carve(shape, bf):
        nonlocal o2
        n = int(np.prod(shape[1:]))
        nw = (n + 1) // 2 if bf else n
        ap = Hst[0:shape[0], o2:o2 + nw]
        if bf:
            ap = ap.bitcast(BF16)
        o2 += nw
        return _shape(ap, shape)
    RB = dict(EC=carve([64, 3, 256], False), PC=carve([64, 3, 256], True), PCT=carve([128, 2, 192], True), SUMC=carve([64, 4], False),
              SC2=carve([64, 64], False), M8=carve([64, 16], False), SC=carve([64, 4, 64], False), SMpad=carve([64, 4, 128], True),
              pcs=psall[0:64, 1 * 512:1 * 512 + 768].rearrange("p (r k) -> p r k", r=3), ck=[("ps", 1), ("ps", 2)],
              poc=cx.ps[0], pock=("ps", 0), pool=PsPool(cx, [4]), tag="B")
    assert o2 <= 4096
    RA = dict(EC=EC, PC=PC, PCT=PCT, SUMC=SUMC, SC2=SC2, M8=M8, SC=SC, SMpad=SMpad,
              pcs=psall[0:64, 6 * 512:6 * 512 + 768].rearrange("p (r k) -> p r k", r=3), ck=[("ps", 6), ("ps", 7)],
              poc=cx.ps[5], pock=("ps", 5), pool=pp1, tag="A")
    for R in (RA, RB):
        MEMSET(P, "dve", R["PC"], 0.0, w=[("PC", R["tag"])])
        MEMSET(P, "dve", R["PCT"], 0.0, w=[("PCT", R["tag"])])
        MEMSET(P, "dve", R["SMpad"], 0.0, w=[("SM", R["tag"])])
        MEMSET(P, "dve", R["EC"], 0.0, w=[("EC", R["tag"])])

    hin_v = h_in.rearrange("(kc p) t -> p kc t", p=128)
    hout_v = h_out.rearrange("(kc p) t -> p kc t", p=128)
    for ch in range(NCH):
        t0 = ch * TC
        P.dma("sync", Hc, hin_v[:, :, t0:t0 + TC], w=[hkey])
        rms_chunk(P, cx, Hc, hkey, gn, xnT, "xnT", TC)
        for h in range(12):
            ps, pk = pp.get()
            for kc in range(8):
                MM(P, ps[0:64, 0:TC], WIN[:, kc, h * 64:(h + 1) * 64], xnT[:, kc, :], kc == 0, kc == 7, r=[("xnT", kc), "WIN"], w=[pk])
            ACT(P, QS[0:64, h, :], ps[0:64, 0:TC], AF.Identity, r=[pk], w=[("QS", h)], scale=0.125)
            CP(P, "dve", QS[64:128, h, :], QS[0:64, h, :], r=[("QS", h)], w=[("QSw", h)])
        for h in range(4):
            ps, pk = pp.get()
            for kc in range(8):
                MM(P, ps[0:64, 0:TC], WIN[:, kc, 804 + h * 64:804 + (h + 1) * 64], xnT[:, kc, :], kc == 0, kc == 7, r=[("xnT", kc), "WIN"], w=[pk])
            CP(P, "act", MQ[:, h, :], ps[0:64, 0:TC], r=[pk], w=[("MQ", h)])
        for cl in range(4):
            ps, pk = pp.get()
            for kc in range(8):
                MM(P, ps[0:64, 0:36], xnT[:, kc, cl * 64:(cl + 1) * 64], WIN[:, kc, 768:804], kc == 0, kc == 7, r=[("xnT", kc), "WIN"], w=[pk])
            ACT(P, GT[:, cl, :], ps[0:64, 0:36], AF.Sigmoid, r=[pk], w=[("GT", cl)])
        def phase1(c, cl, R):
            tsl = slice(cl * 64, (cl + 1) * 64)
            hf = c % 2
            pr = (c // 2) % 2
            ncol = min(255, 4 * c + 3)
            nkt = 1 if ncol <= 128 else 2
            sct = SCT4[:, cl, :, :]
            sctk = "SCT4"
            kn0, kn1 = max(0, 4 * c - 9), min(ncol, 4 * c + 3)
            x0 = kn0 - (4 * c - 9)
            qm = QSel[pr]
            if hf == 0:
                CP(P, "pool", qm[0:64, :, :], QS[0:64, :, cl * 64:cl * 64 + 128], r=[("QS", h) for h in range(12)], w=[("QSelq", pr)])
            EC, PC, PCT, SUMC, SC2, M8, SC, SMpad = R["EC"], R["PC"], R["PCT"], R["SUMC"], R["SC2"], R["M8"], R["SC"], R["SMpad"]
            tg = R["tag"]
            for g in range(4):
                hs = slice(3 * g, 3 * g + 3)
                qmk = ("QM", pr, g, hf)
                pcs = R["pcs"]
                ck = R["ck"]
                for r in range(3):
                    h = 3 * g + r
                    MM(P, pcs[:, r, 0:ncol], QS[0:64, h, tsl], KcT[0:64, g, 0:ncol], True, True, r=[("QS", h), "KcT"], w=ck)
                yield
                TT(P, "dve", pcs[:, :, kn0:kn1], pcs[:, :, kn0:kn1], Fn[:, hs, x0:x0 + (kn1 - kn0)], ALU.add, r=ck + ["Fn"], w=ck)
                yield
                ACT(P, EC[:, :, 0:ncol], pcs[:, :, 0:ncol], AF.Exp, r=ck, w=[("EC", tg)])
                yield
                P.op("dve", lambda e, ncol=ncol: e.reduce_sum(out=SUMC[:, 0:3], in_=EC[:, :, 0:ncol], axis=AX.X), r=[("EC", tg)], w=[("SUMC", tg)])
                yield
                if c == 0:
                    TS(P, "dve", SUMC[:, 0:3], SUMC[:, 0:3], 1e-30, None, ALU.max, None, r=[("SUMC", tg)], w=[("SUMC", tg)])
                    yield
                RECIP(P, SUMC[:, 0:3], SUMC[:, 0:3], r=[("SUMC", tg)], w=[("SUMC", tg)])
                yield
                TT(P, "dve", PC[:, :, 0:ncol], EC[:, :, 0:ncol], SUMC[:, 0:3, None].to_broadcast([64, 3, ncol]), ALU.mult,
                   r=[("EC", tg), ("SUMC", tg)], w=[("PC", tg)])
                yield
                ps, pk = R["pool"].get()
                for kt in range(nkt):
                    for r in range(3):
                        MM(P, ps[:, (kt * 3 + r) * 64:(kt * 3 + r + 1) * 64], PC[0:64, r, kt * 128:(kt + 1) * 128], cx.ident[0:64, 0:64],
                           True, True, r=[("PC", tg)], w=[pk])
                CP(P, "act", PCT[:, 0:nkt, :].rearrange("p a b -> p (a b)"), ps[:, 0:nkt * 192], r=[pk], w=[("PCT", tg)])
                yield
                poc = R["poc"]
                pock = R["pock"]
                for r in range(3):
                    for kt in range(nkt):
                        MM(P, poc[0:64, r * 65:(r + 1) * 65], PCT[:, kt, r * 64:(r + 1) * 64], VC[:, kt, g, :], kt == 0, kt == nkt - 1,
                           r=[("PCT", tg), "VC"], w=[pock])
                n = 0
                for r in range(3):
                    for kt in range(nkt):
                        MM(P, poc[0:64, 256:320], PCT[:, kt, r * 64:(r + 1) * 64], OV[:, kt, :], n == 0, n == 3 * nkt - 1,
                           r=[("PCT", tg), "OV"], w=[pock])
                        n += 1
                yield
                gv = GT[:, cl, 9 * g:9 * g + 9].rearrange("p (r x) -> p r x", r=3)[:, :, 0:1]
                ov = poc[0:64, 0:195].rearrange("p (r d) -> p r d", r=3)
                TT(P, "dve", CMPC[pr][:, hf, hs, :], ov[:, :, 0:64], gv.to_broadcast([64, 3, 64]), ALU.mult,
                   r=[pock, ("GT", cl)], w=[("CMPC", pr, g, hf)])
                yield
                TT(P, "dve", SC[:, g, :], poc[0:64, 256:320], sct[:, 1, :], ALU.add, r=[pock, sctk], w=[("SC", tg, g)])
                yield
                P.op("dve", lambda e, g=g: e.max(out=M8[:, 0:8], in_=SC[:, g, :]), r=[("SC", tg, g)], w=[("M8", tg)])
                yield
                P.op("dve", lambda e, g=g: e.match_replace(out=SC2, in_to_replace=M8[:, 0:8], in_values=SC[:, g, :], imm_value=-1e9),
                     r=[("M8", tg), ("SC", tg, g)], w=[("SC2", tg)])
                yield
                P.op("dve", lambda e: e.max(out=M8[:, 8:16], in_=SC2), r=[("SC2", tg)], w=[("M8", tg)])
                yield
                TS(P, "dve", SMpad[:, g, 64:128], SC[:, g, :], M8[:, 15:16], None, ALU.is_ge, None, r=[("SC", tg, g), ("M8", tg)], w=[("SM", tg)])
                yield
                ps, pk = R["pool"].get()
                MM(P, ps[:, 0:64], SMpad[0:64, g, :], cx.ident[0:64, 0:64], True, True, r=[("SM", tg)], w=[pk])
                TS(P, "dve", qm[64:128, hs, hf * 64:(hf + 1) * 64], ps[64:128, None, 0:64].to_broadcast([64, 3, 64]), 30000.0, -30000.0,
                   ALU.mult, ALU.add, r=[pk], w=[qmk])
                yield

        def phase2(c0, cl0):
            tsl2 = slice(cl0 * 64, cl0 * 64 + 128)
            kt_c = c0 // 2
            pr = (c0 // 2) % 2
            qm = QSel[pr]
            tok = cx.ps[3]
            tkey = ("ps", 3)
            for g in range(4):
                hs = slice(3 * g, 3 * g + 3)
                for br in range(2):
                    if br == 0:
                        kts = list(range(0, kt_c + 1))
                        qkey = [("QSelq", pr), ("QM", pr, g, 0), ("QM", pr, g, 1)]
                        VA = VS
                        vkey = "VS"
                        pob = cx.ps[4]
                        okey = ("ps", 4)
                    else:
                        kts = list(range(max(0, (c0 - 8) // 2), kt_c + 1))
                        qkey = [("QSw", 3 * g + r) for r in range(3)]
                        VA = VW
                        vkey = "VW"
                        pob = cx.ps[0]
                        okey = ("ps", 0)
                    def score(ki, kt):
                        ps, pk = pp.get()
                        if br == 0:
                            MM(P, ps[:, 0:384], KSA[:, g, kt * 128:(kt + 1) * 128], qm[:, hs, :], True, True,
                               r=["KSA"] + qkey, w=[pk])
                        else:
                            MM(P, ps[:, 0:384], KW[:, g, kt * 128:(kt + 1) * 128], QS[64:128, hs, tsl2], True, True,
                               r=["KW"] + qkey, w=[pk])
                        return ps, pk

                    def finish(ki, kt, ps, pk):
                        pt = PTs[(ki + br) % 3]
                        ptk = ("PTs", (ki + br) % 3)
                        corr = None
                        if kt == kt_c:
                            corr, corrk = CPT[0][:, hs, :], ("CPT", 0)
                        elif kt == kt_c - 1:
                            corr, corrk = CPT[1][:, hs, :], ("CPT", 1)
                        elif br == 1 and c0 >= 8 and ki == 0:
                            corr, corrk = WMP[:, None, :].to_broadcast([128, 3, 128]), "WMP"
                        if corr is not None:
                            sb = SB[ki % 2]
                            TT(P, "dve", sb.rearrange("p (r q) -> p r q", r=3), ps[:, 0:384].rearrange("p (r q) -> p r q", r=3),
                               corr, ALU.add, r=[pk, corrk], w=[("SB", ki % 2)])
                            ACT(P, pt, sb, AF.Exp, r=[("SB", ki % 2)], w=[ptk])
                        else:
                            ACT(P, pt, ps[:, 0:384], AF.Exp, r=[pk], w=[ptk])
                        MM(P, pob[0:65, 0:384], VA[:, kt, g, :], pt, ki == 0, ki == len(kts) - 1, r=[ptk, vkey], w=[okey])

                    prev = None
                    for ki, kt in enumerate(kts):
                        cur = (ki, kt) + score(ki, kt)
                        if prev is not None:
                            finish(*prev)
                            yield
                        prev = cur
                    finish(*prev)
                    yield
                    CP(P, "act" if br == 0 else "dve", OT[br], pob[0:65, 0:384], r=[okey], w=[("OT", br)])
                    for hf in range(2):
                        for r in range(3):
                            MM(P, tok[0:64, (hf * 3 + r) * 65:(hf * 3 + r + 1) * 65], OT[br][0:65, r * 128 + hf * 64:r * 128 + hf * 64 + 64],
                               cx.identf[0:65, 0:65], True, True, r=[("OT", br)], w=[tkey])
                    x = br + 1
                    ov = tok[0:64, 0:390].rearrange("p (h r d) -> p h r d", h=2, r=3)
                    wx = WX[:, :, :, x:x + 1]
                    TS(P, "dve", wx, ov[:, :, :, 64:65], 1e-30, None, ALU.max, None, r=[tkey], w=[("WX", x)])
                    RECIP(P, wx, wx, r=[("WX", x)], w=[("WX", x)])
                    gv = GT[:, cl0:cl0 + 2, 9 * g:9 * g + 9].rearrange("p h (r x) -> p h r x", r=3)[:, :, :, x:x + 1]
                    TT(P, "dve", wx, wx, gv, ALU.mult, r=[("WX", x), ("GT", cl0), ("GT", cl0 + 1)], w=[("WX", x)])
                    TT(P, "dve", TMP[br], ov[:, :, :, 0:64], wx.to_broadcast([64, 2, 3, 64]), ALU.mult,
                       r=[tkey, ("WX", x)], w=[("TMP", br)])
                    yield
                mq = MAINQ[:, :, 192 * g:192 * (g + 1)].rearrange("p h (r d) -> p h r d", r=3)
                TT(P, "dve", TMP[0], TMP[0], CMPC[pr][:, :, hs, :], ALU.add,
                   r=[("TMP", 0), ("CMPC", pr, g, 0), ("CMPC", pr, g, 1)], w=[("TMP", 0)])
                TT(P, "dve", mq, TMP[0], TMP[1], ALU.add, r=[("TMP", 0), ("TMP", 1)], w=["MAINQ"])
                yield
            for hf in range(2):
                ps, pk = pp.get()
                for c6 in range(6):
                    MM(P, ps[:, c6 * 64:(c6 + 1) * 64], MAINQ[0:64, hf, c6 * 128:(c6 + 1) * 128], cx.ident[0:64, 0:64], True, True, r=["MAINQ"], w=[pk])
                CP(P, "act", MIXT[:, 0:6, (cl0 + hf) * 64:(cl0 + hf + 1) * 64], ps[:, 0:384].rearrange("p (c q) -> p c q", c=6), r=[pk],
                   w=[("MIXT", i) for i in range(6)])
            yield

        def chain(*gens):
            for gq in gens:
                for _ in gq:
                    yield

        P.dma("sync", SCT4, cx.d["sct4b"][ch], w=["SCT4"])
        ga, gb = phase1(ch * 4, 0, RA), phase1(ch * 4 + 1, 1, RB)
        done_a = done_b = False
        while not (done_a and done_b):
            if not done_a:
                done_a = next(ga, "end") == "end"
            if not done_b:
                done_b = next(gb, "end") == "end"
        for pi in range(2):
            c0 = ch * 4 + 2 * pi
            g2 = phase2(c0, 2 * pi)
            g1 = chain(phase1(c0 + 2, 2, RA), phase1(c0 + 3, 3, RA)) if pi == 0 else iter(())
            n2 = 4 * ((c0 // 2 + 1) + (c0 // 2 + 1 - max(0, (c0 - 8) // 2)) + 5) + 1
            per = max(1, -(-140 // n2))
            for _ in g2:
                for _k in range(per):
                    next(g1, None)
            for _ in g1:
                pass
        mem_attn_chunk(P, cx, pp, MQ, KmT, Vm, PT, Rb, MIXT, TC)
        out_proj_chunk(P, cx, pp, WO, MIXT, Hc, hkey, TC)
        P.dma("sync", hout_v[:, :, t0:t0 + TC], Hc, r=[hkey])
    P.barrier()


def final_pass(P, cx, h_in, out):
    TC = 256
    A = cx.arena
    A.reset()
    Hb = [A.f32([128, 8, TC]) for _ in range(2)]
    Ob = [A.f32([128, 8, TC]) for _ in range(2)]
    cx.sq = [A.bf16([128, TC]) for _ in range(2)]
    cx.rstd = A.f32([128, TC])
    gn = A.f32([128, 8])
    P.dma("sync", gn, cx.d["final_normT"], w=["gn"])
    hin_v = h_in.rearrange("(kc p) t -> p kc t", p=128)
    P.dma("sync", Hb[0], hin_v[:, :, 0:TC], w=[("H", 0)])
    for ch in range(S // TC):
        t0 = ch * TC
        b = ch % 2
        if ch + 1 < S // TC:
            P.dma("sync", Hb[1 - b], hin_v[:, :, t0 + TC:t0 + 2 * TC], w=[("H", 1 - b)])
        rms_chunk(P, cx, Hb[b], ("H", b), gn, Ob[b], ("O", b), TC)
        o = P.dma("sync", out.rearrange("(kc p) t -> p kc t", p=128)[:, :, t0:t0 + TC], Ob[b],
                  r=[(("O", b), kc) for kc in range(8)])
        P.out_dmas.append(o)
    P.barrier()


IN_SPECS = [
    ("xT", [D, S]), ("memT", [D, MEM]),
    ("norm_mixT", [DEPTH, 128, 8]), ("norm_memT", [DEPTH, 128, 8]), ("norm_ffnT", [DEPTH, 128, 8]),
    ("final_normT", [128, 8]),
    ("w_up", [DEPTH, D, 2 * FFN]), ("w_down", [DEPTH, FFN, D]),
    ("conv_wT", [DEPTH, 128, 44, 3]), ("conv_bT", [DEPTH, 128, 44]),
    ("ones", [128, 128]),
    ("w_mem_kv", [DEPTH, D, 512]), ("w_out", [DEPTH, D, D]),
    ("gla_w_in", [2, D, 2576]), ("gla_w_gate_up", [2, 16, 384]), ("gla_b_gate", [2, 384]), ("gla_out_norm_rep", [2, 128, 192]),
    ("cmat", [6, 128, 128]),
    ("nsa_w_in", [2, D, 1060]), ("kv_normT", [128, 8]), ("w_kv_shared", [D, 1536]),
    ("cmp_posT", [2, 64, 32]), ("cmp_w1", [2, 2048, 128]), ("cmp_b1T", [128, 2]), ("cmp_w2", [2, 128, 64]),
    ("cmp_b2", [2, 64]), ("cmp_b2T", [64, 2]), ("rel_bias", [32, 12]),
    ("ind", [128, S]), ("ovl", [128, 128]), ("sct4b", [16, 64, 4, 2, 64]), ("oh_u", [33, NSA_U]),
]


def build(stages=("ffn0", "final")):
    nc = bass.Bass("TRN2", target_bir_lowering=False)
    cx = Ctx()
    cx.d = {}
    for name, shape in IN_SPECS:
        cx.d[name] = nc.dram_tensor(name, shape, F32, kind="ExternalInput").ap()
    outT = nc.dram_tensor("outT", [D, S], F32, kind="ExternalOutput").ap()
    hA = nc.dram_tensor("hA", [D, S], F32, kind="Internal").ap()
    cx.d_KT = nc.dram_tensor("d_KT", [128, 4 * S], BF16, kind="Internal").ap()
    cx.d_VS = nc.dram_tensor("d_VS", [128, 32 * 4 * 65], BF16, kind="Internal").ap()
    cx.d_VW = nc.dram_tensor("d_VW", [128, 32 * 4 * 65], BF16, kind="Internal").ap()
    cx.d_KC = nc.dram_tensor("d_KC", [64, 4 * 256], BF16, kind="Internal").ap()
    cx.d_VC = nc.dram_tensor("d_VC", [128, 2 * 4 * 65], BF16, kind="Internal").ap()
    cx.d_GV = nc.dram_tensor("d_GV", [12, NSA_U], F32, kind="Internal").ap()
    P = Prog(nc)
    with ExitStack() as st:
        NW = 51200
        arena_t = st.enter_context(nc.sbuf_tensor("arena", [128, NW], F32))
        ones_bf = st.enter_context(nc.sbuf_tensor("ones_bf", [128, 128], BF16))
        ones_f = st.enter_context(nc.sbuf_tensor("ones_f", [128, 128], F32))
        cx.arena = Arena(arena_t, NW)
        cx.ones_bf = ones_bf
        psall = st.enter_context(nc.psum_tensor("psall", [128, 4096], F32))
        cx.psall = psall
        cx.ps = [psall[:, i * 512:(i + 1) * 512] for i in range(8)]
        sems = {e: st.enter_context(nc.semaphore("s_" + e)) for e in Prog.ENGS}
        dsems = [st.enter_context(nc.semaphore("d%d" % i)) for i in range(NDSEM)]
        block = st.enter_context(nc.Block())

        P.dma("sync", ones_f[:, :], cx.d["ones"], w=["ones_f"])
        P.op("dve", lambda e: e.tensor_copy(out=ones_bf[:, :], in_=ones_f[:, :]), r=["ones_f"], w=["ones"])
        cmf = st.enter_context(nc.sbuf_tensor("cmf", [128, 6, 128], F32))
        cmb = st.enter_context(nc.sbuf_tensor("cmb", [128, 6, 128], BF16))
        one_col = st.enter_context(nc.sbuf_tensor("one_col", [128, 2], F32))
        P.dma("sync", cmf[:, :, :], cx.d["cmat"].rearrange("c p n -> p c n"), w=["cmf"])
        P.op("dve", lambda e: e.tensor_copy(out=cmb[:, :, :], in_=cmf[:, :, :]), r=["cmf"], w=["cmb"])
        P.op("dve", lambda e: e.memset(one_col[:, :], 1.0), w=["one_col"])
        cx.ident = cmb[:, 0, :]
        cx.identf = cmf[:, 0, :]
        cx.triu = cmf[:, 1, :]
        cx.strictl = cmf[:, 2, :]
        cx.maskji = cmf[:, 3, :]
        cx.ones1 = cmb[:, 4, :]
        cx.one_col = one_col
        cx.antiid = cmf[:, 5, :]
        P.barrier()

        h = cx.d["xT"]
        for stg in stages:
            if stg.startswith("ffn"):
                ffn_pass(P, cx, int(stg[3:]), h, hA)
                h = hA
            elif stg.startswith("gla"):
                gla_pass(P, cx, int(stg[3:]), h, hA)
                h = hA
            elif stg == "kv":
                kv_pass(P, cx, h)
            elif stg.startswith("nsa"):
                nsa_pass(P, cx, int(stg[3:]), h, hA)
                h = hA
            elif stg == "final":
                final_pass(P, cx, h, outT)
        P.barrier()
        P.emit(block, sems, dsems)
    cx.P = P
    return nc, cx


def prep_common(inp):
    f = np.float32

    def featT(v):
        v = np.asarray(v, f)
        return np.ascontiguousarray(v.reshape(v.shape[:-1] + (8, 128)).swapaxes(-1, -2))

    m = {}
    m["norm_mixT"] = featT(inp["norm_mix"])
    m["norm_memT"] = featT(inp["norm_mem"])
    m["norm_ffnT"] = featT(inp["norm_ffn"])
    m["final_normT"] = featT(inp["final_norm"])
    m["w_up"] = np.ascontiguousarray(np.asarray(inp["w_up"], f))
    m["w_down"] = np.ascontiguousarray(np.asarray(inp["w_down"], f))
    cwv = np.asarray(inp["conv_w"], f)
    m["conv_wT"] = np.ascontiguousarray(cwv.reshape(DEPTH, 3, 44, 128).transpose(0, 3, 2, 1))
    cbv = np.asarray(inp["conv_b"], f)
    m["conv_bT"] = np.ascontiguousarray(cbv.reshape(DEPTH, 44, 128).transpose(0, 2, 1))
    m["ones"] = np.full((128, 128), 1.0 / D, f)
    for k in ("w_mem_kv", "w_out", "gla_w_in", "gla_w_gate_up", "gla_b_gate"):
        m[k] = np.ascontiguousarray(np.asarray(inp[k], f))
    m["gla_out_norm_rep"] = np.ascontiguousarray(np.broadcast_to(np.asarray(inp["gla_out_norm"], f)[:, None, :], (2, 128, 192)))
    jj, ii = np.meshgrid(np.arange(128), np.arange(128), indexing="ij")
    cm = np.zeros((6, 128, 128), f)
    cm[0] = np.eye(128)
    cm[1] = (jj <= ii) * (-1.0 / 16.0)
    cm[2] = (jj > ii) * (-1.0 / 16.0)
    cm[3] = (jj <= ii) * 1.0
    cm[4] = 1.0
    cm[5] = np.eye(128)[::-1]
    m["cmat"] = cm
    for k in ("nsa_w_in", "w_kv_shared", "cmp_w1", "cmp_w2", "cmp_b2", "rel_bias"):
        m[k] = np.ascontiguousarray(np.asarray(inp[k], f))
    m["kv_normT"] = featT(inp["kv_norm"])
    m["cmp_posT"] = np.ascontiguousarray(np.asarray(inp["cmp_pos"], f).transpose(0, 2, 1))
    m["cmp_b1T"] = np.ascontiguousarray(np.asarray(inp["cmp_b1"], f).T)
    m["cmp_b2T"] = np.ascontiguousarray(np.asarray(inp["cmp_b2"], f).T)
    pidx = np.arange(128)[:, None] % 64
    m["ind"] = (np.arange(S)[None, :] // 64 == pidx).astype(f)
    kk = (np.arange(2)[None, :, None] * 128 + np.arange(128)[:, None, None])
    jb = np.arange(64)[None, None, :]
    ov = ((16 * kk < 64 * jb + 64) & (16 * kk + 31 >= 64 * jb) & (kk < 255)).astype(f)
    m["ovl"] = np.ascontiguousarray(ov.reshape(128, 128))
    cb = np.arange(64)[:, None]
    jj2 = np.arange(64)[None, :]
    forced = (jj2 == 0) | (jj2 == cb) | (jj2 == cb - 1)
    valid = (jj2 <= cb) & ~forced
    add = np.where(forced, 1e4, np.where(jj2 <= cb, 0.0, -1.0))
    sct = np.stack([valid.astype(f), add.astype(f)], axis=1).reshape(16, 1, 4, 2, 64)
    m["sct4b"] = np.ascontiguousarray(np.broadcast_to(sct, (16, 64, 4, 2, 64)))
    dd = np.arange(NSA_U) - 127
    dcl = np.maximum(dd, 0)
    large = 16 + (np.log(np.maximum(dcl, 1).astype(f) / f(16)) / f(np.log(128 / 16)) * f(16)).astype(np.int32)
    bucket = np.where(dcl < 16, dcl, np.minimum(large, 31))
    near = (dd >= 0) & (dd < 113)
    oh = np.zeros((33, NSA_U), f)
    for b in range(31):
        oh[b] = (near & (bucket == b))
    oh[31] = -1.0 * near
    oh[32] = ((dd < 0) | (dd >= 512))
    assert not (near & (bucket == 31)).any()
    m["oh_u"] = oh
    return m


def prep_inputs(inp, b, common):
    f = np.float32
    m = dict(common)
    m["xT"] = np.ascontiguousarray(np.asarray(inp["x"][b], f).T)
    m["memT"] = np.ascontiguousarray(np.asarray(inp["mem"][b], f).T)
    return m


def run(inp, stages, ncores=8):
    nc, cx = build(stages)
    common = prep_common(inp)
    per_b = [prep_inputs(inp, b, common) for b in range(4)]
    in_maps = [per_b[c // 2] for c in range(ncores)]
    res = run_bass_kernel_spmd(nc, in_maps, core_ids=list(range(ncores)))
    out = np.stack([np.ascontiguousarray(res.results[2 * b]["outT"].T) for b in range((ncores + 1) // 2)], axis=0)
    return out.astype(np.float32)


FULL = ("gla0", "ffn0", "gla1", "ffn1", "kv", "nsa2", "ffn2", "nsa3", "ffn3", "final")


def kernel(**inputs):
    return run(inputs, FULL)
```
